# Optimizing a Trainium2 kernel written in Bass

```python
import jax
import jax.numpy as jnp
from jax import lax
import numpy as np

D_MODEL = 1024
BATCH = 32
SEQ = 256
DEPTH = 2
DEC_BATCH = 8
DEC_SEQ = 4096
PAST_LEN = 512

GRID_W = 64
N_MIXERS = 2
N_MLSTM_LAYERS = (DEPTH + 1) // 2
N_ATTN_LAYERS = DEPTH // 2
D_FF = 2816
FFN_RES = 0.5
NH_M = 8
DK_M = 64
DV_M = 128
MLSTM_CHUNK = 128
NH_A = 16
NKV_A = 4
G_A = NH_A // NKV_A
HD_A = 64
WINDOW = 128
ATTN_BLOCK = 128
ROPE_THETA = 10000.0
EPS = 1e-6
NEG_INF = -1e30
N_MOD = 9
M_IN = 2 * NH_M * DK_M + 2 * NH_M * DV_M + 4 * NH_M
A_IN = (NH_A + 2 * NKV_A) * HD_A

kernel_name = "hybrid_mlstm_swa_macaron_dit_step"


def _rmsnorm(x, g):
    xf = x.astype(jnp.float32)
    y = xf * lax.rsqrt(jnp.mean(xf * xf, axis=-1, keepdims=True) + EPS)
    return (y * g.astype(jnp.float32)).astype(x.dtype)


def _adaln(cond, w_mod, b_mod):
    return (jax.nn.silu(cond) @ w_mod + b_mod).reshape(cond.shape[0], N_MOD, D_MODEL)


def _modulated_norm(x, g, shift, scale):
    return _rmsnorm(x, g) * (1 + scale[:, None, :]) + shift[:, None, :]


def _swiglu(h, w_gu, w_down):
    gate, up = jnp.split(h @ w_gu, 2, axis=-1)
    return (jax.nn.silu(gate) * up) @ w_down


def _ffn_sub(x, mod, j, g, w_gu, w_down):
    shift, scale, gate = mod[:, 3 * j], mod[:, 3 * j + 1], mod[:, 3 * j + 2]
    return x + FFN_RES * gate[:, None, :] * _swiglu(_modulated_norm(x, g, shift, scale), w_gu, w_down)


def _mlstm_chunkwise(q, k, v, log_i, log_f, c0, n0, m0):
    bsz, n_tok, nh, _ = q.shape
    dv = v.shape[-1]
    nc = n_tok // MLSTM_CHUNK

    def chunks(a):
        a = a.reshape((bsz, nc, MLSTM_CHUNK) + a.shape[2:])
        return jnp.moveaxis(jnp.moveaxis(a, 1, 0), 3, 2)

    tril = jnp.tril(jnp.ones((MLSTM_CHUNK, MLSTM_CHUNK), bool))

    def step(carry, xs):
        c, n, m = carry
        qc, kc, vc, li, lf = xs
        b = jnp.cumsum(lf, axis=-1)
        d = jnp.where(tril, b[..., :, None] - b[..., None, :] + li[..., None, :], -jnp.inf)
        m_inter = b + m[..., None]
        m_t = jnp.maximum(jnp.max(d, axis=-1), m_inter)
        w = jnp.einsum('bhtd,bhsd->bhts', qc, kc) * jnp.exp(d - m_t[..., None])
        inter = jnp.exp(m_inter - m_t)
        num = jnp.einsum('bhts,bhsv->bhtv', w, vc) + inter[..., None] * jnp.einsum('bhtd,bhdv->bhtv', qc, c)
        den = jnp.sum(w, axis=-1) + inter * jnp.einsum('bhtd,bhd->bht', qc, n)
        h = num / jnp.maximum(jnp.abs(den), jnp.exp(-m_t))[..., None]
        b_end = b[..., -1]
        to_end = b_end[..., None] - b + li
        m_new = jnp.maximum(b_end + m, jnp.max(to_end, axis=-1))
        w_end = jnp.exp(to_end - m_new[..., None])
        keep = jnp.exp(b_end + m - m_new)
        c_new = keep[..., None, None] * c + jnp.einsum('bhs,bhsd,bhsv->bhdv', w_end, kc, vc)
        n_new = keep[..., None] * n + jnp.einsum('bhs,bhsd->bhd', w_end, kc)
        return (c_new, n_new, m_new), h

    xs = (chunks(q), chunks(k), chunks(v), chunks(log_i), chunks(log_f))
    (c_f, n_f, m_f), h = lax.scan(step, (c0, n0, m0), xs)
    h = jnp.swapaxes(jnp.moveaxis(h, 0, 1), 2, 3).reshape(bsz, n_tok, nh, dv)
    return h, c_f, n_f, m_f


def _mlstm_mixer(h, w_in, b_gate, g_head, w_out, c0, n0, m0):
    f32 = jnp.float32
    bsz, n_tok, _ = h.shape
    qk_w, v_w = NH_M * DK_M, NH_M * DV_M
    q, k, v, o, gates = jnp.split(h @ w_in, [qk_w, 2 * qk_w, 2 * qk_w + v_w, 2 * qk_w + 2 * v_w], axis=-1)
    q = q.reshape(bsz, n_tok, NH_M, DK_M).astype(f32) * DK_M ** -0.5
    k = k.reshape(bsz, n_tok, NH_M, DK_M).astype(f32)
    v = v.reshape(bsz, n_tok, NH_M, DV_M).astype(f32)
    gates = (gates.astype(f32) + b_gate.astype(f32)).reshape(bsz, n_tok, 2, 2, NH_M)
    log_i = gates[:, :, :, 0]
    log_f = jax.nn.log_sigmoid(gates[:, :, :, 1])
    c0, n0, m0 = c0.astype(f32), n0.astype(f32), m0.astype(f32)
    h_f, c_f, n_f, m_f = _mlstm_chunkwise(q, k, v, log_i[:, :, 0], log_f[:, :, 0], c0[:, 0], n0[:, 0], m0[:, 0])
    rev = lambda a: jnp.flip(a, axis=1)
    h_b, c_b, n_b, m_b = _mlstm_chunkwise(rev(q), rev(k), rev(v), rev(log_i[:, :, 1]), rev(log_f[:, :, 1]),
                                          c0[:, 1], n0[:, 1], m0[:, 1])
    hs = h_f + rev(h_b)
    hs = hs * lax.rsqrt(jnp.mean(hs * hs, axis=-1, keepdims=True) + EPS) * g_head.astype(f32).reshape(NH_M, DV_M)
    out = (hs.reshape(bsz, n_tok, v_w).astype(h.dtype) * jax.nn.sigmoid(o)) @ w_out
    dt = h.dtype
    return (out, jnp.stack([c_f, c_b], axis=1).astype(dt), jnp.stack([n_f, n_b], axis=1).astype(dt),
            jnp.stack([m_f, m_b], axis=1).astype(dt))


def _attn_project(h, w_in):
    bsz, n_tok, _ = h.shape
    q, k, v = jnp.split(h @ w_in, [NH_A * HD_A, (NH_A + NKV_A) * HD_A], axis=-1)
    return (q.reshape(bsz, n_tok, NKV_A, G_A, HD_A), k.reshape(bsz, n_tok, NKV_A, HD_A),
            v.reshape(bsz, n_tok, NKV_A, HD_A))


def _axial_rope_tables(n_rows):
    quarter = HD_A // 4
    freqs = ROPE_THETA ** (-jnp.arange(quarter, dtype=jnp.float32) / quarter)
    row = jnp.repeat(jnp.arange(n_rows, dtype=jnp.float32), GRID_W)
    col = jnp.tile(jnp.arange(GRID_W, dtype=jnp.float32), n_rows)
    ang_r, ang_c = row[:, None] * freqs, col[:, None] * freqs
    ang = jnp.concatenate([ang_r, ang_r, ang_c, ang_c], axis=-1)
    return jnp.cos(ang), jnp.sin(ang)


def _axial_rope(x, cos, sin):
    shp = (cos.shape[0],) + (1,) * (x.ndim - 3) + (HD_A,)
    cs, sn = cos.reshape(shp).astype(x.dtype), sin.reshape(shp).astype(x.dtype)
    xr = x.reshape(x.shape[:-1] + (2, 2, HD_A // 4))
    rot = jnp.stack([-xr[..., 1, :], xr[..., 0, :]], axis=-2).reshape(x.shape)
    return x * cs + rot * sn


def _sink_softmax_attend(qb, k, v, sink, mask):
    s = jnp.einsum('blkgd,bskd->bkgls', qb, k).astype(jnp.float32) * HD_A ** -0.5
    if mask is not None:
        s = jnp.where(mask, s, NEG_INF)
    sink_col = jnp.broadcast_to(sink.astype(jnp.float32).reshape(1, NKV_A, G_A, 1, 1), s.shape[:-1] + (1,))
    p = jax.nn.softmax(jnp.concatenate([s, sink_col], axis=-1), axis=-1)[..., :-1]
    return jnp.einsum('bkgls,bskd->blkgd', p.astype(v.dtype), v)


def _attn_context(q, k, v, sink):
    bsz, n_ctx = q.shape[:2]

    def block(j):
        qb = lax.dynamic_slice_in_dim(q, j * ATTN_BLOCK, ATTN_BLOCK, axis=1)
        return _sink_softmax_attend(qb, k, v, sink, None)

    out = lax.map(block, jnp.arange(n_ctx // ATTN_BLOCK))
    return jnp.moveaxis(out, 0, 1).reshape(bsz, n_ctx, NH_A * HD_A)


def _attn_latent(q, k, v, k_ctx, v_ctx, sink):
    bsz, n_tok = q.shape[:2]
    n_ctx = k_ctx.shape[1]
    pad = ((0, 0), (ATTN_BLOCK, ATTN_BLOCK), (0, 0), (0, 0))
    k_pad, v_pad = jnp.pad(k, pad), jnp.pad(v, pad)
    ctx_mask = jnp.ones((ATTN_BLOCK, n_ctx), bool)

    def block(j):
        start = j * ATTN_BLOCK
        qb = lax.dynamic_slice_in_dim(q, start, ATTN_BLOCK, axis=1)
        kb = lax.dynamic_slice_in_dim(k_pad, start, 3 * ATTN_BLOCK, axis=1)
        vb = lax.dynamic_slice_in_dim(v_pad, start, 3 * ATTN_BLOCK, axis=1)
        t_pos = start + jnp.arange(ATTN_BLOCK)
        s_pos = start - ATTN_BLOCK + jnp.arange(3 * ATTN_BLOCK)
        local = ((jnp.abs(t_pos[:, None] - s_pos[None, :]) <= WINDOW)
                 & (s_pos >= 0)[None, :] & (s_pos < n_tok)[None, :])
        mask = jnp.concatenate([local, ctx_mask], axis=1)
        return _sink_softmax_attend(qb, jnp.concatenate([kb, k_ctx], axis=1),
                                    jnp.concatenate([vb, v_ctx], axis=1), sink, mask)

    out = lax.map(block, jnp.arange(n_tok // ATTN_BLOCK))
    return jnp.moveaxis(out, 0, 1).reshape(bsz, n_tok, NH_A * HD_A)


def setup_inputs(seed: int = 0) -> dict:
    key = jax.random.key(seed)
    ks = jax.random.split(key, 24)
    f32 = jnp.float32
    nrm = lambda kk, shp: jax.random.normal(kk, shp, f32)
    D = D_MODEL
    gate_base = jnp.tile(jnp.concatenate([jnp.full((NH_M,), -1.0, f32), jnp.full((NH_M,), 3.0, f32)]), 2)
    return {
        "x_prompt": nrm(ks[0], (BATCH, SEQ, D)),
        "x_sample": nrm(ks[1], (DEC_BATCH, DEC_SEQ, D)),
        "state_c": 0.5 * nrm(ks[2], (DEC_BATCH, N_MLSTM_LAYERS, 2, NH_M, DK_M, DV_M)),
        "state_n": 0.5 * nrm(ks[3], (DEC_BATCH, N_MLSTM_LAYERS, 2, NH_M, DK_M)),
        "state_m": nrm(ks[4], (DEC_BATCH, N_MLSTM_LAYERS, 2, NH_M)),
        "cache_k": nrm(ks[5], (DEC_BATCH, N_ATTN_LAYERS, PAST_LEN, NKV_A, HD_A)),
        "cache_v": nrm(ks[6], (DEC_BATCH, N_ATTN_LAYERS, PAST_LEN, NKV_A, HD_A)),
        "c": nrm(ks[7], (DEC_BATCH, D)),
        "c_ctx": nrm(ks[8], (D,)),
        "w_mod": 0.5 * D ** -0.5 * nrm(ks[9], (DEPTH, D, N_MOD * D)),
        "b_mod": 0.02 * nrm(ks[10], (DEPTH, N_MOD * D)),
        "norm_g": 1.0 + 0.02 * nrm(ks[11], (DEPTH, 3, D)),
        "ffn1_w_gu": D ** -0.5 * nrm(ks[12], (DEPTH, D, 2 * D_FF)),
        "ffn1_w_down": D_FF ** -0.5 * nrm(ks[13], (DEPTH, D_FF, D)),
        "ffn2_w_gu": D ** -0.5 * nrm(ks[14], (DEPTH, D, 2 * D_FF)),
        "ffn2_w_down": D_FF ** -0.5 * nrm(ks[15], (DEPTH, D_FF, D)),
        "mlstm_w_in": D ** -0.5 * nrm(ks[16], (N_MLSTM_LAYERS, D, M_IN)),
        "mlstm_b_gate": gate_base[None, :] + 0.1 * nrm(ks[17], (N_MLSTM_LAYERS, 4 * NH_M)),
        "mlstm_g_head": 1.0 + 0.02 * nrm(ks[18], (N_MLSTM_LAYERS, NH_M * DV_M)),
        "mlstm_w_out": (NH_M * DV_M) ** -0.5 * nrm(ks[19], (N_MLSTM_LAYERS, NH_M * DV_M, D)),
        "attn_w_in": D ** -0.5 * nrm(ks[20], (N_ATTN_LAYERS, D, A_IN)),
        "attn_sink": 0.5 * nrm(ks[21], (N_ATTN_LAYERS, NH_A)),
        "attn_w_out": (NH_A * HD_A) ** -0.5 * nrm(ks[22], (N_ATTN_LAYERS, NH_A * HD_A, D)),
        "final_g": 1.0 + 0.02 * nrm(ks[23], (D,)),
    }


def reference(x_prompt, x_sample, state_c, state_n, state_m, cache_k, cache_v, c, c_ctx, w_mod, b_mod, norm_g,
              ffn1_w_gu, ffn1_w_down, ffn2_w_gu, ffn2_w_down, mlstm_w_in, mlstm_b_gate, mlstm_g_head, mlstm_w_out,
              attn_w_in, attn_sink, attn_w_out, final_g):
    n_rows = x_sample.shape[1] // GRID_W
    cos, sin = _axial_rope_tables(n_rows)
    bp = x_prompt.shape[0]
    xp, xs = x_prompt, x_sample
    new_c, new_n, new_m, new_k, new_v = [], [], [], [], []
    for l in range(DEPTH):
        mod_p = _adaln(c_ctx[None, :], w_mod[l], b_mod[l])
        mod_s = _adaln(c, w_mod[l], b_mod[l])
        xp = _ffn_sub(xp, mod_p, 0, norm_g[l, 0], ffn1_w_gu[l], ffn1_w_down[l])
        xs = _ffn_sub(xs, mod_s, 0, norm_g[l, 0], ffn1_w_gu[l], ffn1_w_down[l])
        hp = _modulated_norm(xp, norm_g[l, 1], mod_p[:, 3], mod_p[:, 4])
        hs = _modulated_norm(xs, norm_g[l, 1], mod_s[:, 3], mod_s[:, 4])
        i = l // N_MIXERS
        if l % N_MIXERS == 0:
            zc = jnp.zeros((bp, 2, NH_M, DK_M, DV_M), xp.dtype)
            zn = jnp.zeros((bp, 2, NH_M, DK_M), xp.dtype)
            zm = jnp.zeros((bp, 2, NH_M), xp.dtype)
            out_p, sc, sn, sm = _mlstm_mixer(hp, mlstm_w_in[i], mlstm_b_gate[i], mlstm_g_head[i], mlstm_w_out[i],
                                             zc, zn, zm)
            out_s, _, _, _ = _mlstm_mixer(hs, mlstm_w_in[i], mlstm_b_gate[i], mlstm_g_head[i], mlstm_w_out[i],
                                          state_c[:, i], state_n[:, i], state_m[:, i])
            new_c.append(sc)
            new_n.append(sn)
            new_m.append(sm)
        else:
            qp, kp, vp = _attn_project(hp, attn_w_in[i])
            out_p = _attn_context(qp, kp, vp, attn_sink[i]) @ attn_w_out[i]
            qs, ks_, vs = _attn_project(hs, attn_w_in[i])
            qs, ks_ = _axial_rope(qs, cos, sin), _axial_rope(ks_, cos, sin)
            out_s = _attn_latent(qs, ks_, vs, cache_k[:, i], cache_v[:, i], attn_sink[i]) @ attn_w_out[i]
            new_k.append(kp)
            new_v.append(vp)
        xp = xp + mod_p[:, 5][:, None, :] * out_p
        xs = xs + mod_s[:, 5][:, None, :] * out_s
        xp = _ffn_sub(xp, mod_p, 2, norm_g[l, 2], ffn2_w_gu[l], ffn2_w_down[l])
        xs = _ffn_sub(xs, mod_s, 2, norm_g[l, 2], ffn2_w_gu[l], ffn2_w_down[l])
    y_prompt = _rmsnorm(xp, final_g)
    y_sample = _rmsnorm(xs, final_g)
    new_state_c = jnp.stack(new_c, axis=1)
    new_state_n = jnp.stack(new_n, axis=1)
    new_state_m = jnp.stack(new_m, axis=1)
    new_cache_k = jnp.stack(new_k, axis=1)
    new_cache_v = jnp.stack(new_v, axis=1)
    return (y_prompt, y_sample, new_state_c, new_state_n, new_state_m, new_cache_k, new_cache_v)
```

```python
import numpy as np
from contextlib import ExitStack
import concourse.bass as bass
import concourse.mybir as mybir
from concourse.bass_utils import run_bass_kernel_spmd

F32 = mybir.dt.float32
BF16 = mybir.dt.bfloat16
AF = mybir.ActivationFunctionType
ALU = mybir.AluOpType
AX = mybir.AxisListType

NTOK = 5120
TT = 512
NTILE = 10
NCH = 40
EPS = 1e-6
WSLOT = 33792
STRICT_SAME_ENGINE = False


class Dep:
    __slots__ = ("w", "rd")

    def __init__(self):
        self.w = None
        self.rd = {}


def _flat(xs):
    out = []
    for x in xs:
        if isinstance(x, (list, tuple)):
            out.extend(_flat(x))
        elif x is not None:
            out.append(x)
    return out


class Eng:
    def __init__(self, name, h, sem):
        self.name = name
        self.h = h
        self.sem = sem
        self.cnt = 0
        self.known = {}
        self.nwait = 0
        self.nins = 0

    def wait_tok(self, tok):
        sem, val, _ = tok
        k = id(sem)
        if self.known.get(k, 0) < val:
            self.h.wait_ge(sem, val)
            self.known[k] = val
            self.nwait += 1

    def _collect(self, reads, writes):
        toks = []
        for d in reads:
            if d.w is not None:
                toks.append(d.w)
        strict = STRICT_SAME_ENGINE and self.name != "pe"
        for d in writes:
            if d.w is not None and (strict or d.w[2] != self.name):
                toks.append(d.w)
            for en, t in d.rd.items():
                if strict or en != self.name:
                    toks.append(t)
        return toks

    def op(self, fn, reads=(), writes=(), inc=True):
        reads = _flat(reads)
        writes = _flat(writes)
        for t in self._collect(reads, writes):
            self.wait_tok(t)
        ins = fn()
        self.nins += 1
        if inc:
            self.cnt += 1
            ins.then_inc(self.sem, 1)
            tok = (self.sem, self.cnt, self.name)
        else:
            tok = (self.sem, self.cnt + 1, self.name)
        for d in reads:
            d.rd[self.name] = tok
        for d in writes:
            d.w = tok
            d.rd = {}
        return ins

    def last_tok(self):
        return (self.sem, self.cnt, self.name) if self.cnt else None


class Queue(Eng):
    def __init__(self, name, h, sems):
        super().__init__(name, h, None)
        self.sems = sems
        self.k = 0

    def dma(self, out, in_, reads=(), writes=(), **kw):
        reads = _flat(reads)
        writes = _flat(writes)
        for t in self._collect(reads, writes):
            self.wait_tok(t)
        ns = len(self.sems)
        slot = self.k % ns
        gen = self.k // ns
        sem = self.sems[slot]
        if gen > 0:
            self.wait_tok((sem, 16 * gen, self.name))
        ins = self.h.dma_start(out=out, in_=in_, **kw)
        ins.then_inc(sem, 16)
        self.nins += 1
        tok = (sem, 16 * (gen + 1), "%s#%d" % (self.name, slot))
        self.k += 1
        for d in reads:
            d.rd[tok[2]] = tok
        for d in writes:
            d.w = tok
            d.rd = {}
        return ins

    def all_toks(self):
        ns = len(self.sems)
        out = []
        for slot in range(min(ns, self.k)):
            n = (self.k - 1 - slot) // ns + 1
            out.append((self.sems[slot], 16 * n, "%s#%d" % (self.name, slot)))
        return out


class K:
    pass


def _barrier(k):
    toks = []
    for e in (k.pe, k.act, k.dve, k.pool):
        t = e.last_tok()
        if t:
            toks.append(t)
    toks += k.qs.all_toks() + k.qg.all_toks()
    for e in (k.pe, k.act, k.dve, k.pool, k.qs):
        for t in toks:
            if t[2] != e.name:
                e.wait_tok(t)


def build_program(upto=99):
    nc = bass.Bass("TRN2", target_bir_lowering=False)
    k = K()
    k.nc = nc
    k.upto = upto

    def din(name, shape):
        return nc.dram_tensor(name, list(shape), F32, kind="ExternalInput").ap()

    def dout(name, shape):
        return nc.dram_tensor(name, list(shape), F32, kind="ExternalOutput").ap()

    def dscr(name, shape, dt):
        return nc.dram_tensor(name, list(shape), dt, kind="Internal").ap()

    k.x_in = din("x_in", [NTOK, 1024])
    k.cvec = din("cvec", [2, 1024])
    k.st_c = din("st_c", [2, 8, 64, 128])
    k.st_n = din("st_n", [2, 8, 64])
    k.st_m = din("st_m", [2, 8])
    k.ck = din("ck", [512, 256])
    k.cv = din("cv", [512, 256])
    k.w_mod = din("w_mod", [2, 1024, 9216])
    k.b_mod = din("b_mod", [2, 9216])
    k.norm_g = din("norm_g", [2, 3, 1024])
    k.f_gu = [din("ffn1_w_gu", [2, 1024, 5632]), din("ffn2_w_gu", [2, 1024, 5632])]
    k.f_dn = [din("ffn1_w_down", [2, 2816, 1024]), din("ffn2_w_down", [2, 2816, 1024])]
    k.m_win = din("mlstm_w_in", [1024, 3104])
    k.m_bg = din("mlstm_b_gate", [1, 32])
    k.m_gh = din("mlstm_g_head", [1024])
    k.m_wout = din("mlstm_w_out", [1024, 1024])
    k.a_win = din("attn_w_in_ext", [1024, 2816])
    k.a_sink = din("attn_sinkT", [128, 8])
    k.a_wout = din("attn_w_out_p", [1024, 1024])
    k.final_g = din("final_g", [1024])
    k.c_ident = din("c_ident", [128, 128])
    k.c_masku = din("c_masku", [128, 128])
    k.c_maskl = din("c_maskl", [128, 128])
    k.c_e16 = din("c_e16", [64, 16])
    k.c_cos = din("c_cos", [128, 4096])
    k.c_sin = din("c_sin", [128, 4096])
    k.y_out = dout("y_out", [NTOK, 1024])
    k.nsc = dout("nsc", [4, 2, 8, 64, 128])
    k.nsn = dout("nsn", [4, 2, 8, 64])
    k.nsm = dout("nsm", [4, 2, 8])
    k.nck = dout("nck", [4, 256, 256])
    k.ncv = dout("ncv", [4, 256, 256])
    if upto < 99:
        k.xT_s = nc.dram_tensor("xT_s", [8, 128, NTOK], F32, kind="ExternalOutput").ap()
    else:
        k.xT_s = dscr("xT_s", [8, 128, NTOK], F32)
    k.hT_s = dscr("hT_s", [8, 128, NTOK], BF16)
    k.qT_s = dscr("qT_s", [8, 128, NTOK], BF16)
    k.kT_s = dscr("kT_s", [4, 128, NTOK], BF16)
    k.oT_s = dscr("oT_s", [8, 128, NTOK], BF16)
    k.kt_s = dscr("kt_s", [NTOK, 512], BF16)
    k.vt_s = dscr("vt_s", [NTOK, 1024], BF16)
    k.hf_s = dscr("hf_s", [NTOK, 1024], BF16)
    k.hb_s = dscr("hb_s", [NTOK, 1024], BF16)
    k.d_xs = [Dep() for _ in range(NTILE)]
    k.d_hs = [Dep() for _ in range(NTILE)]
    k.d_q = [Dep() for _ in range(NTILE)]
    k.d_k = [Dep() for _ in range(NTILE)]
    k.d_o = [Dep() for _ in range(NTILE)]
    k.d_kt = [Dep() for _ in range(NTILE)]
    k.d_vt = [Dep() for _ in range(NTILE)]
    k.d_hf = [[Dep() for _ in range(NCH)] for _ in range(2)]
    k.d_out = Dep()

    with ExitStack() as es:
        k.es = es

        def S(n):
            return es.enter_context(nc.semaphore(n))

        k.pe = Eng("pe", nc.tensor, S("s_pe"))
        k.act = Eng("act", nc.scalar, S("s_act"))
        k.dve = Eng("dve", nc.vector, S("s_dve"))
        k.pool = Eng("pool", nc.gpsimd, S("s_pool"))
        k.qs = Queue("qs", nc.sync, [S("s_qs%d" % i) for i in range(8)])
        k.qg = Queue("qg", nc.gpsimd, [S("s_qg%d" % i) for i in range(6)])
        k.qg.known = k.pool.known

        k.uid = 0

        def sb(name, shape, dt, st=es):
            k.uid += 1
            return st.enter_context(nc.sbuf_tensor("%s_%d" % (name, k.uid), list(shape), dt))

        k.sb = sb
        k.W = [sb("W0", [128, WSLOT], BF16), sb("W1", [128, WSLOT], BF16)]
        k.dW = [Dep(), Dep()]
        k.ident_f = sb("ident_f", [128, 128], F32)
        k.ident_b = sb("ident_b", [128, 128], BF16)
        k.ones_b = sb("ones_b", [128, 128], BF16)
        k.ones_f = sb("ones_f", [128, 128], F32)
        k.masku_f = sb("masku_f", [128, 128], F32)
        k.maskl_f = sb("maskl_f", [128, 128], F32)
        k.masku4 = sb("masku4", [128, 1, 128], BF16)
        k.maskl4 = sb("maskl4", [128, 1, 128], BF16)
        k.negu4 = sb("negu4", [128, 1, 128], BF16)
        k.negl4 = sb("negl4", [128, 1, 128], BF16)
        k.e16 = sb("e16", [64, 16], F32)
        k.modT = [sb("modT0", [128, 72, 2], F32), sb("modT1", [128, 72, 2], F32)]
        k.A = [sb("A0", [128, 3, 8, 2], F32), sb("A1", [128, 3, 8, 2], F32)]
        k.G = [sb("G0", [128, 3, 8, 2], F32), sb("G1", [128, 3, 8, 2], F32)]
        k.fgT = sb("fgT", [128, 8], F32)
        k.ghT = sb("ghT", [128, 8], F32)
        k.esT = sb("esT", [128, 8], F32)
        k.bgate = sb("bgate", [128, 32], F32)
        k.d_const = Dep()
        k.psall = es.enter_context(nc.psum_tensor("psall", [128, 4096], F32))
        k.ps = [k.psall[:, i * 512:(i + 1) * 512] for i in range(8)]
        k.dps = [Dep() for _ in range(8)]
        k.dbank = [Dep() for _ in range(8)]

        phases = _phase_list()
        _phase0(k)
        for ph in phases:
            ph(k)
        _barrier(k)
    return nc


def _wslot_views_ffn(k, s):
    W = k.W[s]
    wgu = W[:, 0:22528].rearrange("p (k n) -> p k n", k=8)
    wdn = W[:, 22528:33792].rearrange("p (k n) -> p k n", k=11)
    return wgu, wdn


def _load_ffn_weights(k, s, l, which, half):
    wgu, wdn = _wslot_views_ffn(k, s)
    gu = k.f_gu[which][l].rearrange("(k p) n -> p k n", p=128)
    dn = k.f_dn[which][l]
    c0 = half * 1408
    for kk in range(0, 8, 2):
        k.qg.dma(wgu[:, kk:kk + 2, 0:1408], gu[:, kk:kk + 2, c0:c0 + 1408], writes=[k.dW[s]])
        k.qg.dma(wgu[:, kk:kk + 2, 1408:2816], gu[:, kk:kk + 2, 2816 + c0:2816 + c0 + 1408], writes=[k.dW[s]])
    dnv = dn[c0:c0 + 1408, :].rearrange("(k p) n -> p k n", p=128)
    k.qg.dma(wdn[:, 0:6, :], dnv[:, 0:6, :], writes=[k.dW[s]])
    k.qg.dma(wdn[:, 6:11, :], dnv[:, 6:11, :], writes=[k.dW[s]])


def _load_mlstm_weights(k, s):
    W = k.W[s]
    win = W[:, 0:24832].rearrange("p (k n) -> p k n", k=8)
    wout = W[:, 24832:24832 + 8192].rearrange("p (k n) -> p k n", k=8)
    src = k.m_win.rearrange("(k p) n -> p k n", p=128)
    for kk in range(0, 8, 2):
        k.qg.dma(win[:, kk:kk + 2, :], src[:, kk:kk + 2, :], writes=[k.dW[s]])
    k.qg.dma(wout, k.m_wout.rearrange("(k p) n -> p k n", p=128), writes=[k.dW[s]])
    return win, wout


def _load_attn_weights(k, s):
    W = k.W[s]
    win = W[:, 0:22528].rearrange("p (k n) -> p k n", k=8)
    wout = W[:, 22528:22528 + 8192].rearrange("p (k n) -> p k n", k=8)
    src = k.a_win.rearrange("(k p) n -> p k n", p=128)
    for kk in range(0, 8, 2):
        k.qg.dma(win[:, kk:kk + 2, :], src[:, kk:kk + 2, :], writes=[k.dW[s]])
    k.qg.dma(wout, k.a_wout.rearrange("(k p) n -> p k n", p=128), writes=[k.dW[s]])
    return win, wout


def _phase_list():
    specs = []
    for l in range(2):
        specs.append(("ffn", l, 0, 0))
        specs.append(("ffn", l, 0, 1))
        specs.append(("mix", l))
        specs.append(("ffn", l, 1, 0))
        specs.append(("ffn", l, 1, 1))
    n = len(specs)

    def loader(i):
        sp = specs[i]
        s = i % 2
        if sp[0] == "ffn":
            return lambda k: _load_ffn_weights(k, s, sp[1], sp[2], sp[3])
        if sp[1] == 0:
            return lambda k: _load_mlstm_weights(k, s)
        return lambda k: _load_attn_weights(k, s)

    groups = []
    i = 0
    while i < n:
        if specs[i][0] == "ffn":
            g = [i]
            while i + 1 < n and specs[i + 1][0] == "ffn":
                i += 1
                g.append(i)
            groups.append(g)
        else:
            groups.append([i])
        i += 1

    phases = []
    for g in groups:
        def run(k, g=g):
            g2 = [i for i in g if i < k.upto]
            if not g2:
                return
            if specs[g2[0]][0] == "ffn":
                segs = []
                for i in g2:
                    nxt = loader(i + 1) if (i + 1 < n and i + 1 < k.upto) else None
                    segs.append((i % 2, specs[i][1], specs[i][2], specs[i][3], i == n - 1, nxt))
                _ffn_run(k, segs)
            else:
                i = g2[0]
                if i + 1 < n and i + 1 < k.upto:
                    loader(i + 1)(k)
                if specs[i][1] == 0:
                    _mlstm_phase(k, i % 2)
                else:
                    _attn_phase(k, i % 2)
            _barrier(k)
        phases.append(run)
    return phases


def _xs_tile(k, tt):
    return k.xT_s[:, :, tt * TT:(tt + 1) * TT].rearrange("c p t -> p c t")


def _phase0(k):
    nc = k.nc
    pe, act, dve, pool, qs, qg = k.pe, k.act, k.dve, k.pool, k.qs, k.qg
    dc = k.d_const
    with ExitStack() as st:
        sb = lambda n, s, d: k.sb(n, s, d, st)
        with nc.allow_non_contiguous_dma(reason="small strided constant loads"):
            qs.dma(k.ident_f[:], k.c_ident, writes=[dc])
            qs.dma(k.masku_f[:], k.c_masku, writes=[dc])
            qs.dma(k.maskl_f[:], k.c_maskl, writes=[dc])
            qs.dma(k.e16[:], k.c_e16, writes=[dc])
            qs.dma(k.esT[:], k.a_sink, writes=[dc])
            qs.dma(k.fgT[:], k.final_g.rearrange("(c p) -> p c", p=128), writes=[dc])
            qs.dma(k.ghT[:], k.m_gh.rearrange("(c p) -> p c", p=128), writes=[dc])
            qs.dma(k.bgate[:], k.m_bg.partition_broadcast(128), writes=[dc])
            ngT = sb("ngT", [128, 2, 3, 8], F32)
            for l in range(2):
                for j in range(3):
                    qs.dma(ngT[:, l, j, :], k.norm_g[l, j].rearrange("(c p) -> p c", p=128), writes=[dc])
            sT = sb("sT", [128, 8, 2], F32)
            for c in range(2):
                qs.dma(sT[:, :, c], k.cvec[c].rearrange("(k p) -> p k", p=128), writes=[dc])
            bmT = sb("bmT", [128, 2, 72], F32)
            for l in range(2):
                qs.dma(bmT[:, l, :], k.b_mod[l].rearrange("(j p) -> p j", p=128), writes=[dc])
        pool.op(lambda: nc.gpsimd.memset(k.ones_b[:], 1.0), writes=[dc])
        pool.op(lambda: nc.gpsimd.memset(k.ones_f[:], 1.0), writes=[dc])
        act.op(lambda: nc.scalar.copy(out=k.ident_b[:], in_=k.ident_f[:]), reads=[dc], writes=[dc])
        for i in range(1):
            act.op(lambda: nc.scalar.copy(out=k.masku4[:, i, :], in_=k.masku_f[:]), reads=[dc], writes=[dc])
            act.op(lambda: nc.scalar.copy(out=k.maskl4[:, i, :], in_=k.maskl_f[:]), reads=[dc], writes=[dc])
            dve.op(lambda: nc.vector.tensor_scalar(out=k.negu4[:, i, :], in0=k.masku_f[:], scalar1=-1.0, scalar2=30000.0, op0=ALU.add, op1=ALU.mult),
                   reads=[dc], writes=[dc])
            dve.op(lambda: nc.vector.tensor_scalar(out=k.negl4[:, i, :], in0=k.maskl_f[:], scalar1=-1.0, scalar2=30000.0, op0=ALU.add, op1=ALU.mult),
                   reads=[dc], writes=[dc])
        act.op(lambda: nc.scalar.activation(out=k.esT[:], in_=k.esT[:], func=AF.Exp), reads=[dc], writes=[dc])
        sTb = sb("sTb", [128, 8, 2], BF16)
        act.op(lambda: nc.scalar.activation(out=sTb[:], in_=sT[:], func=AF.Silu), reads=[dc], writes=[dc])

        stg = [sb("stg0", [128, 1024], F32), sb("stg1", [128, 1024], F32)]
        dstg = [Dep(), Dep()]
        xt = [sb("p0xt0", [128, 8, 512], F32), sb("p0xt1", [128, 8, 512], F32)]
        dxt = [Dep(), Dep()]
        n = 0
        for tt in range(NTILE):
            xs = tt % 2
            for tb in range(4):
                s = n % 2
                n += 1
                t0 = tt * TT + tb * 128
                qs.dma(stg[s][:], k.x_in[t0:t0 + 128, :], writes=[dstg[s]])
                for hb in range(2):
                    pi = 4 + 2 * (tb % 2) + hb
                    pv = k.ps[pi][:].rearrange("p (a b) -> p a b", a=4)
                    for c4 in range(4):
                        c = hb * 4 + c4
                        pe.op(lambda: nc.tensor.transpose(pv[:, c4, :], stg[s][:, c * 128:(c + 1) * 128], k.ident_f[:]),
                              reads=[dstg[s], dc], writes=[k.dps[pi]])
                    eng = act if hb == 0 else dve
                    if hb == 0:
                        act.op(lambda: nc.scalar.copy(out=xt[xs][:, 0:4, tb * 128:(tb + 1) * 128], in_=pv),
                               reads=[k.dps[pi]], writes=[dxt[xs]])
                    else:
                        dve.op(lambda: nc.vector.tensor_copy(out=xt[xs][:, 4:8, tb * 128:(tb + 1) * 128], in_=pv),
                               reads=[k.dps[pi]], writes=[dxt[xs]])
            qs.dma(_xs_tile(k, tt), xt[xs][:], reads=[dxt[xs]], writes=[k.d_xs[tt]])
        modtok = sb("modtok", [2, 4608], F32)
        d_mt = Dep()
        NB = 1152
        wv = [k.W[1][:, i * 9216:(i + 1) * 9216].rearrange("p (k n) -> p k n", k=8) for i in range(3)]
        dwv = [Dep() for _ in range(3)]
        tmpA = sb("tmpA", [128, 8, 2], F32)
        d_tmpA = Dep()
        bi = 0
        for l in range(2):
            src = k.w_mod[l].rearrange("(k p) n -> p k n", p=128)
            for hh in range(2):
                for b4 in range(4):
                    b = hh * 4 + b4
                    s = bi % 3
                    bi += 1
                    qg.dma(wv[s], src[:, :, b * NB:(b + 1) * NB], writes=[dwv[s]])
                    if bi == 3:
                        _load_ffn_weights(k, 0, 0, 0, 0)
                    for cg in range(3):
                        pi = 1 + (cg % 2)
                        for kk in range(8):
                            pe.op(lambda: nc.tensor.matmul(k.ps[pi][0:2, 0:384], sTb[:, kk, :], wv[s][:, kk, cg * 384:(cg + 1) * 384],
                                                           start=(kk == 0), stop=(kk == 7)),
                                  reads=[dc, dwv[s]], writes=[k.dps[pi]], inc=(kk == 7))
                        c0 = b4 * NB + cg * 384
                        dve.op(lambda: nc.vector.tensor_copy(out=modtok[0:2, c0:c0 + 384], in_=k.ps[pi][0:2, 0:384]),
                               reads=[k.dps[pi]], writes=[d_mt])
                pT = k.ps[3][:, 0:72].rearrange("p (j c) -> p j c", c=2)
                for j in range(36):
                    pe.op(lambda: nc.tensor.matmul(pT[:, j, :], modtok[0:2, j * 128:(j + 1) * 128], k.ident_f[0:2, 0:2],
                                                   start=True, stop=True),
                          reads=[d_mt, dc], writes=[k.dps[3]])
                dve.op(lambda: nc.vector.tensor_tensor(out=k.modT[l][:, hh * 36:(hh + 1) * 36, :], in0=pT,
                                                       in1=bmT[:, l, hh * 36:(hh + 1) * 36].unsqueeze(2).to_broadcast([128, 36, 2]),
                                                       op=ALU.add),
                       reads=[k.dps[3], dc], writes=[dc])
            for j in range(3):
                dve.op(lambda: nc.vector.tensor_scalar(out=tmpA[:], in0=k.modT[l][:, (3 * j + 1) * 8:(3 * j + 2) * 8, :],
                                                       scalar1=1.0, scalar2=None, op0=ALU.add),
                       reads=[dc], writes=[d_tmpA])
                dve.op(lambda: nc.vector.tensor_tensor(out=k.A[l][:, j], in0=tmpA[:],
                                                       in1=ngT[:, l, j, :].unsqueeze(2).to_broadcast([128, 8, 2]), op=ALU.mult),
                       reads=[d_tmpA, dc], writes=[dc])
                dve.op(lambda: nc.vector.tensor_scalar(out=k.G[l][:, j], in0=k.modT[l][:, (3 * j + 2) * 8:(3 * j + 3) * 8, :],
                                                       scalar1=(1.0 if j == 1 else 0.5), scalar2=None, op0=ALU.mult),
                       reads=[dc], writes=[dc])

        _barrier(k)


def _norm_stat(k, B, xt, dxt, c):
    nc = k.nc
    s = c % 2
    k.act.op(lambda: nc.scalar.activation(out=B.sqc[s][:], in_=xt[:, c, :], func=AF.Square),
             reads=[dxt[c]], writes=[B.dsq[s]])
    k.pe.op(lambda: nc.tensor.matmul(k.ps[0][:], k.ones_b[:], B.sqc[s][:], start=(c == 0), stop=(c == 7)),
            reads=[B.dsq[s], k.d_const], writes=[k.dps[0]])


def _norm_rstd(k, B):
    nc = k.nc
    k.act.op(lambda: nc.scalar.activation(out=B.rstd[:], in_=k.ps[0][:], func=AF.Sqrt, bias=B.epsc[:, 0:1], scale=1.0 / 1024),
             reads=[k.dps[0]], writes=[B.drstd])
    k.dve.op(lambda: nc.vector.reciprocal(out=B.rstd[:], in_=B.rstd[:]), reads=[B.drstd], writes=[B.drstd])


def _norm_mod(k, B, xt, dxt, l, j, cond, hT, dhT, c):
    nc = k.nc
    s = c % 2
    k.dve.op(lambda: nc.vector.scalar_tensor_tensor(out=B.tmp[s][:], in0=xt[:, c, :],
                                                    scalar=k.A[l][:, j, c, cond:cond + 1], in1=B.rstd[:],
                                                    op0=ALU.mult, op1=ALU.mult),
             reads=[dxt[c], B.drstd, k.d_const], writes=[B.dtmp[s]])
    k.act.op(lambda: nc.scalar.activation(out=hT[:, c, :], in_=B.tmp[s][:], func=AF.Identity,
                                          bias=k.modT[l][:, 3 * j * 8 + c, cond:cond + 1], scale=1.0),
             reads=[B.dtmp[s], k.d_const], writes=[dhT[c]])


def _modnorm(k, B, xt, dxt, l, j, cond, hT, dhT):
    for c in range(8):
        _norm_stat(k, B, xt, dxt, c)
    _norm_rstd(k, B)
    if hT is None:
        return
    for c in range(8):
        _norm_mod(k, B, xt, dxt, l, j, cond, hT, dhT, c)


def _emit_units_pipelined(k, B, units, tt, xt1, dxt1, l, j, hTs, dhTs):
    n = len(units)
    pipe = tt + 1 < NTILE
    if pipe:
        k.qs.dma(xt1[:], _xs_tile(k, tt + 1), reads=[k.d_xs[tt + 1]], writes=dxt1)
    s0 = max(0, n - 20)
    ncond = 0 if tt + 1 < 8 else 1
    for gi, u in enumerate(units):
        u()
        if pipe:
            if s0 <= gi < s0 + 8:
                _norm_stat(k, B, xt1, dxt1, gi - s0)
            if gi == s0 + 7:
                _norm_rstd(k, B)
            if s0 + 8 <= gi < s0 + 16:
                _norm_mod(k, B, xt1, dxt1, l, j, ncond, hTs[(tt + 1) % 2], dhTs[(tt + 1) % 2], gi - s0 - 8)
    assert n >= s0 + 16


def _norm_bufs(k, st):
    B = K()
    B.sqc = [k.sb("sqc0", [128, TT], BF16, st), k.sb("sqc1", [128, TT], BF16, st)]
    B.dsq = [Dep(), Dep()]
    B.rstd = k.sb("rstd", [128, TT], F32, st)
    B.drstd = Dep()
    B.tmp = [k.sb("ntmp0", [128, TT], F32, st), k.sb("ntmp1", [128, TT], F32, st)]
    B.dtmp = [Dep(), Dep()]
    B.epsc = k.sb("epsc", [128, 1], F32, st)
    k.pool.op(lambda: k.nc.gpsimd.memset(B.epsc[:], EPS), writes=[B.drstd])
    return B


def _ffn_run(k, segs):
    nc = k.nc
    pe, act, dve, pool, qs = k.pe, k.act, k.dve, k.pool, k.qs
    tiles = []
    for si, sg_ in enumerate(segs):
        for tt in range(NTILE):
            tiles.append((si, tt))
    with ExitStack() as st:
        B = _norm_bufs(k, st)
        xt = [k.sb("xt0", [128, 8, TT], F32, st), k.sb("xt1", [128, 8, TT], F32, st)]
        dxt = [[Dep() for _ in range(8)] for _ in range(2)]
        hTs = [k.sb("hT0", [128, 8, TT], BF16, st), k.sb("hT1", [128, 8, TT], BF16, st)]
        dhTs = [[Dep() for _ in range(8)] for _ in range(2)]
        actT = k.sb("actT", [128, 11, TT], BF16, st)
        dact = [Dep() for _ in range(11)]
        sg = k.sb("sg", [128, TT], F32, st)
        dsg = Dep()
        dyv = [Dep(), Dep()]

        def hview(tt):
            return k.hT_s[:, :, tt * TT:(tt + 1) * TT].rearrange("c p t -> p c t")

        def cond_of(tt):
            return 0 if tt < 8 else 1

        def par(g):
            si, tt = tiles[g]
            slot, l, which, half, final, loader = segs[si]
            return slot, l, (0 if which == 0 else 2), half, final, tt

        slot, l, j, half, final, tt = par(0)
        qs.dma(xt[0][:], _xs_tile(k, tt), reads=[k.d_xs[tt]], writes=dxt[0])
        if half == 0:
            _modnorm(k, B, xt[0], dxt[0], l, j, cond_of(tt), hTs[0], dhTs[0])
            qs.dma(hview(tt), hTs[0][:], reads=dhTs[0], writes=[k.d_hs[tt]])
        else:
            qs.dma(hTs[0][:], hview(tt), reads=[k.d_hs[tt]], writes=dhTs[0])
        ny = 0
        for g in range(len(tiles)):
            slot, l, j, half, final, tt = par(g)
            si = tiles[g][0]
            if tt == 0 and segs[si][5] is not None:
                segs[si][5](k)
            wgu, wdn = _wslot_views_ffn(k, slot)
            dW = k.dW[slot]
            xs = g % 2
            cond = cond_of(tt)
            X, dX = xt[xs], dxt[xs]
            hT, dhT = hTs[xs], dhTs[xs]
            nxt = g + 1 < len(tiles)
            pipe = False
            if nxt:
                nslot, nl, nj, nhalf, nfinal, ntt = par(g + 1)
                qs.dma(xt[1 - xs][:], _xs_tile(k, ntt), reads=[k.d_xs[ntt]], writes=dxt[1 - xs])
                if nhalf == 1:
                    qs.dma(hTs[1 - xs][:], hview(ntt), reads=[k.d_hs[ntt]], writes=dhTs[1 - xs])
                pipe = nhalf == 0
            for fc in range(11):
                pg = 1 + 2 * (fc % 2)
                pu = pg + 1
                for kk in range(8):
                    pe.op(lambda: nc.tensor.matmul(k.ps[pg], wgu[:, kk, fc * 128:(fc + 1) * 128], hT[:, kk, :],
                                                   start=(kk == 0), stop=(kk == 7)),
                          reads=[dW, dhT[kk]], writes=[k.dps[pg]], inc=(kk == 7))
                for kk in range(8):
                    pe.op(lambda: nc.tensor.matmul(k.ps[pu], wgu[:, kk, 1408 + fc * 128:1408 + (fc + 1) * 128], hT[:, kk, :],
                                                   start=(kk == 0), stop=(kk == 7)),
                          reads=[dW, dhT[kk]], writes=[k.dps[pu]], inc=(kk == 7))
                act.op(lambda: nc.scalar.activation(out=sg[:], in_=k.ps[pg], func=AF.Silu),
                       reads=[k.dps[pg]], writes=[dsg])
                dve.op(lambda: nc.vector.tensor_tensor(out=actT[:, fc, :], in0=k.ps[pu], in1=sg[:], op=ALU.mult),
                       reads=[k.dps[pu], dsg], writes=[dact[fc]])
                if pipe and fc >= 7:
                    for c in (2 * (fc - 7), 2 * (fc - 7) + 1):
                        _norm_stat(k, B, xt[1 - xs], dxt[1 - xs], c)
            if pipe:
                _norm_rstd(k, B)
            for dc in range(8):
                po = 5 + (dc % 2)
                for fc in range(11):
                    pe.op(lambda: nc.tensor.matmul(k.ps[po], wdn[:, fc, dc * 128:(dc + 1) * 128], actT[:, fc, :],
                                                   start=(fc == 0), stop=(fc == 10)),
                          reads=[dW, dact[fc]], writes=[k.dps[po]], inc=(fc == 10))
                dve.op(lambda: nc.vector.scalar_tensor_tensor(out=X[:, dc, :], in0=k.ps[po],
                                                              scalar=k.G[l][:, j, dc, cond:cond + 1], in1=X[:, dc, :],
                                                              op0=ALU.mult, op1=ALU.add),
                       reads=[k.dps[po], dX[dc], k.d_const], writes=[dX[dc]])
                if pipe:
                    _norm_mod(k, B, xt[1 - xs], dxt[1 - xs], nl, nj, cond_of(ntt), hTs[1 - xs], dhTs[1 - xs], dc)
            if pipe:
                qs.dma(hview(ntt), hTs[1 - xs][:], reads=dhTs[1 - xs], writes=[k.d_hs[ntt]])
            if not final:
                qs.dma(_xs_tile(k, tt), X[:], reads=dX, writes=[k.d_xs[tt]])
            else:
                yv = k.W[1 - slot][:, 0:4096].bitcast(F32).rearrange("p (a n) -> p a n", a=2)
                _modnorm(k, B, X, dX, l, j, cond, None, None)
                for c in range(8):
                    dve.op(lambda: nc.vector.scalar_tensor_tensor(out=X[:, c, :], in0=X[:, c, :], scalar=k.fgT[:, c:c + 1],
                                                                  in1=B.rstd[:], op0=ALU.mult, op1=ALU.mult),
                           reads=[dX[c], B.drstd, k.d_const], writes=[dX[c]])
                for tb in range(4):
                    ys = ny % 2
                    ny += 1
                    for hb in range(2):
                        pi = 1 + 2 * (tb % 2) + hb
                        pv = k.ps[pi].rearrange("p (a b) -> p a b", a=4)
                        for c4 in range(4):
                            c = hb * 4 + c4
                            pe.op(lambda: nc.tensor.transpose(pv[:, c4, :], X[:, c, tb * 128:(tb + 1) * 128], k.ident_f[:]),
                                  reads=[dX[c], k.d_const], writes=[k.dps[pi]])
                        if hb == 0:
                            act.op(lambda: nc.scalar.copy(out=yv[:, ys, 0:512], in_=k.ps[pi]), reads=[k.dps[pi]], writes=[dyv[ys], k.dW[1 - slot]])
                        else:
                            dve.op(lambda: nc.vector.tensor_copy(out=yv[:, ys, 512:1024], in_=k.ps[pi]), reads=[k.dps[pi]], writes=[dyv[ys], k.dW[1 - slot]])
                    t0 = tt * TT + tb * 128
                    qs.dma(k.y_out[t0:t0 + 128, :], yv[:, ys, :], reads=[dyv[ys]], writes=[k.d_out])


def _mlstm_phase(k, s):
    nc = k.nc
    pe, act, dve, pool, qs = k.pe, k.act, k.dve, k.pool, k.qs
    W = k.W[s]
    win = W[:, 0:24832].rearrange("p (k n) -> p k n", k=8)
    wout = W[:, 24832:24832 + 8192].rearrange("p (k n) -> p k n", k=8)
    dW = k.dW[s]
    dc_ = k.d_const
    l, j = 0, 1
    with ExitStack() as st0:
        gates = k.sb("gates", [128, NCH, 32], F32, st0)
        d_gates = Dep()
        cs = k.sb("cs", [128, NCH, 16], F32, st0)
        ecl = k.sb("ecl", [128, NCH, 16], F32, st0)
        gsel = k.sb("gsel", [128, NCH, 2, 4], F32, st0)
        cs_bf = k.sb("cs_bf", [128, NCH, 16], BF16, st0)
        d_cs, d_ecl, d_gsel = Dep(), Dep(), Dep()

        with ExitStack() as st:
            B = _norm_bufs(k, st)
            xt1 = k.sb("xtP", [128, 8, TT], F32, st)
            dxt1 = [Dep() for _ in range(8)]
            hTs = [k.sb("hT0", [128, 8, TT], BF16, st), k.sb("hT1", [128, 8, TT], BF16, st)]
            dhTs = [[Dep() for _ in range(8)] for _ in range(2)]
            stage = [k.sb("stage%d" % i, [128, 4, TT], BF16, st) for i in range(2)]
            dstage = [Dep() for _ in range(2)]
            nst = [0]
            npb = [0]
            nev = [0]

            def next_stage():
                i = nst[0] % 2
                nst[0] += 1
                return stage[i], dstage[i]

            def next_bank():
                i = 1 + npb[0] % 4
                npb[0] += 1
                return i

            def evac(out, pi, func=None, scale=1.0, wdep=None):
                if func is not None or nev[0] % 2 == 0:
                    f = func if func is not None else AF.Copy
                    act.op(lambda: nc.scalar.activation(out=out, in_=k.ps[pi], func=f, scale=scale),
                           reads=[k.dps[pi]], writes=[wdep])
                else:
                    dve.op(lambda: nc.vector.tensor_copy(out=out, in_=k.ps[pi]), reads=[k.dps[pi]], writes=[wdep])
                nev[0] += 1

            def units_for(tt, hT, dhT):
                t0 = tt * TT
                units = []

                def fmaj(col0, nch, func, scale, dst, ddst):
                    for g4 in range(nch // 4):
                        holder = {}
                        for f4 in range(4):
                            def u(g4=g4, f4=f4, holder=holder):
                                if f4 == 0:
                                    holder["s"] = next_stage()
                                sg_, dsg_ = holder["s"]
                                fc = g4 * 4 + f4
                                pi = next_bank()
                                for kk in range(8):
                                    pe.op(lambda: nc.tensor.matmul(k.ps[pi], win[:, kk, col0 + fc * 128:col0 + (fc + 1) * 128], hT[:, kk, :],
                                                                   start=(kk == 0), stop=(kk == 7)),
                                          reads=[dW, dhT[kk]], writes=[k.dps[pi]], inc=(kk == 7))
                                evac(sg_[:, f4, :], pi, func, scale, dsg_)
                                if f4 == 3:
                                    qs.dma(dst[g4 * 4:(g4 + 1) * 4, :, t0:t0 + TT].rearrange("c p t -> p c t"), sg_[:], reads=[dsg_], writes=[ddst[tt]])
                            units.append(u)

                def tmaj(col0, dst, dcol0, ddst):
                    holder = {}
                    for tb in range(4):
                        def u(tb=tb, holder=holder):
                            if tb == 0:
                                holder["s"] = next_stage()
                            sg_, dsg_ = holder["s"]
                            pi = next_bank()
                            for kk in range(8):
                                pe.op(lambda: nc.tensor.matmul(k.ps[pi], hT[:, kk, tb * 128:(tb + 1) * 128], win[:, kk, col0:col0 + 512],
                                                               start=(kk == 0), stop=(kk == 7)),
                                      reads=[dW, dhT[kk]], writes=[k.dps[pi]], inc=(kk == 7))
                            evac(sg_[:, tb, :], pi, None, 1.0, dsg_)
                            if tb == 3:
                                qs.dma(dst[t0:t0 + TT, dcol0:dcol0 + 512].rearrange("(b t) f -> t b f", t=128), sg_[:], reads=[dsg_], writes=[ddst[tt]])
                        units.append(u)

                fmaj(0, 4, AF.Copy, 0.125, k.qT_s, k.d_q)
                fmaj(512, 4, None, 1.0, k.kT_s, k.d_k)
                fmaj(2048, 8, AF.Sigmoid, 1.0, k.oT_s, k.d_o)
                tmaj(512, k.kt_s, 0, k.d_kt)
                tmaj(1024, k.vt_s, 0, k.d_vt)
                tmaj(1536, k.vt_s, 512, k.d_vt)
                for tb in range(4):
                    def u(tb=tb):
                        pgt = k.ps[5][:, tb * 32:(tb + 1) * 32]
                        for kk in range(8):
                            pe.op(lambda: nc.tensor.matmul(pgt, hT[:, kk, tb * 128:(tb + 1) * 128], win[:, kk, 3072:3104],
                                                           start=(kk == 0), stop=(kk == 7)),
                                  reads=[dW, dhT[kk]], writes=[k.dps[5]], inc=(kk == 7))
                        dve.op(lambda: nc.vector.tensor_tensor(out=gates[:, tt * 4 + tb, :], in0=pgt, in1=k.bgate[:], op=ALU.add),
                               reads=[k.dps[5], dc_], writes=[d_gates])
                    units.append(u)
                return units

            qs.dma(xt1[:], _xs_tile(k, 0), reads=[k.d_xs[0]], writes=dxt1)
            _modnorm(k, B, xt1, dxt1, l, j, 0, hTs[0], dhTs[0])
            for tt in range(NTILE):
                units = units_for(tt, hTs[tt % 2], dhTs[tt % 2])
                _emit_units_pipelined(k, B, units, tt, xt1, dxt1, l, j, hTs, dhTs)
        _barrier(k)
        if getattr(k, "dbg_stop", "") == "P":
            return

        SEQS = [(list(range(0, 32)), 46, None)] + [([32 + 2 * i, 33 + 2 * i], 47, i) for i in range(4)]
        with ExitStack() as st:
            nlf = k.sb("nlf", [128, NCH, 64], F32, st)
            apad = k.sb("apad", [128, NCH, 64], F32, st)
            nbS = k.sb("nbS", [128, NCH, 16], F32, st)
            pre = k.sb("pre", [128, NCH, 16], F32, st)
            pre2 = k.sb("pre2", [128, NCH, 16], F32, st)
            amaxT = k.sb("amaxT", [64, NCH], F32, st)
            totS = k.sb("totS", [64, NCH], F32, st)
            MT = k.sb("MT", [64, NCH], F32, st)
            dG = k.sb("dG", [64, NCH], F32, st)
            Gx = k.sb("Gx", [64, NCH], F32, st)
            mst = [k.sb("mstF", [64, 48], F32, st), k.sb("mstB", [64, 48], F32, st)]
            RM = k.sb("RM", [64, NCH, 16], F32, st)
            RG = k.sb("RG", [64, NCH, 16], F32, st)
            d_nlf, d_apad, d_nbS, d_pre, d_pre2, d_amax, d_tot, d_RM, d_RG, d_Gx = [Dep() for _ in range(10)]
            d_ch = [Dep(), Dep()]
            pool.op(lambda: nc.gpsimd.memset(nlf[:], 0.0), writes=[d_nlf])
            pool.op(lambda: nc.gpsimd.memset(apad[:], 0.0), writes=[d_apad])
            pool.op(lambda: nc.gpsimd.memset(MT[:], 0.0), writes=[d_ch[0], d_ch[1]])
            pool.op(lambda: nc.gpsimd.memset(dG[:], 0.0), writes=[d_ch[0], d_ch[1]])
            for d in range(2):
                pool.op(lambda: nc.gpsimd.memset(mst[d][:], 0.0), writes=[d_ch[d]])
            with nc.allow_non_contiguous_dma(reason="tiny state loads"):
                for d in range(2):
                    qs.dma(mst[d][32 * d:32 * d + 8, 46:47], k.st_m[d].rearrange("(h o) -> h o", o=1), writes=[d_ch[d]])
            for d in range(2):
                act.op(lambda: nc.scalar.activation(out=nlf[:, :, 32 * d:32 * d + 8], in_=gates[:, :, 16 * d + 8:16 * d + 16],
                                                    func=AF.Exp, scale=-1.0), reads=[d_gates, d_nlf], writes=[d_nlf])
            for d in range(2):
                act.op(lambda: nc.scalar.activation(out=nlf[:, :, 32 * d:32 * d + 8], in_=nlf[:, :, 32 * d:32 * d + 8],
                                                    func=AF.Ln, bias=1.0, scale=1.0), reads=[d_nlf], writes=[d_nlf])
            if getattr(k, "dbg_stop", "") == "G1":
                return
            pb = [k.ps[1][:, 0:320].rearrange("p (c h) -> p c h", h=8), k.ps[2][:, 0:320].rearrange("p (c h) -> p c h", h=8)]
            pe.op(lambda: nc.tensor.matmul(pb[0], k.masku_f[:], nlf[:, :, 0:8], start=True, stop=True),
                  reads=[d_nlf, dc_], writes=[k.dps[1]])
            if getattr(k, "dbg_stop", "") == "G2a":
                return
            pe.op(lambda: nc.tensor.matmul(pb[1], k.maskl_f[:], nlf[:, :, 32:40], start=True, stop=True),
                  reads=[d_nlf, dc_], writes=[k.dps[2]])
            if getattr(k, "dbg_stop", "") == "G2b":
                return
            for d in range(2):
                dve.op(lambda: nc.vector.tensor_tensor(out=apad[:, :, 32 * d:32 * d + 8], in0=pb[d], in1=gates[:, :, 16 * d:16 * d + 8], op=ALU.add),
                       reads=[k.dps[1 + d], d_gates, d_apad], writes=[d_apad])
                if getattr(k, "dbg_stop", "") == "G2c":
                    return
                dve.op(lambda: nc.vector.tensor_copy(out=nbS[:, :, 8 * d:8 * d + 8], in_=pb[d]), reads=[k.dps[1 + d]], writes=[d_nbS])
            if getattr(k, "dbg_stop", "") == "G2":
                return
            for g in range(NCH // 4):
                pi = 3 + g % 2
                pv = k.ps[pi][:].rearrange("p (a b) -> p a b", a=4)
                for i in range(4):
                    c = 4 * g + i
                    pe.op(lambda: nc.tensor.transpose(pv[0:64, i, :], apad[:, c, :], k.ident_f[:]), reads=[d_apad, dc_], writes=[k.dps[pi]])
                dve.op(lambda: nc.vector.reduce_max(out=amaxT[:, 4 * g:4 * g + 4], in_=pv[0:64, :, :], axis=AX.X),
                       reads=[k.dps[pi]], writes=[d_amax])
            if getattr(k, "dbg_stop", "") == "G3":
                return
            for c in range(NCH):
                pe.op(lambda: nc.tensor.matmul(k.ps[5][0:64, c:c + 1], nlf[:, c, :], k.ones_f[:, 0:1], start=True, stop=True),
                      reads=[d_nlf, dc_], writes=[k.dps[5]])
            act.op(lambda: nc.scalar.copy(out=totS[:], in_=k.ps[5][0:64, 0:NCH]), reads=[k.dps[5]], writes=[d_tot])
            if getattr(k, "dbg_stop", "") == "G4":
                return
            colp = [0, 0]
            for (chunks, init_col, pseq) in SEQS:
                for d in range(2):
                    eng = dve
                    E = nc.vector
                    r = slice(32 * d, 32 * d + 8)
                    order = chunks if d == 0 else chunks[::-1]
                    cur = init_col
                    for c in order:
                        eng.op(lambda: E.tensor_tensor(out=MT[r, c:c + 1], in0=mst[d][r, cur:cur + 1], in1=amaxT[r, c:c + 1], op=ALU.max),
                               reads=[d_ch[d], d_amax], writes=[d_ch[d]])
                        eng.op(lambda: E.tensor_tensor(out=dG[r, c:c + 1], in0=mst[d][r, cur:cur + 1], in1=MT[r, c:c + 1], op=ALU.subtract),
                               reads=[d_ch[d]], writes=[d_ch[d]])
                        nxt = colp[d]
                        colp[d] += 1
                        eng.op(lambda: E.tensor_tensor(out=mst[d][r, nxt:nxt + 1], in0=MT[r, c:c + 1], in1=totS[r, c:c + 1], op=ALU.subtract),
                               reads=[d_ch[d], d_tot], writes=[d_ch[d]])
                        cur = nxt
                    if pseq is not None:
                        with nc.allow_non_contiguous_dma(reason="tiny state store"):
                            qs.dma(k.nsm[pseq, d, :].rearrange("(h o) -> h o", o=1), mst[d][r, cur:cur + 1], reads=[d_ch[d]], writes=[k.d_out])
            if getattr(k, "dbg_stop", "") == "G5":
                return
            act.op(lambda: nc.scalar.activation(out=Gx[:], in_=dG[:], func=AF.Exp), reads=[d_ch[0], d_ch[1]], writes=[d_Gx])
            e16b = k.e16[:].unsqueeze(1).to_broadcast([64, NCH, 16])
            dve.op(lambda: nc.vector.tensor_tensor(out=RM[:], in0=e16b, in1=MT[:].unsqueeze(2).to_broadcast([64, NCH, 16]), op=ALU.mult),
                   reads=[d_ch[0], d_ch[1], dc_], writes=[d_RM])
            dve.op(lambda: nc.vector.tensor_tensor(out=RG[:], in0=e16b, in1=Gx[:].unsqueeze(2).to_broadcast([64, NCH, 16]), op=ALU.mult),
                   reads=[d_Gx, dc_], writes=[d_RG])
            if getattr(k, "dbg_stop", "") == "G6":
                return
            Mv, Gv = [], []
            for hh in range(2):
                pm = k.ps[1 + hh][:, 0:320].rearrange("p (c h) -> p c h", h=16)
                pg = k.ps[3 + hh][:, 0:320].rearrange("p (c h) -> p c h", h=16)
                pe.op(lambda: nc.tensor.matmul(pm, k.ones_f[0:64, :], RM[:, 20 * hh:20 * hh + 20, :], start=True, stop=True),
                      reads=[d_RM, dc_], writes=[k.dps[1 + hh]])
                pe.op(lambda: nc.tensor.matmul(pg, k.ones_f[0:64, :], RG[:, 20 * hh:20 * hh + 20, :], start=True, stop=True),
                      reads=[d_RG, dc_], writes=[k.dps[3 + hh]])
                Mv.append(pm)
                Gv.append(pg)
            if getattr(k, "dbg_stop", "") == "G7":
                return
            for hh in range(2):
                cr = slice(20 * hh, 20 * hh + 20)
                for d in range(2):
                    dve.op(lambda: nc.vector.tensor_tensor(out=pre[:, cr, 8 * d:8 * d + 8], in0=apad[:, cr, 32 * d:32 * d + 8],
                                                           in1=Mv[hh][:, :, 8 * d:8 * d + 8], op=ALU.subtract),
                           reads=[d_apad, k.dps[1 + hh]], writes=[d_pre])
                dve.op(lambda: nc.vector.tensor_tensor(out=pre2[:, cr, :], in0=nbS[:, cr, :], in1=Mv[hh], op=ALU.subtract),
                       reads=[d_nbS, k.dps[1 + hh]], writes=[d_pre2])
                gv5 = k.ps[3 + hh][:, 0:320].rearrange("p (c d h t) -> p c d h t", d=2, h=4, t=2)
                dve.op(lambda: nc.vector.tensor_copy(out=gsel[0:64, cr, :, :], in_=gv5[0:64, :, :, :, 0]), reads=[k.dps[3 + hh]], writes=[d_gsel])
                dve.op(lambda: nc.vector.tensor_copy(out=gsel[64:128, cr, :, :], in_=gv5[64:128, :, :, :, 1]), reads=[k.dps[3 + hh]], writes=[d_gsel])
            if getattr(k, "dbg_stop", "") == "G8":
                return
            if getattr(k, "dbg_stop", "") == "G9":
                _barrier(k)
            act.op(lambda: nc.scalar.activation(out=cs[:], in_=pre[:], func=AF.Exp), reads=[d_pre], writes=[d_cs])
            if getattr(k, "dbg_stop", "") == "G10":
                return
            act.op(lambda: nc.scalar.activation(out=ecl[:], in_=pre2[:], func=AF.Exp), reads=[d_pre2], writes=[d_ecl])
            dve.op(lambda: nc.vector.tensor_copy(out=cs_bf[:], in_=cs[:]), reads=[d_cs], writes=[d_cs])
        _barrier(k)
        if getattr(k, "dbg_stop", "") == "G":
            return

        with ExitStack() as st:
            sbl = lambda n, sh, dt: k.sb(n, sh, dt, st)
            qTc = [[sbl("qTc%d%d" % (d, i), [128, 4, 128], BF16) for i in range(2)] for d in range(2)]
            kTc = [[sbl("kTc%d%d" % (d, i), [128, 4, 128], BF16) for i in range(2)] for d in range(2)]
            ktc = [[sbl("ktc%d%d" % (d, i), [128, 8, 64], BF16) for i in range(2)] for d in range(2)]
            vtc = [[sbl("vtc%d%d" % (d, i), [128, 8, 128], BF16) for i in range(2)] for d in range(2)]
            d_ld = [[Dep(), Dep()] for d in range(2)]
            vp = [sbl("vp%d" % d, [128, 8, 128], BF16) for d in range(2)]
            d_vp = [Dep(), Dep()]
            C = [sbl("C%d" % d, [128, 4, 129], F32) for d in range(2)]
            Cg = [sbl("Cg%d" % d, [128, 4, 129], F32) for d in range(2)]
            Cgb = [sbl("Cgb%d" % d, [128, 4, 129], BF16) for d in range(2)]
            d_C, d_Cg, d_Cgb = [Dep(), Dep()], [Dep(), Dep()], [Dep(), Dep()]
            PT = [sbl("PT%d" % d, [128, 4, 2, 128], BF16) for d in range(2)]
            d_PT = [[Dep(), Dep()], [Dep(), Dep()]]
            ad = [sbl("ad%d" % d, [128, 8], F32) for d in range(2)]
            rr = [sbl("rr%d" % d, [128, 8], F32) for d in range(2)]
            d_ad, d_rr = [Dep(), Dep()], [Dep(), Dep()]
            hbuf = [[sbl("hbuf%d%d" % (d, i), [128, 8, 128], BF16) for i in range(2)] for d in range(2)]
            d_hbuf = [[Dep(), Dep()], [Dep(), Dep()]]
            NUM = [k.ps[4].rearrange("p (a b) -> p a b", a=4), k.ps[5].rearrange("p (a b) -> p a b", a=4)]
            DEN = k.ps[6][:, 0:8]
            UNn = k.ps[6][:, 8:12]
            UN = k.ps[7].rearrange("p (a b) -> p a b", a=4)
            d_NUM, d_DEN, d_UN = [Dep(), Dep()], Dep(), Dep()
            nld = [0, 0]

            def stageL(d, c):
                tt = c // 4
                t0 = c * 128
                s2 = nld[d] % 2
                nld[d] += 1
                qs.dma(qTc[d][s2][:], k.qT_s[0:4, :, t0:t0 + 128].rearrange("c p t -> p c t"), reads=[k.d_q[tt]], writes=[d_ld[d][s2]])
                qs.dma(kTc[d][s2][:], k.kT_s[0:4, :, t0:t0 + 128].rearrange("c p t -> p c t"), reads=[k.d_k[tt]], writes=[d_ld[d][s2]])
                qs.dma(ktc[d][s2][:], k.kt_s[t0:t0 + 128, :].rearrange("t (h e) -> t h e", h=8), reads=[k.d_kt[tt]], writes=[d_ld[d][s2]])
                qs.dma(vtc[d][s2][:], k.vt_s[t0:t0 + 128, :].rearrange("t (h e) -> t h e", h=8), reads=[k.d_vt[tt]], writes=[d_ld[d][s2]])
                return s2

            def stageA(d, c, s2):
                for h in range(8):
                    hp, par = h // 2, h % 2
                    rs_ = slice(64 * par, 64 * par + 64)
                    bank = 2 * d + par
                    STv = k.ps[bank].rearrange("p (a b) -> p a b", a=4)
                    pe.op(lambda: nc.tensor.matmul(STv[:, hp, :], kTc[d][s2][rs_, hp, :], qTc[d][s2][rs_, hp, :], start=True, stop=True),
                          reads=[d_ld[d][s2]], writes=[k.dps[bank]])

            def stageB(d, c, s2):
                mask = k.masku4 if d == 0 else k.maskl4
                dve.op(lambda: nc.vector.tensor_tensor(out=Cg[d][:], in0=C[d][:], in1=gsel[:, c, d, :].unsqueeze(2).to_broadcast([128, 4, 129]), op=ALU.mult),
                       reads=[d_C[d], d_gsel], writes=[d_Cg[d]])
                act.op(lambda: nc.scalar.copy(out=Cgb[d][:], in_=Cg[d][:]), reads=[d_Cg[d]], writes=[d_Cgb[d]])
                for par in range(2):
                    bank = 2 * d + par
                    STv = k.ps[bank].rearrange("p (a b) -> p a b", a=4)
                    dve.op(lambda: nc.vector.tensor_tensor(out=PT[d][:, :, par, :], in0=STv, in1=mask[:, 0:1, :].to_broadcast([128, 4, 128]), op=ALU.mult),
                           reads=[k.dps[bank], dc_], writes=[d_PT[d][par]])
                pool.op(lambda: nc.gpsimd.tensor_tensor(out=vp[d][:], in0=vtc[d][s2][:], in1=cs_bf[:, c, 8 * d:8 * d + 8].unsqueeze(2).to_broadcast([128, 8, 128]), op=ALU.mult),
                        reads=[d_ld[d][s2], d_cs], writes=[d_vp[d]])

            def stageC(d, c, s2):
                for h in range(8):
                    hp, par = h // 2, h % 2
                    rs_ = slice(64 * par, 64 * par + 64)
                    pth = PT[d][:, hp, par, :]
                    csc = cs_bf[:, c, 8 * d + h:8 * d + h + 1]
                    nb = h // 4
                    pe.op(lambda: nc.tensor.matmul(NUM[nb][:, h % 4, :], pth, vp[d][:, h, :], start=True, stop=False),
                          reads=[d_PT[d][par], d_vp[d]], writes=[d_NUM[nb], k.dbank[4 + nb]])
                    pe.op(lambda: nc.tensor.matmul(NUM[nb][:, h % 4, :], qTc[d][s2][rs_, hp, :], Cgb[d][rs_, hp, 0:128], start=False, stop=True),
                          reads=[d_Cgb[d], d_ld[d][s2]], writes=[d_NUM[nb], k.dbank[4 + nb]])
                    pe.op(lambda: nc.tensor.matmul(DEN[:, h:h + 1], pth, csc, start=True, stop=False),
                          reads=[d_PT[d][par], d_cs], writes=[d_DEN, k.dbank[6]])
                    pe.op(lambda: nc.tensor.matmul(DEN[:, h:h + 1], qTc[d][s2][rs_, hp, :], Cgb[d][rs_, hp, 128:129], start=False, stop=True),
                          reads=[d_Cgb[d], d_ld[d][s2]], writes=[d_DEN, k.dbank[6]])
                    pe.op(lambda: nc.tensor.matmul(UN[rs_, hp, :], ktc[d][s2][:, h, :], vp[d][:, h, :], start=True, stop=True),
                          reads=[d_vp[d], d_ld[d][s2]], writes=[d_UN, k.dbank[7]])
                    pe.op(lambda: nc.tensor.matmul(UNn[rs_, hp:hp + 1], ktc[d][s2][:, h, :], csc, start=True, stop=True),
                          reads=[d_cs, d_ld[d][s2]], writes=[d_DEN, k.dbank[6]])

            def stageD(d, c, s2):
                t0 = c * 128
                act.op(lambda: nc.scalar.activation(out=ad[d][:], in_=DEN, func=AF.Abs), reads=[d_DEN], writes=[d_ad[d], k.dbank[6]])
                dve.op(lambda: nc.vector.tensor_tensor(out=C[d][:, :, 128], in0=Cg[d][:, :, 128], in1=UNn, op=ALU.add),
                       reads=[d_Cg[d], d_DEN], writes=[d_C[d], k.dbank[6]])
                dve.op(lambda: nc.vector.tensor_tensor(out=C[d][:, :, 0:128], in0=Cg[d][:, :, 0:128], in1=UN, op=ALU.add),
                       reads=[d_Cg[d], d_UN], writes=[d_C[d], k.dbank[7]])
                dve.op(lambda: nc.vector.tensor_tensor(out=rr[d][:], in0=ad[d][:], in1=ecl[:, c, 8 * d:8 * d + 8], op=ALU.max),
                       reads=[d_ad[d], d_ecl], writes=[d_rr[d]])
                dve.op(lambda: nc.vector.reciprocal(out=rr[d][:], in_=rr[d][:]), reads=[d_rr[d]], writes=[d_rr[d]])
                hb_ = hbuf[d][s2]
                for g4 in range(2):
                    rb = rr[d][:, 4 * g4:4 * g4 + 4].unsqueeze(2).to_broadcast([128, 4, 128])
                    dve.op(lambda: nc.vector.tensor_tensor(out=hb_[:, 4 * g4:4 * g4 + 4, :], in0=NUM[g4], in1=rb, op=ALU.mult),
                           reads=[d_NUM[g4], d_rr[d]], writes=[d_hbuf[d][s2], k.dbank[4 + g4]])
                dst = k.hf_s if d == 0 else k.hb_s
                qs.dma(dst[t0:t0 + 128, :], hb_[:].rearrange("p h e -> p (h e)"), reads=[d_hbuf[d][s2]], writes=[k.d_hf[d][c]])

            for (chunks, init_col, pseq) in SEQS:
                orders = [chunks, chunks[::-1]]
                for d in range(2):
                    if pseq is None:
                        with nc.allow_non_contiguous_dma(reason="state load"):
                            cview = k.st_c[d].rearrange("(hp two) dd v -> two dd hp v", two=2)
                            nview = k.st_n[d].rearrange("(hp two) dd -> two dd hp", two=2)
                            for two in range(2):
                                qs.dma(C[d][64 * two:64 * two + 64, :, 0:128], cview[two], writes=[d_C[d]])
                                qs.dma(C[d][64 * two:64 * two + 64, :, 128], nview[two], writes=[d_C[d]])
                    else:
                        pool.op(lambda: nc.gpsimd.memset(C[d][:], 0.0), writes=[d_C[d]])
                n = len(chunks)
                slots = [[stageL(d, orders[d][0]) for d in range(2)]]
                for i in range(n):
                    if i + 1 < n:
                        slots.append([stageL(d, orders[d][i + 1]) for d in range(2)])
                    for d in range(2):
                        stageA(d, orders[d][i], slots[i][d])
                    for d in range(2):
                        stageB(d, orders[d][i], slots[i][d])
                    for d in range(2):
                        stageC(d, orders[d][i], slots[i][d])
                        stageD(d, orders[d][i], slots[i][d])
                if pseq is not None:
                    with nc.allow_non_contiguous_dma(reason="state store"):
                        for d in range(2):
                            cview = k.nsc[pseq, d].rearrange("(hp two) dd v -> two dd hp v", two=2)
                            nview = k.nsn[pseq, d].rearrange("(hp two) dd -> two dd hp", two=2)
                            for two in range(2):
                                qs.dma(cview[two], C[d][64 * two:64 * two + 64, :, 0:128], reads=[d_C[d]], writes=[k.d_out])
                                qs.dma(nview[two], C[d][64 * two:64 * two + 64, :, 128], reads=[d_C[d]], writes=[k.d_out])
        _barrier(k)

    with ExitStack() as st:
        sbl = lambda n, sh, dt: k.sb(n, sh, dt, st)
        hfc = [sbl("hfc%d" % i, [128, 8, 128], BF16) for i in range(2)]
        hbc = [sbl("hbc%d" % i, [128, 8, 128], BF16) for i in range(2)]
        hsum = [sbl("hsum%d" % i, [128, 8, 128], F32) for i in range(2)]
        oTc = [sbl("oTc%d" % i, [128, 8, 128], BF16) for i in range(2)]
        d_ld = [Dep(), Dep()]
        d_hs = [Dep(), Dep()]
        sqh = [sbl("sqh%d" % i, [128, 8, 128], F32) for i in range(2)]
        d_sqh = [Dep(), Dep()]
        ms = [sbl("ms%d" % i, [128, 8], F32) for i in range(2)]
        d_ms = [Dep(), Dep()]
        epsc = sbl("epsc2", [128, 1], F32)
        pool.op(lambda: nc.gpsimd.memset(epsc[:], EPS), writes=[d_ms[0], d_ms[1]])
        hn = [sbl("hn%d" % i, [128, 8, 128], BF16) for i in range(2)]
        d_hn = [Dep(), Dep()]
        hgT = [sbl("hgT%d" % i, [128, 8, TT], BF16) for i in range(2)]
        d_hgT = [Dep(), Dep()]
        xq = [sbl("xq%d" % i, [128, 2, TT], F32) for i in range(2)]
        dxq = [[Dep(), Dep()], [Dep(), Dep()]]
        nxq = [0]
        tpbs = [k.ps[1].bitcast(BF16).rearrange("p (a b) -> p a b", a=8), k.ps[2].bitcast(BF16).rearrange("p (a b) -> p a b", a=8)]
        nl = [0]

        d_lo = [Dep(), Dep()]

        def loadh(c):
            s2 = c % 2
            t0 = c * 128
            qs.dma(hfc[s2][:], k.hf_s[t0:t0 + 128, :].rearrange("t (h e) -> t h e", h=8), reads=[k.d_hf[0][c]], writes=[d_ld[s2]])
            qs.dma(hbc[s2][:], k.hb_s[t0:t0 + 128, :].rearrange("t (h e) -> t h e", h=8), reads=[k.d_hf[1][c]], writes=[d_ld[s2]])

        def loado(c):
            s2 = c % 2
            t0 = c * 128
            qs.dma(oTc[s2][:], k.oT_s[:, :, t0:t0 + 128].rearrange("c p t -> p c t"), reads=[k.d_o[c // 4]], writes=[d_lo[s2]])

        def prep_stages(c, s2, hg, dhg):
            b2 = c % 2
            hs_ = hsum[s2]
            tpb = tpbs[b2]
            cpos = c % 4

            def s1():
                pool.op(lambda: nc.gpsimd.tensor_tensor(out=hs_[:], in0=hfc[s2][:], in1=hbc[s2][:], op=ALU.add),
                        reads=[d_ld[s2]], writes=[d_hs[s2]])

            def s2_():
                act.op(lambda: nc.scalar.activation(out=sqh[b2][:], in_=hs_[:], func=AF.Square), reads=[d_hs[s2]], writes=[d_sqh[b2]])

            def s3():
                dve.op(lambda: nc.vector.reduce_sum(out=ms[b2][:], in_=sqh[b2][:], axis=AX.X), reads=[d_sqh[b2]], writes=[d_ms[b2]])

            def s4():
                act.op(lambda: nc.scalar.activation(out=ms[b2][:], in_=ms[b2][:], func=AF.Sqrt, bias=epsc[:, 0:1], scale=1.0 / 128),
                       reads=[d_ms[b2]], writes=[d_ms[b2]])

            def s5():
                dve.op(lambda: nc.vector.reciprocal(out=ms[b2][:], in_=ms[b2][:]), reads=[d_ms[b2]], writes=[d_ms[b2]])
                dve.op(lambda: nc.vector.tensor_tensor(out=hn[b2][:], in0=hs_[:], in1=ms[b2][:].unsqueeze(2).to_broadcast([128, 8, 128]), op=ALU.mult),
                       reads=[d_hs[s2], d_ms[b2]], writes=[d_hn[b2]])

            def s6():
                for fc in range(8):
                    pe.op(lambda: nc.tensor.transpose(tpb[:, fc, :], hn[b2][:, fc, :], k.ident_b[:]), reads=[d_hn[b2], dc_], writes=[k.dps[1 + b2]])

            def s7():
                dve.op(lambda: nc.vector.tensor_tensor(out=sqh[b2][:], in0=tpb, in1=k.ghT[:].unsqueeze(2).to_broadcast([128, 8, 128]), op=ALU.mult),
                       reads=[k.dps[1 + b2], dc_], writes=[d_sqh[b2]])

            def s8():
                pool.op(lambda: nc.gpsimd.tensor_tensor(out=hg[:, :, cpos * 128:(cpos + 1) * 128], in0=sqh[b2][:], in1=oTc[s2][:], op=ALU.mult),
                        reads=[d_sqh[b2], d_lo[s2]], writes=[dhg])

            return [s1, s2_, s3, s4, s5, s6, s7, s8]

        def outproj_part(tt, part):
            cond = 0 if tt < 8 else 1
            hg, dhg = hgT[tt % 2], d_hgT[tt % 2]
            qi = nxq[0] % 2
            nxq[0] += 1
            xt, dxt = xq[qi], dxq[qi]
            xv = k.xT_s[2 * part:2 * part + 2, :, tt * TT:(tt + 1) * TT].rearrange("c p t -> p c t")
            qs.dma(xt[:], xv, reads=[k.d_xs[tt]], writes=dxt)
            for d2 in range(2):
                dc = 2 * part + d2
                pb = 3 + dc % 2
                po = k.ps[pb]
                for fc in range(8):
                    pe.op(lambda: nc.tensor.matmul(po, wout[:, fc, dc * 128:(dc + 1) * 128], hg[:, fc, :],
                                                   start=(fc == 0), stop=(fc == 7)),
                          reads=[dW, dhg], writes=[k.dps[pb]], inc=(fc == 7))
                dve.op(lambda: nc.vector.scalar_tensor_tensor(out=xt[:, d2, :], in0=po, scalar=k.G[l][:, j, dc, cond:cond + 1],
                                                              in1=xt[:, d2, :], op0=ALU.mult, op1=ALU.add),
                       reads=[k.dps[pb], dxt[d2], dc_], writes=[dxt[d2]])
            qs.dma(xv, xt[:], reads=dxt, writes=[k.d_xs[tt]])

        for c in (0, 1):
            loadh(c)
            loado(c)
        for tt in range(NTILE):
            hg, dhg = hgT[tt % 2], d_hgT[tt % 2]
            for pr in range(2):
                c0 = 4 * tt + 2 * pr
                stA = prep_stages(c0, c0 % 2, hg, dhg)
                stB = prep_stages(c0 + 1, (c0 + 1) % 2, hg, dhg)
                for si_, (fa, fb) in enumerate(zip(stA, stB)):
                    fa()
                    fb()
                    if si_ == 0:
                        for cn in (c0 + 2, c0 + 3):
                            if cn < NCH:
                                loadh(cn)
                    if si_ == 7:
                        for cn in (c0 + 2, c0 + 3):
                            if cn < NCH:
                                loado(cn)
                if tt > 0:
                    outproj_part(tt - 1, 2 * pr)
                    outproj_part(tt - 1, 2 * pr + 1)
        for part in range(4):
            outproj_part(NTILE - 1, part)


def _attn_phase(k, s):
    nc = k.nc
    pe, act, dve, pool, qs, qg = k.pe, k.act, k.dve, k.pool, k.qs, k.qg
    W = k.W[s]
    win = W[:, 0:22528].rearrange("p (k n) -> p k n", k=8)
    wout = W[:, 22528:22528 + 8192].rearrange("p (k n) -> p k n", k=8)
    dW = k.dW[s]
    dc_ = k.d_const
    l, j = 1, 1
    with ExitStack() as st0:
        kctxT = k.sb("kctxT", [128, 2, 512], BF16, st0)
        vctx = k.sb("vctx", [128, 4, 256], BF16, st0)
        d_kctx, d_vctx = Dep(), Dep()
        with ExitStack() as st:
            B = _norm_bufs(k, st)
            xt1 = k.sb("xtP", [128, 8, TT], F32, st)
            dxt1 = [Dep() for _ in range(8)]
            hTs = [k.sb("hT0", [128, 8, TT], BF16, st), k.sb("hT1", [128, 8, TT], BF16, st)]
            dhTs = [[Dep() for _ in range(8)] for _ in range(2)]
            stage = [k.sb("stage%d" % i, [128, 4, TT], BF16, st) for i in range(2)]
            dstage = [Dep() for _ in range(2)]
            ckf = stage[0][:].bitcast(F32)
            d_ckf = dstage[0]
            cosT = k.sb("cosT", [128, TT], F32, st)
            sinT = k.sb("sinT", [128, TT], F32, st)
            d_rope = Dep()
            tA = k.sb("tA", [128, TT], F32, st)
            tB = k.sb("tB", [128, TT], F32, st)
            d_tA, d_tB = Dep(), Dep()
            stf = [tA[:, 0:256], tB[:, 0:256]]
            d_stf = [d_tA, d_tB]
            nst, npb, nev, nsf = [0], [0], [0], [0]

            def next_stage():
                i = nst[0] % 2
                nst[0] += 1
                return stage[i], dstage[i]

            def next_bank():
                i = 1 + npb[0] % 6
                npb[0] += 1
                return i

            def evac(out, src, pi, wdep):
                if nev[0] % 2 == 0:
                    act.op(lambda: nc.scalar.activation(out=out, in_=src, func=AF.Copy), reads=[k.dps[pi]], writes=[wdep])
                else:
                    dve.op(lambda: nc.vector.tensor_copy(out=out, in_=src), reads=[k.dps[pi]], writes=[wdep])
                nev[0] += 1

            qs.dma(ckf, k.ck.rearrange("(b s) f -> s b f", s=128), writes=[d_ckf])
            qg.dma(vctx[:], k.cv.rearrange("(b s) f -> s b f", s=128), writes=[d_vctx])
            for b in range(4):
                for gp in range(2):
                    pi = next_bank()
                    pe.op(lambda: nc.tensor.transpose(k.ps[pi][:, 0:128], ckf[:, b, gp * 128:(gp + 1) * 128], k.ident_f[:]),
                          reads=[d_ckf, dc_], writes=[k.dps[pi]])
                    evac(kctxT[:, gp, b * 128:(b + 1) * 128], k.ps[pi][:, 0:128], pi, d_kctx)

            def units_for(tt, hT, dhT):
                rope = tt < 8
                t0 = tt * TT
                units = []

                def proj(col, pi):
                    for kk in range(8):
                        pe.op(lambda: nc.tensor.matmul(k.ps[pi], win[:, kk, col:col + 128], hT[:, kk, :],
                                                       start=(kk == 0), stop=(kk == 7)),
                              reads=[dW, dhT[kk]], writes=[k.dps[pi]], inc=(kk == 7))

                def fmaj(colA, colB, nch, dst, ddst):
                    for g4 in range((nch + 3) // 4):
                        holder = {}
                        n4 = min(4, nch - 4 * g4)
                        for f4 in range(n4):
                            def uA(g4=g4, f4=f4, holder=holder, n4=n4):
                                if f4 == 0:
                                    holder["s"] = next_stage()
                                sg_, dsg_ = holder["s"]
                                fc = g4 * 4 + f4
                                pa = next_bank()
                                holder["pa"] = pa
                                proj(colA + fc * 128, pa)
                                if not rope:
                                    evac(sg_[:, f4, :], k.ps[pa], pa, dsg_)
                                    if f4 == n4 - 1:
                                        qs.dma(dst[g4 * 4:g4 * 4 + n4, :, t0:t0 + TT].rearrange("c p t -> p c t"), sg_[:, 0:n4, :], reads=[dsg_], writes=[ddst[tt]])
                            units.append(uA)
                            if rope:
                                def uB(g4=g4, f4=f4, holder=holder, n4=n4):
                                    sg_, dsg_ = holder["s"]
                                    fc = g4 * 4 + f4
                                    pa = holder["pa"]
                                    pb_ = next_bank()
                                    proj(colB + fc * 128, pb_)
                                    dve.op(lambda: nc.vector.tensor_tensor(out=tA[:], in0=k.ps[pa], in1=cosT[:], op=ALU.mult),
                                           reads=[k.dps[pa], d_rope], writes=[d_tA])
                                    dve.op(lambda: nc.vector.tensor_tensor(out=tB[:], in0=k.ps[pb_], in1=sinT[:], op=ALU.mult),
                                           reads=[k.dps[pb_], d_rope], writes=[d_tB])
                                    pool.op(lambda: nc.gpsimd.tensor_tensor(out=sg_[:, f4, :], in0=tA[:], in1=tB[:], op=ALU.add),
                                            reads=[d_tA, d_tB], writes=[dsg_])
                                    if f4 == n4 - 1:
                                        qs.dma(dst[g4 * 4:g4 * 4 + n4, :, t0:t0 + TT].rearrange("c p t -> p c t"), sg_[:, 0:n4, :], reads=[dsg_], writes=[ddst[tt]])
                                units.append(uB)

                fmaj(0, 1536, 8, k.qT_s, k.d_q)
                fmaj(1024, 2560, 2, k.kT_s, k.d_k)
                holder = {}
                for tb in range(4):
                    def uV(tb=tb, holder=holder):
                        if tb == 0:
                            holder["s"] = next_stage()
                        sg_, dsg_ = holder["s"]
                        pi = next_bank()
                        pv = k.ps[pi][:, 0:256]
                        for kk in range(8):
                            pe.op(lambda: nc.tensor.matmul(pv, hT[:, kk, tb * 128:(tb + 1) * 128], win[:, kk, 1280:1536],
                                                           start=(kk == 0), stop=(kk == 7)),
                                  reads=[dW, dhT[kk]], writes=[k.dps[pi]], inc=(kk == 7))
                        if rope:
                            dve.op(lambda: nc.vector.tensor_copy(out=sg_[:, tb, 0:256], in_=pv), reads=[k.dps[pi]], writes=[dsg_])
                        else:
                            seq = (tt - 8) * 2 + tb // 2
                            tl = (tb % 2) * 128
                            sf = nsf[0] % 2
                            nsf[0] += 1
                            act.op(lambda: nc.scalar.activation(out=stf[sf], in_=pv, func=AF.Copy), reads=[k.dps[pi]], writes=[d_stf[sf]])
                            dve.op(lambda: nc.vector.tensor_copy(out=sg_[:, tb, 0:256], in_=stf[sf]), reads=[d_stf[sf]], writes=[dsg_])
                            qs.dma(k.ncv[seq, tl:tl + 128, :], stf[sf], reads=[d_stf[sf]], writes=[k.d_out])
                        if tb == 3:
                            qs.dma(k.vt_s[t0:t0 + TT, 0:256].rearrange("(b t) f -> t b f", t=128), sg_[:, :, 0:256], reads=[dsg_], writes=[k.d_vt[tt]])
                    units.append(uV)
                    if not rope:
                        def uK(tb=tb):
                            seq = (tt - 8) * 2 + tb // 2
                            tl = (tb % 2) * 128
                            pi2 = next_bank()
                            pk = k.ps[pi2][:, 0:256]
                            for kk in range(8):
                                pe.op(lambda: nc.tensor.matmul(pk, hT[:, kk, tb * 128:(tb + 1) * 128], win[:, kk, 1024:1280],
                                                               start=(kk == 0), stop=(kk == 7)),
                                      reads=[dW, dhT[kk]], writes=[k.dps[pi2]], inc=(kk == 7))
                            sf = nsf[0] % 2
                            nsf[0] += 1
                            act.op(lambda: nc.scalar.activation(out=stf[sf], in_=pk, func=AF.Copy), reads=[k.dps[pi2]], writes=[d_stf[sf]])
                            qs.dma(k.nck[seq, tl:tl + 128, :], stf[sf], reads=[d_stf[sf]], writes=[k.d_out])
                        units.append(uK)
                return units

            qs.dma(xt1[:], _xs_tile(k, 0), reads=[k.d_xs[0]], writes=dxt1)
            _modnorm(k, B, xt1, dxt1, l, j, 0, hTs[0], dhTs[0])
            for tt in range(NTILE):
                if tt < 8:
                    t0 = tt * TT
                    qs.dma(cosT[:], k.c_cos[:, t0:t0 + TT], writes=[d_rope])
                    qs.dma(sinT[:], k.c_sin[:, t0:t0 + TT], writes=[d_rope])
                units = units_for(tt, hTs[tt % 2], dhTs[tt % 2])
                _emit_units_pipelined(k, B, units, tt, xt1, dxt1, l, j, hTs, dhTs)
        _barrier(k)
        if getattr(k, "dbg_stop", "") == "AP":
            return

        with ExitStack() as st:
            sbl = lambda n, sh, dt: k.sb(n, sh, dt, st)
            qT = [sbl("qT%d" % i, [128, 8, TT], BF16) for i in range(2)]
            kwin = [sbl("kwin%d" % i, [128, 2, 768], BF16) for i in range(2)]
            vwin = [sbl("vwin%d" % i, [128, 6, 256], BF16) for i in range(2)]
            d_ld = [Dep(), Dep()]
            xt = sbl("xtA", [128, 8, TT], F32)
            dxt = [Dep() for _ in range(8)]
            oT = sbl("oT", [128, 8, TT], BF16)
            d_oT = Dep()
            PT = [sbl("PT%d" % i, [128, 2, 4, 128], BF16) for i in range(3)]
            d_PT = [Dep() for _ in range(3)]
            dtmp = sbl("dtmp", [128, 4, 128], F32)
            d_dtmp = Dep()
            npt, nstb = [0], [0]

            def issue_loads(tt):
                s2 = tt % 2
                if tt < 8:
                    blo, bhi = max(4 * tt - 1, 0), min(4 * tt + 4, 31)
                else:
                    blo, bhi = 4 * tt, 4 * tt + 3
                nb = bhi - blo + 1
                tiles = sorted(set(b // 4 for b in range(blo, bhi + 1)))
                qs.dma(qT[s2][:], k.qT_s[:, :, tt * TT:(tt + 1) * TT].rearrange("c p t -> p c t"), reads=[k.d_q[tt]], writes=[d_ld[s2]])
                qs.dma(kwin[s2][:, :, 0:nb * 128], k.kT_s[0:2, :, blo * 128:(bhi + 1) * 128].rearrange("c p t -> p c t"),
                       reads=[k.d_k[t] for t in tiles], writes=[d_ld[s2]])
                qs.dma(vwin[s2][:, 0:nb, :], k.vt_s[blo * 128:(bhi + 1) * 128, 0:256].rearrange("(b t) f -> t b f", t=128),
                       reads=[k.d_vt[t] for t in tiles], writes=[d_ld[s2]])
                return blo

            blo_next = issue_loads(0)
            for tt in range(NTILE):
                s2 = tt % 2
                cond = 0 if tt < 8 else 1
                blo = blo_next
                if tt + 1 < NTILE:
                    blo_next = issue_loads(tt + 1)
                qs.dma(xt[:], _xs_tile(k, tt), reads=[k.d_xs[tt]], writes=dxt)
                steps = []
                for qb in range(4):
                    jb = 4 * tt + qb
                    keys = []
                    if tt < 8:
                        for kb, m in ((jb - 1, k.negl4), (jb, None), (jb + 1, k.negu4)):
                            if 0 <= kb <= 31:
                                keys.append(("lat", kb - blo, m))
                        for cb in range(4):
                            keys.append(("ctx", cb, None))
                    else:
                        base = jb - (jb % 2)
                        keys = [("lat", base - blo, None), ("lat", base + 1 - blo, None)]
                    for gp in range(2):
                        for ki, (kind, bi, m) in enumerate(keys):
                            steps.append((qb, gp, kind, bi, m, ki == 0, ki == len(keys) - 1))

                def emit_qk(stp):
                    qb, gp, kind, bi, m, first, last = stp
                    pr = nstb[0] % 2
                    nstb[0] += 1
                    pt = npt[0] % 3
                    npt[0] += 1
                    for ph in range(2):
                        rs_ = slice(64 * ph, 64 * ph + 64)
                        if kind == "lat":
                            kop, kd = kwin[s2][rs_, gp, bi * 128:(bi + 1) * 128], d_ld[s2]
                        else:
                            kop, kd = kctxT[rs_, gp, bi * 128:(bi + 1) * 128], d_kctx
                        pst = 2 * pr + ph
                        STv = k.ps[pst].rearrange("p (a b) -> p a b", a=4)
                        pe.op(lambda: nc.tensor.matmul(STv, kop, qT[s2][rs_, 4 * gp:4 * gp + 4, qb * 128:(qb + 1) * 128], start=True, stop=(m is None)),
                              reads=[kd, d_ld[s2]], writes=[k.dps[pst]])
                    if m is not None:
                        for ph in range(2):
                            pst = 2 * pr + ph
                            STv = k.ps[pst].rearrange("p (a b) -> p a b", a=4)
                            pe.op(lambda: nc.tensor.matmul(STv, k.ident_b[:], m[:, 0:1, :].to_broadcast([128, 4, 128]), start=False, stop=True),
                                  reads=[dc_], writes=[k.dps[pst]])
                    ST2 = k.psall[:, 2 * pr * 512:(2 * pr + 2) * 512]
                    act.op(lambda: nc.scalar.activation(out=PT[pt][:].rearrange("p a b c -> p (a b c)"), in_=ST2, func=AF.Exp, scale=0.125),
                           reads=[k.dps[2 * pr], k.dps[2 * pr + 1]], writes=[d_PT[pt]])
                    return pt

                def emit_pv(stp, pt):
                    qb, gp, kind, bi, m, first, last = stp
                    pn, pd = 4 + gp, 6 + gp
                    NUMv = k.ps[pn].rearrange("p (a b) -> p a b", a=4)
                    DENv = k.ps[pd].rearrange("p (a b) -> p a b", a=4)
                    for ph in range(2):
                        g = 2 * gp + ph
                        rs_ = slice(64 * ph, 64 * ph + 64)
                        if kind == "lat":
                            vop, vd = vwin[s2][:, bi, g * 64:(g + 1) * 64], d_ld[s2]
                        else:
                            vop, vd = vctx[:, bi, g * 64:(g + 1) * 64], d_vctx
                        pe.op(lambda: nc.tensor.matmul(NUMv[rs_, :, :], vop, PT[pt][:, ph], start=first, stop=last),
                              reads=[vd, d_PT[pt]], writes=[k.dps[pn]])
                    for ph in range(2):
                        rs_ = slice(64 * ph, 64 * ph + 64)
                        pe.op(lambda: nc.tensor.matmul(DENv[rs_, :, :], k.ones_b[:, 0:64], PT[pt][:, ph], start=first, stop=last),
                              reads=[dc_, d_PT[pt]], writes=[k.dps[pd]])
                    if last:
                        dve.op(lambda: nc.vector.tensor_tensor(out=dtmp[:], in0=DENv, in1=k.esT[:, 4 * gp:4 * gp + 4].unsqueeze(2).to_broadcast([128, 4, 128]), op=ALU.add),
                               reads=[k.dps[pd], dc_], writes=[d_dtmp])
                        dve.op(lambda: nc.vector.reciprocal(out=dtmp[:], in_=dtmp[:]), reads=[d_dtmp], writes=[d_dtmp])
                        dve.op(lambda: nc.vector.tensor_tensor(out=oT[:, 4 * gp:4 * gp + 4, qb * 128:(qb + 1) * 128], in0=NUMv, in1=dtmp[:], op=ALU.mult),
                               reads=[k.dps[pn], d_dtmp], writes=[d_oT])

                pend = emit_qk(steps[0])
                for si in range(len(steps)):
                    nxt_pts = emit_qk(steps[si + 1]) if si + 1 < len(steps) else None
                    emit_pv(steps[si], pend)
                    pend = nxt_pts
                for dc in range(8):
                    po = k.ps[0]
                    for fc in range(8):
                        pe.op(lambda: nc.tensor.matmul(po[:], wout[:, fc, dc * 128:(dc + 1) * 128], oT[:, fc, :], start=(fc == 0), stop=(fc == 7)),
                              reads=[dW, d_oT], writes=[k.dps[0]], inc=(fc == 7))
                    dve.op(lambda: nc.vector.scalar_tensor_tensor(out=xt[:, dc, :], in0=po[:], scalar=k.G[l][:, j, dc, cond:cond + 1],
                                                                  in1=xt[:, dc, :], op0=ALU.mult, op1=ALU.add),
                           reads=[k.dps[0], dxt[dc], dc_], writes=[dxt[dc]])
                qs.dma(_xs_tile(k, tt), xt[:], reads=dxt, writes=[k.d_xs[tt]])


_PROG = {}


def _consts():
    c = {}
    c["c_ident"] = np.eye(128, dtype=np.float32)
    s = np.arange(128)[:, None]
    t = np.arange(128)[None, :]
    c["c_masku"] = (t >= s).astype(np.float32)
    c["c_maskl"] = (t <= s).astype(np.float32)
    e = np.zeros((64, 16), np.float32)
    for h in range(8):
        e[h, h] = 1.0
        e[32 + h, 8 + h] = 1.0
    c["c_e16"] = e
    quarter = 16
    freqs = (np.float32(10000.0) ** (-np.arange(quarter, dtype=np.float32) / np.float32(quarter))).astype(np.float32)
    row = np.repeat(np.arange(64, dtype=np.float32), 64)
    col = np.tile(np.arange(64, dtype=np.float32), 64)
    ang_r = row[:, None] * freqs
    ang_c = col[:, None] * freqs
    ang = np.concatenate([ang_r, ang_r, ang_c, ang_c], axis=-1).astype(np.float32)
    cos = np.cos(ang).astype(np.float32).T
    sin = np.sin(ang).astype(np.float32).T
    sgn = np.where((np.arange(64) % 32) < 16, -1.0, 1.0).astype(np.float32)[:, None]
    c["c_cos"] = np.ascontiguousarray(np.concatenate([cos, cos], axis=0))
    c["c_sin"] = np.ascontiguousarray(np.concatenate([sin * sgn, sin * sgn], axis=0))
    return c


def _attn_perm():
    n = np.arange(1024)
    fc = n // 128
    ph = (n % 128) // 64
    d = n % 64
    h = 8 * (fc // 4) + 4 * ph + (fc % 4)
    sw = np.where((d % 32) < 16, d + 16, d - 16)
    cols_q = h * 64 + d
    cols_qs = h * 64 + sw
    nk = np.arange(256)
    dk = nk % 64
    swk = np.where((dk % 32) < 16, dk + 16, dk - 16)
    cols_k = 1024 + nk
    cols_ks = 1024 + (nk // 64) * 64 + swk
    return cols_q, cols_qs, cols_k, cols_ks, h


def _make_in_maps(inp):
    f = lambda a: np.ascontiguousarray(np.asarray(a, dtype=np.float32))
    cols_q, cols_qs, cols_k, cols_ks, hperm = _attn_perm()
    awi = np.asarray(inp["attn_w_in"][0])
    a_ext = f(np.concatenate([awi[:, cols_q], awi[:, cols_k], awi[:, 1280:1536], awi[:, cols_qs], awi[:, cols_ks]], axis=1))
    a_wout = f(np.asarray(inp["attn_w_out"][0])[cols_q, :])
    sink = np.asarray(inp["attn_sink"][0])
    hT = hperm.reshape(8, 2, 64)[:, :, 0]
    sinkT = np.empty((128, 8), np.float32)
    for fc in range(8):
        for ph in range(2):
            sinkT[ph * 64:(ph + 1) * 64, fc] = sink[hT[fc, ph]]
    shared = dict(
        w_mod=f(inp["w_mod"]), b_mod=f(inp["b_mod"]), norm_g=f(inp["norm_g"]),
        ffn1_w_gu=f(inp["ffn1_w_gu"]), ffn1_w_down=f(inp["ffn1_w_down"]),
        ffn2_w_gu=f(inp["ffn2_w_gu"]), ffn2_w_down=f(inp["ffn2_w_down"]),
        mlstm_w_in=f(inp["mlstm_w_in"][0]), mlstm_b_gate=f(inp["mlstm_b_gate"]), mlstm_g_head=f(inp["mlstm_g_head"][0]),
        mlstm_w_out=f(inp["mlstm_w_out"][0]), attn_w_in_ext=a_ext, attn_sinkT=sinkT, attn_w_out_p=a_wout,
        final_g=f(inp["final_g"]),
    )
    shared.update(_consts())
    maps = []
    for i in range(8):
        m = dict(shared)
        xs = np.asarray(inp["x_sample"][i])
        xp = np.asarray(inp["x_prompt"][4 * i:4 * i + 4]).reshape(1024, 1024)
        m["x_in"] = f(np.concatenate([xs, xp], axis=0))
        m["cvec"] = f(np.stack([np.asarray(inp["c"][i]), np.asarray(inp["c_ctx"])], axis=0))
        m["st_c"] = f(inp["state_c"][i, 0])
        m["st_n"] = f(inp["state_n"][i, 0])
        m["st_m"] = f(inp["state_m"][i, 0])
        m["ck"] = f(np.asarray(inp["cache_k"][i, 0]).reshape(512, 256))
        m["cv"] = f(np.asarray(inp["cache_v"][i, 0]).reshape(512, 256))
        maps.append(m)
    return maps


def kernel(**inputs):
    if "p" not in _PROG:
        _PROG["p"] = build_program()
    nc = _PROG["p"]
    maps = _make_in_maps(inputs)
    res = run_bass_kernel_spmd(nc, maps, core_ids=list(range(8)))
    R = res.results
    y = np.stack([r["y_out"] for r in R], axis=0)
    y_sample = np.ascontiguousarray(y[:, :4096, :])
    y_prompt = np.ascontiguousarray(y[:, 4096:, :].reshape(32, 256, 1024))
    nsc = np.concatenate([r["nsc"] for r in R], axis=0).reshape(32, 1, 2, 8, 64, 128)
    nsn = np.concatenate([r["nsn"] for r in R], axis=0).reshape(32, 1, 2, 8, 64)
    nsm = np.concatenate([r["nsm"] for r in R], axis=0).reshape(32, 1, 2, 8)
    nck = np.concatenate([r["nck"] for r in R], axis=0).reshape(32, 1, 256, 4, 64)
    ncv = np.concatenate([r["ncv"] for r in R], axis=0).reshape(32, 1, 256, 4, 64)
    return (y_prompt.astype(np.float32), y_sample.astype(np.float32), nsc.astype(np.float32), nsn.astype(np.float32),
            nsm.astype(np.float32), nck.astype(np.float32), ncv.astype(np.float32))
```

```python
import numpy as np
from contextlib import ExitStack
import concourse.bass as bass
import concourse.mybir as mybir
from concourse.bass_utils import run_bass_kernel_spmd

F32 = mybir.dt.float32
BF16 = mybir.dt.bfloat16
AF = mybir.ActivationFunctionType
ALU = mybir.AluOpType
AX = mybir.AxisListType

NTOK = 5120
TT = 512
NTILE = 10
NCH = 40
EPS = 1e-6
WSLOT = 33792
STRICT_SAME_ENGINE = False


class Dep:
    __slots__ = ("w", "rd")

    def __init__(self):
        self.w = None
        self.rd = {}


def _flat(xs):
    out = []
    for x in xs:
        if isinstance(x, (list, tuple)):
            out.extend(_flat(x))
        elif x is not None:
            out.append(x)
    return out


class Eng:
    def __init__(self, name, h, sem):
        self.name = name
        self.h = h
        self.sem = sem
        self.cnt = 0
        self.known = {}
        self.nwait = 0
        self.nins = 0

    def wait_tok(self, tok):
        sem, val, _ = tok
        k = id(sem)
        if self.known.get(k, 0) < val:
            self.h.wait_ge(sem, val)
            self.known[k] = val
            self.nwait += 1

    def _collect(self, reads, writes):
        toks = []
        for d in reads:
            if d.w is not None:
                toks.append(d.w)
        strict = STRICT_SAME_ENGINE and self.name != "pe"
        for d in writes:
            if d.w is not None and (strict or d.w[2] != self.name):
                toks.append(d.w)
            for en, t in d.rd.items():
                if strict or en != self.name:
                    toks.append(t)
        return toks

    def op(self, fn, reads=(), writes=(), inc=True):
        reads = _flat(reads)
        writes = _flat(writes)
        for t in self._collect(reads, writes):
            self.wait_tok(t)
        ins = fn()
        self.nins += 1
        if inc:
            self.cnt += 1
            ins.then_inc(self.sem, 1)
            tok = (self.sem, self.cnt, self.name)
        else:
            tok = (self.sem, self.cnt + 1, self.name)
        for d in reads:
            d.rd[self.name] = tok
        for d in writes:
            d.w = tok
            d.rd = {}
        return ins

    def last_tok(self):
        return (self.sem, self.cnt, self.name) if self.cnt else None


class Queue(Eng):
    def __init__(self, name, h, sems):
        super().__init__(name, h, None)
        self.sems = sems
        self.k = 0

    def dma(self, out, in_, reads=(), writes=(), **kw):
        reads = _flat(reads)
        writes = _flat(writes)
        for t in self._collect(reads, writes):
            self.wait_tok(t)
        ns = len(self.sems)
        slot = self.k % ns
        gen = self.k // ns
        sem = self.sems[slot]
        if gen > 0:
            self.wait_tok((sem, 16 * gen, self.name))
        ins = self.h.dma_start(out=out, in_=in_, **kw)
        ins.then_inc(sem, 16)
        self.nins += 1
        tok = (sem, 16 * (gen + 1), "%s#%d" % (self.name, slot))
        self.k += 1
        for d in reads:
            d.rd[tok[2]] = tok
        for d in writes:
            d.w = tok
            d.rd = {}
        return ins

    def all_toks(self):
        ns = len(self.sems)
        out = []
        for slot in range(min(ns, self.k)):
            n = (self.k - 1 - slot) // ns + 1
            out.append((self.sems[slot], 16 * n, "%s#%d" % (self.name, slot)))
        return out


class K:
    pass


def _barrier(k):
    toks = []
    for e in (k.pe, k.act, k.dve, k.pool):
        t = e.last_tok()
        if t:
            toks.append(t)
    toks += k.qs.all_toks() + k.qg.all_toks()
    for e in (k.pe, k.act, k.dve, k.pool, k.qs):
        for t in toks:
            if t[2] != e.name:
                e.wait_tok(t)


def build_program(upto=99):
    nc = bass.Bass("TRN2", target_bir_lowering=False)
    k = K()
    k.nc = nc
    k.upto = upto

    def din(name, shape):
        return nc.dram_tensor(name, list(shape), F32, kind="ExternalInput").ap()

    def dout(name, shape):
        return nc.dram_tensor(name, list(shape), F32, kind="ExternalOutput").ap()

    def dscr(name, shape, dt):
        return nc.dram_tensor(name, list(shape), dt, kind="Internal").ap()

    k.x_in = din("x_in", [NTOK, 1024])
    k.cvec = din("cvec", [2, 1024])
    k.st_c = din("st_c", [2, 8, 64, 128])
    k.st_n = din("st_n", [2, 8, 64])
    k.st_m = din("st_m", [2, 8])
    k.ck = din("ck", [512, 256])
    k.cv = din("cv", [512, 256])
    k.w_mod = din("w_mod", [2, 1024, 9216])
    k.b_mod = din("b_mod", [2, 9216])
    k.norm_g = din("norm_g", [2, 3, 1024])
    k.f_gu = [din("ffn1_w_gu", [2, 1024, 5632]), din("ffn2_w_gu", [2, 1024, 5632])]
    k.f_dn = [din("ffn1_w_down", [2, 2816, 1024]), din("ffn2_w_down", [2, 2816, 1024])]
    k.m_win = din("mlstm_w_in", [1024, 3104])
    k.m_bg = din("mlstm_b_gate", [1, 32])
    k.m_gh = din("mlstm_g_head", [1024])
    k.m_wout = din("mlstm_w_out", [1024, 1024])
    k.a_win = din("attn_w_in_ext", [1024, 2816])
    k.a_sink = din("attn_sinkT", [128, 8])
    k.a_wout = din("attn_w_out_p", [1024, 1024])
    k.final_g = din("final_g", [1024])
    k.c_ident = din("c_ident", [128, 128])
    k.c_masku = din("c_masku", [128, 128])
    k.c_maskl = din("c_maskl", [128, 128])
    k.c_e16 = din("c_e16", [64, 16])
    k.c_cos = din("c_cos", [128, 4096])
    k.c_sin = din("c_sin", [128, 4096])
    k.y_out = dout("y_out", [NTOK, 1024])
    k.nsc = dout("nsc", [4, 2, 8, 64, 128])
    k.nsn = dout("nsn", [4, 2, 8, 64])
    k.nsm = dout("nsm", [4, 2, 8])
    k.nck = dout("nck", [4, 256, 256])
    k.ncv = dout("ncv", [4, 256, 256])
    if upto < 99:
        k.xT_s = nc.dram_tensor("xT_s", [8, 128, NTOK], F32, kind="ExternalOutput").ap()
    else:
        k.xT_s = dscr("xT_s", [8, 128, NTOK], F32)
    k.hT_s = dscr("hT_s", [8, 128, NTOK], BF16)
    k.qT_s = dscr("qT_s", [8, 128, NTOK], BF16)
    k.kT_s = dscr("kT_s", [4, 128, NTOK], BF16)
    k.oT_s = dscr("oT_s", [8, 128, NTOK], BF16)
    k.kt_s = dscr("kt_s", [NTOK, 512], BF16)
    k.vt_s = dscr("vt_s", [NTOK, 1024], BF16)
    k.hf_s = dscr("hf_s", [NTOK, 1024], BF16)
    k.hb_s = dscr("hb_s", [NTOK, 1024], BF16)
    k.d_xs = [Dep() for _ in range(NTILE)]
    k.d_hs = [Dep() for _ in range(NTILE)]
    k.d_q = [Dep() for _ in range(NTILE)]
    k.d_k = [Dep() for _ in range(NTILE)]
    k.d_o = [Dep() for _ in range(NTILE)]
    k.d_kt = [Dep() for _ in range(NTILE)]
    k.d_vt = [Dep() for _ in range(NTILE)]
    k.d_hf = [[Dep() for _ in range(NCH)] for _ in range(2)]
    k.d_out = Dep()

    with ExitStack() as es:
        k.es = es

        def S(n):
            return es.enter_context(nc.semaphore(n))

        k.pe = Eng("pe", nc.tensor, S("s_pe"))
        k.act = Eng("act", nc.scalar, S("s_act"))
        k.dve = Eng("dve", nc.vector, S("s_dve"))
        k.pool = Eng("pool", nc.gpsimd, S("s_pool"))
        k.qs = Queue("qs", nc.sync, [S("s_qs%d" % i) for i in range(8)])
        k.qg = Queue("qg", nc.gpsimd, [S("s_qg%d" % i) for i in range(6)])
        k.qg.known = k.pool.known

        k.uid = 0

        def sb(name, shape, dt, st=es):
            k.uid += 1
            return st.enter_context(nc.sbuf_tensor("%s_%d" % (name, k.uid), list(shape), dt))

        k.sb = sb
        k.W = [sb("W0", [128, WSLOT], BF16), sb("W1", [128, WSLOT], BF16)]
        k.dW = [Dep(), Dep()]
        k.ident_f = sb("ident_f", [128, 128], F32)
        k.ident_b = sb("ident_b", [128, 128], BF16)
        k.ones_b = sb("ones_b", [128, 128], BF16)
        k.ones_f = sb("ones_f", [128, 128], F32)
        k.masku_f = sb("masku_f", [128, 128], F32)
        k.maskl_f = sb("maskl_f", [128, 128], F32)
        k.masku4 = sb("masku4", [128, 1, 128], BF16)
        k.maskl4 = sb("maskl4", [128, 1, 128], BF16)
        k.negu4 = sb("negu4", [128, 1, 128], BF16)
        k.negl4 = sb("negl4", [128, 1, 128], BF16)
        k.e16 = sb("e16", [64, 16], F32)
        k.modT = [sb("modT0", [128, 72, 2], F32), sb("modT1", [128, 72, 2], F32)]
        k.A = [sb("A0", [128, 3, 8, 2], F32), sb("A1", [128, 3, 8, 2], F32)]
        k.G = [sb("G0", [128, 3, 8, 2], F32), sb("G1", [128, 3, 8, 2], F32)]
        k.fgT = sb("fgT", [128, 8], F32)
        k.ghT = sb("ghT", [128, 8], F32)
        k.esT = sb("esT", [128, 8], F32)
        k.bgate = sb("bgate", [128, 32], F32)
        k.d_const = Dep()
        k.psall = es.enter_context(nc.psum_tensor("psall", [128, 4096], F32))
        k.ps = [k.psall[:, i * 512:(i + 1) * 512] for i in range(8)]
        k.dps = [Dep() for _ in range(8)]
        k.dbank = [Dep() for _ in range(8)]

        phases = _phase_list()
        _phase0(k)
        for ph in phases:
            ph(k)
        _barrier(k)
    return nc


def _wslot_views_ffn(k, s):
    W = k.W[s]
    wgu = W[:, 0:22528].rearrange("p (k n) -> p k n", k=8)
    wdn = W[:, 22528:33792].rearrange("p (k n) -> p k n", k=11)
    return wgu, wdn


def _load_ffn_weights(k, s, l, which, half):
    wgu, wdn = _wslot_views_ffn(k, s)
    gu = k.f_gu[which][l].rearrange("(k p) n -> p k n", p=128)
    dn = k.f_dn[which][l]
    c0 = half * 1408
    for kk in range(0, 8, 2):
        k.qg.dma(wgu[:, kk:kk + 2, 0:1408], gu[:, kk:kk + 2, c0:c0 + 1408], writes=[k.dW[s]])
        k.qg.dma(wgu[:, kk:kk + 2, 1408:2816], gu[:, kk:kk + 2, 2816 + c0:2816 + c0 + 1408], writes=[k.dW[s]])
    dnv = dn[c0:c0 + 1408, :].rearrange("(k p) n -> p k n", p=128)
    k.qg.dma(wdn[:, 0:6, :], dnv[:, 0:6, :], writes=[k.dW[s]])
    k.qg.dma(wdn[:, 6:11, :], dnv[:, 6:11, :], writes=[k.dW[s]])


def _load_mlstm_weights(k, s):
    W = k.W[s]
    win = W[:, 0:24832].rearrange("p (k n) -> p k n", k=8)
    wout = W[:, 24832:24832 + 8192].rearrange("p (k n) -> p k n", k=8)
    src = k.m_win.rearrange("(k p) n -> p k n", p=128)
    for kk in range(0, 8, 2):
        k.qg.dma(win[:, kk:kk + 2, :], src[:, kk:kk + 2, :], writes=[k.dW[s]])
    k.qg.dma(wout, k.m_wout.rearrange("(k p) n -> p k n", p=128), writes=[k.dW[s]])
    return win, wout


def _load_attn_weights(k, s):
    W = k.W[s]
    win = W[:, 0:22528].rearrange("p (k n) -> p k n", k=8)
    wout = W[:, 22528:22528 + 8192].rearrange("p (k n) -> p k n", k=8)
    src = k.a_win.rearrange("(k p) n -> p k n", p=128)
    for kk in range(0, 8, 2):
        k.qg.dma(win[:, kk:kk + 2, :], src[:, kk:kk + 2, :], writes=[k.dW[s]])
    k.qg.dma(wout, k.a_wout.rearrange("(k p) n -> p k n", p=128), writes=[k.dW[s]])
    return win, wout


def _phase_list():
    specs = []
    for l in range(2):
        specs.append(("ffn", l, 0, 0))
        specs.append(("ffn", l, 0, 1))
        specs.append(("mix", l))
        specs.append(("ffn", l, 1, 0))
        specs.append(("ffn", l, 1, 1))
    n = len(specs)

    def loader(i):
        sp = specs[i]
        s = i % 2
        if sp[0] == "ffn":
            return lambda k: _load_ffn_weights(k, s, sp[1], sp[2], sp[3])
        if sp[1] == 0:
            return lambda k: _load_mlstm_weights(k, s)
        return lambda k: _load_attn_weights(k, s)

    groups = []
    i = 0
    while i < n:
        if specs[i][0] == "ffn":
            g = [i]
            while i + 1 < n and specs[i + 1][0] == "ffn":
                i += 1
                g.append(i)
            groups.append(g)
        else:
            groups.append([i])
        i += 1

    phases = []
    for g in groups:
        def run(k, g=g):
            g2 = [i for i in g if i < k.upto]
            if not g2:
                return
            if specs[g2[0]][0] == "ffn":
                segs = []
                for i in g2:
                    nxt = loader(i + 1) if (i + 1 < n and i + 1 < k.upto) else None
                    segs.append((i % 2, specs[i][1], specs[i][2], specs[i][3], i == n - 1, nxt))
                _ffn_run(k, segs)
            else:
                i = g2[0]
                if i + 1 < n and i + 1 < k.upto:
                    loader(i + 1)(k)
                if specs[i][1] == 0:
                    _mlstm_phase(k, i % 2)
                else:
                    _attn_phase(k, i % 2)
            _barrier(k)
        phases.append(run)
    return phases


def _xs_tile(k, tt):
    return k.xT_s[:, :, tt * TT:(tt + 1) * TT].rearrange("c p t -> p c t")


def _phase0(k):
    nc = k.nc
    pe, act, dve, pool, qs, qg = k.pe, k.act, k.dve, k.pool, k.qs, k.qg
    dc = k.d_const
    with ExitStack() as st:
        sb = lambda n, s, d: k.sb(n, s, d, st)
        with nc.allow_non_contiguous_dma(reason="small strided constant loads"):
            qs.dma(k.ident_f[:], k.c_ident, writes=[dc])
            qs.dma(k.masku_f[:], k.c_masku, writes=[dc])
            qs.dma(k.maskl_f[:], k.c_maskl, writes=[dc])
            qs.dma(k.e16[:], k.c_e16, writes=[dc])
            qs.dma(k.esT[:], k.a_sink, writes=[dc])
            qs.dma(k.fgT[:], k.final_g.rearrange("(c p) -> p c", p=128), writes=[dc])
            qs.dma(k.ghT[:], k.m_gh.rearrange("(c p) -> p c", p=128), writes=[dc])
            qs.dma(k.bgate[:], k.m_bg.partition_broadcast(128), writes=[dc])
            ngT = sb("ngT", [128, 2, 3, 8], F32)
            for l in range(2):
                for j in range(3):
                    qs.dma(ngT[:, l, j, :], k.norm_g[l, j].rearrange("(c p) -> p c", p=128), writes=[dc])
            sT = sb("sT", [128, 8, 2], F32)
            for c in range(2):
                qs.dma(sT[:, :, c], k.cvec[c].rearrange("(k p) -> p k", p=128), writes=[dc])
            bmT = sb("bmT", [128, 2, 72], F32)
            for l in range(2):
                qs.dma(bmT[:, l, :], k.b_mod[l].rearrange("(j p) -> p j", p=128), writes=[dc])
        pool.op(lambda: nc.gpsimd.memset(k.ones_b[:], 1.0), writes=[dc])
        pool.op(lambda: nc.gpsimd.memset(k.ones_f[:], 1.0), writes=[dc])
        act.op(lambda: nc.scalar.copy(out=k.ident_b[:], in_=k.ident_f[:]), reads=[dc], writes=[dc])
        for i in range(1):
            act.op(lambda: nc.scalar.copy(out=k.masku4[:, i, :], in_=k.masku_f[:]), reads=[dc], writes=[dc])
            act.op(lambda: nc.scalar.copy(out=k.maskl4[:, i, :], in_=k.maskl_f[:]), reads=[dc], writes=[dc])
            dve.op(lambda: nc.vector.tensor_scalar(out=k.negu4[:, i, :], in0=k.masku_f[:], scalar1=-1.0, scalar2=30000.0, op0=ALU.add, op1=ALU.mult),
                   reads=[dc], writes=[dc])
            dve.op(lambda: nc.vector.tensor_scalar(out=k.negl4[:, i, :], in0=k.maskl_f[:], scalar1=-1.0, scalar2=30000.0, op0=ALU.add, op1=ALU.mult),
                   reads=[dc], writes=[dc])
        act.op(lambda: nc.scalar.activation(out=k.esT[:], in_=k.esT[:], func=AF.Exp), reads=[dc], writes=[dc])
        sTb = sb("sTb", [128, 8, 2], BF16)
        act.op(lambda: nc.scalar.activation(out=sTb[:], in_=sT[:], func=AF.Silu), reads=[dc], writes=[dc])

        stg = [sb("stg0", [128, 1024], F32), sb("stg1", [128, 1024], F32)]
        dstg = [Dep(), Dep()]
        xt = [sb("p0xt0", [128, 8, 512], F32), sb("p0xt1", [128, 8, 512], F32)]
        dxt = [Dep(), Dep()]
        n = 0
        for tt in range(NTILE):
            xs = tt % 2
            for tb in range(4):
                s = n % 2
                n += 1
                t0 = tt * TT + tb * 128
                qs.dma(stg[s][:], k.x_in[t0:t0 + 128, :], writes=[dstg[s]])
                for hb in range(2):
                    pi = 4 + 2 * (tb % 2) + hb
                    pv = k.ps[pi][:].rearrange("p (a b) -> p a b", a=4)
                    for c4 in range(4):
                        c = hb * 4 + c4
                        pe.op(lambda: nc.tensor.transpose(pv[:, c4, :], stg[s][:, c * 128:(c + 1) * 128], k.ident_f[:]),
                              reads=[dstg[s], dc], writes=[k.dps[pi]])
                    eng = act if hb == 0 else dve
                    if hb == 0:
                        act.op(lambda: nc.scalar.copy(out=xt[xs][:, 0:4, tb * 128:(tb + 1) * 128], in_=pv),
                               reads=[k.dps[pi]], writes=[dxt[xs]])
                    else:
                        dve.op(lambda: nc.vector.tensor_copy(out=xt[xs][:, 4:8, tb * 128:(tb + 1) * 128], in_=pv),
                               reads=[k.dps[pi]], writes=[dxt[xs]])
            qs.dma(_xs_tile(k, tt), xt[xs][:], reads=[dxt[xs]], writes=[k.d_xs[tt]])
        modtok = sb("modtok", [2, 4608], F32)
        d_mt = Dep()
        NB = 1152
        wv = [k.W[1][:, i * 9216:(i + 1) * 9216].rearrange("p (k n) -> p k n", k=8) for i in range(3)]
        dwv = [Dep() for _ in range(3)]
        tmpA = sb("tmpA", [128, 8, 2], F32)
        d_tmpA = Dep()
        bi = 0
        for l in range(2):
            src = k.w_mod[l].rearrange("(k p) n -> p k n", p=128)
            for hh in range(2):
                for b4 in range(4):
                    b = hh * 4 + b4
                    s = bi % 3
                    bi += 1
                    qg.dma(wv[s], src[:, :, b * NB:(b + 1) * NB], writes=[dwv[s]])
                    if bi == 3:
                        _load_ffn_weights(k, 0, 0, 0, 0)
                    for cg in range(3):
                        pi = 1 + (cg % 2)
                        for kk in range(8):
                            pe.op(lambda: nc.tensor.matmul(k.ps[pi][0:2, 0:384], sTb[:, kk, :], wv[s][:, kk, cg * 384:(cg + 1) * 384],
                                                           start=(kk == 0), stop=(kk == 7)),
                                  reads=[dc, dwv[s]], writes=[k.dps[pi]], inc=(kk == 7))
                        c0 = b4 * NB + cg * 384
                        dve.op(lambda: nc.vector.tensor_copy(out=modtok[0:2, c0:c0 + 384], in_=k.ps[pi][0:2, 0:384]),
                               reads=[k.dps[pi]], writes=[d_mt])
                pT = k.ps[3][:, 0:72].rearrange("p (j c) -> p j c", c=2)
                for j in range(36):
                    pe.op(lambda: nc.tensor.matmul(pT[:, j, :], modtok[0:2, j * 128:(j + 1) * 128], k.ident_f[0:2, 0:2],
                                                   start=True, stop=True),
                          reads=[d_mt, dc], writes=[k.dps[3]])
                dve.op(lambda: nc.vector.tensor_tensor(out=k.modT[l][:, hh * 36:(hh + 1) * 36, :], in0=pT,
                                                       in1=bmT[:, l, hh * 36:(hh + 1) * 36].unsqueeze(2).to_broadcast([128, 36, 2]),
                                                       op=ALU.add),
                       reads=[k.dps[3], dc], writes=[dc])
            for j in range(3):
                dve.op(lambda: nc.vector.tensor_scalar(out=tmpA[:], in0=k.modT[l][:, (3 * j + 1) * 8:(3 * j + 2) * 8, :],
                                                       scalar1=1.0, scalar2=None, op0=ALU.add),
                       reads=[dc], writes=[d_tmpA])
                dve.op(lambda: nc.vector.tensor_tensor(out=k.A[l][:, j], in0=tmpA[:],
                                                       in1=ngT[:, l, j, :].unsqueeze(2).to_broadcast([128, 8, 2]), op=ALU.mult),
                       reads=[d_tmpA, dc], writes=[dc])
                dve.op(lambda: nc.vector.tensor_scalar(out=k.G[l][:, j], in0=k.modT[l][:, (3 * j + 2) * 8:(3 * j + 3) * 8, :],
                                                       scalar1=(1.0 if j == 1 else 0.5), scalar2=None, op0=ALU.mult),
                       reads=[dc], writes=[dc])

        _barrier(k)


def _norm_stat(k, B, xt, dxt, c):
    nc = k.nc
    s = c % 2
    k.act.op(lambda: nc.scalar.activation(out=B.sqc[s][:], in_=xt[:, c, :], func=AF.Square),
             reads=[dxt[c]], writes=[B.dsq[s]])
    k.pe.op(lambda: nc.tensor.matmul(k.ps[0][:], k.ones_b[:], B.sqc[s][:], start=(c == 0), stop=(c == 7)),
            reads=[B.dsq[s], k.d_const], writes=[k.dps[0]])


def _norm_rstd(k, B):
    nc = k.nc
    k.act.op(lambda: nc.scalar.activation(out=B.rstd[:], in_=k.ps[0][:], func=AF.Sqrt, bias=B.epsc[:, 0:1], scale=1.0 / 1024),
             reads=[k.dps[0]], writes=[B.drstd])
    k.dve.op(lambda: nc.vector.reciprocal(out=B.rstd[:], in_=B.rstd[:]), reads=[B.drstd], writes=[B.drstd])


def _norm_mod(k, B, xt, dxt, l, j, cond, hT, dhT, c):
    nc = k.nc
    s = c % 2
    k.dve.op(lambda: nc.vector.scalar_tensor_tensor(out=B.tmp[s][:], in0=xt[:, c, :],
                                                    scalar=k.A[l][:, j, c, cond:cond + 1], in1=B.rstd[:],
                                                    op0=ALU.mult, op1=ALU.mult),
             reads=[dxt[c], B.drstd, k.d_const], writes=[B.dtmp[s]])
    k.act.op(lambda: nc.scalar.activation(out=hT[:, c, :], in_=B.tmp[s][:], func=AF.Identity,
                                          bias=k.modT[l][:, 3 * j * 8 + c, cond:cond + 1], scale=1.0),
             reads=[B.dtmp[s], k.d_const], writes=[dhT[c]])


def _modnorm(k, B, xt, dxt, l, j, cond, hT, dhT):
    for c in range(8):
        _norm_stat(k, B, xt, dxt, c)
    _norm_rstd(k, B)
    if hT is None:
        return
    for c in range(8):
        _norm_mod(k, B, xt, dxt, l, j, cond, hT, dhT, c)


def _emit_units_pipelined(k, B, units, tt, xt1, dxt1, l, j, hTs, dhTs):
    n = len(units)
    pipe = tt + 1 < NTILE
    if pipe:
        k.qs.dma(xt1[:], _xs_tile(k, tt + 1), reads=[k.d_xs[tt + 1]], writes=dxt1)
    s0 = max(0, n - 20)
    ncond = 0 if tt + 1 < 8 else 1
    for gi, u in enumerate(units):
        u()
        if pipe:
            if s0 <= gi < s0 + 8:
                _norm_stat(k, B, xt1, dxt1, gi - s0)
            if gi == s0 + 7:
                _norm_rstd(k, B)
            if s0 + 8 <= gi < s0 + 16:
                _norm_mod(k, B, xt1, dxt1, l, j, ncond, hTs[(tt + 1) % 2], dhTs[(tt + 1) % 2], gi - s0 - 8)
    assert n >= s0 + 16


def _norm_bufs(k, st):
    B = K()
    B.sqc = [k.sb("sqc0", [128, TT], BF16, st), k.sb("sqc1", [128, TT], BF16, st)]
    B.dsq = [Dep(), Dep()]
    B.rstd = k.sb("rstd", [128, TT], F32, st)
    B.drstd = Dep()
    B.tmp = [k.sb("ntmp0", [128, TT], F32, st), k.sb("ntmp1", [128, TT], F32, st)]
    B.dtmp = [Dep(), Dep()]
    B.epsc = k.sb("epsc", [128, 1], F32, st)
    k.pool.op(lambda: k.nc.gpsimd.memset(B.epsc[:], EPS), writes=[B.drstd])
    return B


def _ffn_run(k, segs):
    nc = k.nc
    pe, act, dve, pool, qs = k.pe, k.act, k.dve, k.pool, k.qs
    tiles = []
    for si, sg_ in enumerate(segs):
        for tt in range(NTILE):
            tiles.append((si, tt))
    with ExitStack() as st:
        B = _norm_bufs(k, st)
        xt = [k.sb("xt0", [128, 8, TT], F32, st), k.sb("xt1", [128, 8, TT], F32, st)]
        dxt = [[Dep() for _ in range(8)] for _ in range(2)]
        hTs = [k.sb("hT0", [128, 8, TT], BF16, st), k.sb("hT1", [128, 8, TT], BF16, st)]
        dhTs = [[Dep() for _ in range(8)] for _ in range(2)]
        actT = k.sb("actT", [128, 11, TT], BF16, st)
        dact = [Dep() for _ in range(11)]
        sg = k.sb("sg", [128, TT], F32, st)
        dsg = Dep()
        dyv = [Dep(), Dep()]

        def hview(tt):
            return k.hT_s[:, :, tt * TT:(tt + 1) * TT].rearrange("c p t -> p c t")

        def cond_of(tt):
            return 0 if tt < 8 else 1

        def par(g):
            si, tt = tiles[g]
            slot, l, which, half, final, loader = segs[si]
            return slot, l, (0 if which == 0 else 2), half, final, tt

        slot, l, j, half, final, tt = par(0)
        qs.dma(xt[0][:], _xs_tile(k, tt), reads=[k.d_xs[tt]], writes=dxt[0])
        if half == 0:
            _modnorm(k, B, xt[0], dxt[0], l, j, cond_of(tt), hTs[0], dhTs[0])
            qs.dma(hview(tt), hTs[0][:], reads=dhTs[0], writes=[k.d_hs[tt]])
        else:
            qs.dma(hTs[0][:], hview(tt), reads=[k.d_hs[tt]], writes=dhTs[0])
        fin_ops = []
        for g in range(len(tiles)):
            slot, l, j, half, final, tt = par(g)
            pend_fin = fin_ops
            fin_ops = []
            si = tiles[g][0]
            if tt == 0 and segs[si][5] is not None:
                segs[si][5](k)
            wgu, wdn = _wslot_views_ffn(k, slot)
            dW = k.dW[slot]
            xs = g % 2
            cond = cond_of(tt)
            X, dX = xt[xs], dxt[xs]
            hT, dhT = hTs[xs], dhTs[xs]
            nxt = g + 1 < len(tiles)
            pipe = False
            defer_x = bool(pend_fin)
            if nxt:
                nslot, nl, nj, nhalf, nfinal, ntt = par(g + 1)
                if not defer_x:
                    qs.dma(xt[1 - xs][:], _xs_tile(k, ntt), reads=[k.d_xs[ntt]], writes=dxt[1 - xs])
                if nhalf == 1:
                    qs.dma(hTs[1 - xs][:], hview(ntt), reads=[k.d_hs[ntt]], writes=dhTs[1 - xs])
                pipe = nhalf == 0
            for fc in range(11):
                pg = 1 + 2 * (fc % 2)
                pu = pg + 1
                for kk in range(8):
                    pe.op(lambda: nc.tensor.matmul(k.ps[pg], wgu[:, kk, fc * 128:(fc + 1) * 128], hT[:, kk, :],
                                                   start=(kk == 0), stop=(kk == 7)),
                          reads=[dW, dhT[kk]], writes=[k.dps[pg]], inc=(kk == 7))
                for kk in range(8):
                    pe.op(lambda: nc.tensor.matmul(k.ps[pu], wgu[:, kk, 1408 + fc * 128:1408 + (fc + 1) * 128], hT[:, kk, :],
                                                   start=(kk == 0), stop=(kk == 7)),
                          reads=[dW, dhT[kk]], writes=[k.dps[pu]], inc=(kk == 7))
                act.op(lambda: nc.scalar.activation(out=sg[:], in_=k.ps[pg], func=AF.Silu),
                       reads=[k.dps[pg]], writes=[dsg])
                dve.op(lambda: nc.vector.tensor_tensor(out=actT[:, fc, :], in0=k.ps[pu], in1=sg[:], op=ALU.mult),
                       reads=[k.dps[pu], dsg], writes=[dact[fc]])
                if pipe and fc >= 7:
                    for c in (2 * (fc - 7), 2 * (fc - 7) + 1):
                        _norm_stat(k, B, xt[1 - xs], dxt[1 - xs], c)
                if pend_fin:
                    pend_fin.pop(0)()
            if pipe:
                _norm_rstd(k, B)
            for dc in range(8):
                po = 5 + (dc % 2)
                for fc in range(11):
                    pe.op(lambda: nc.tensor.matmul(k.ps[po], wdn[:, fc, dc * 128:(dc + 1) * 128], actT[:, fc, :],
                                                   start=(fc == 0), stop=(fc == 10)),
                          reads=[dW, dact[fc]], writes=[k.dps[po]], inc=(fc == 10))
                dve.op(lambda: nc.vector.scalar_tensor_tensor(out=X[:, dc, :], in0=k.ps[po],
                                                              scalar=k.G[l][:, j, dc, cond:cond + 1], in1=X[:, dc, :],
                                                              op0=ALU.mult, op1=ALU.add),
                       reads=[k.dps[po], dX[dc], k.d_const], writes=[dX[dc]])
                if pipe:
                    _norm_mod(k, B, xt[1 - xs], dxt[1 - xs], nl, nj, cond_of(ntt), hTs[1 - xs], dhTs[1 - xs], dc)
                if pend_fin:
                    pend_fin.pop(0)()
            if pipe:
                qs.dma(hview(ntt), hTs[1 - xs][:], reads=dhTs[1 - xs], writes=[k.d_hs[ntt]])
            while pend_fin:
                pend_fin.pop(0)()
            if nxt and defer_x:
                qs.dma(xt[1 - xs][:], _xs_tile(k, ntt), reads=[k.d_xs[ntt]], writes=dxt[1 - xs])
            if not final:
                qs.dma(_xs_tile(k, tt), X[:], reads=dX, writes=[k.d_xs[tt]])
            else:
                fin_ops = _final_ops(k, B, X, dX, slot, tt, dyv)
        for f in fin_ops:
            f()


def _final_ops(k, B, X, dX, slot, tt, dyv):
    nc = k.nc
    pe, act, dve, qs = k.pe, k.act, k.dve, k.qs
    yv = k.W[1 - slot][:, 0:4096].bitcast(F32).rearrange("p (a n) -> p a n", a=2)
    ops = []
    for c2 in range(4):
        def st_(c2=c2):
            _norm_stat(k, B, X, dX, 2 * c2)
            _norm_stat(k, B, X, dX, 2 * c2 + 1)
        ops.append(st_)
    ops.append(lambda: _norm_rstd(k, B))
    for c2 in range(4):
        def sc_(c2=c2):
            for c in (2 * c2, 2 * c2 + 1):
                dve.op(lambda: nc.vector.scalar_tensor_tensor(out=X[:, c, :], in0=X[:, c, :], scalar=k.fgT[:, c:c + 1],
                                                              in1=B.rstd[:], op0=ALU.mult, op1=ALU.mult),
                       reads=[dX[c], B.drstd, k.d_const], writes=[dX[c]])
        ops.append(sc_)
    for tb in range(4):
        for hb in range(2):
            def tr_(tb=tb, hb=hb):
                ys = tb % 2
                pi = 7 if hb == 0 else 0
                pv = k.ps[pi].rearrange("p (a b) -> p a b", a=4)
                for c4 in range(4):
                    c = hb * 4 + c4
                    pe.op(lambda: nc.tensor.transpose(pv[:, c4, :], X[:, c, tb * 128:(tb + 1) * 128], k.ident_f[:]),
                          reads=[dX[c], k.d_const], writes=[k.dps[pi]])
                if hb == 0:
                    act.op(lambda: nc.scalar.copy(out=yv[:, ys, 0:512], in_=k.ps[pi]), reads=[k.dps[pi]], writes=[dyv[ys], k.dW[1 - slot]])
                else:
                    dve.op(lambda: nc.vector.tensor_copy(out=yv[:, ys, 512:1024], in_=k.ps[pi]), reads=[k.dps[pi]], writes=[dyv[ys], k.dW[1 - slot]])
                    t0 = tt * TT + tb * 128
                    qs.dma(k.y_out[t0:t0 + 128, :], yv[:, ys, :], reads=[dyv[ys]], writes=[k.d_out])
            ops.append(tr_)
    return ops


def _mlstm_phase(k, s):
    nc = k.nc
    pe, act, dve, pool, qs = k.pe, k.act, k.dve, k.pool, k.qs
    W = k.W[s]
    win = W[:, 0:24832].rearrange("p (k n) -> p k n", k=8)
    wout = W[:, 24832:24832 + 8192].rearrange("p (k n) -> p k n", k=8)
    dW = k.dW[s]
    dc_ = k.d_const
    l, j = 0, 1
    with ExitStack() as st0:
        gates = k.sb("gates", [128, NCH, 32], F32, st0)
        d_gates = Dep()
        cs = k.sb("cs", [128, NCH, 16], F32, st0)
        ecl = k.sb("ecl", [128, NCH, 16], F32, st0)
        gsel = k.sb("gsel", [128, NCH, 2, 4], F32, st0)
        cs_bf = k.sb("cs_bf", [128, NCH, 16], BF16, st0)
        d_cs, d_ecl, d_gsel = Dep(), Dep(), Dep()

        with ExitStack() as st:
            B = _norm_bufs(k, st)
            xt1 = k.sb("xtP", [128, 8, TT], F32, st)
            dxt1 = [Dep() for _ in range(8)]
            hTs = [k.sb("hT0", [128, 8, TT], BF16, st), k.sb("hT1", [128, 8, TT], BF16, st)]
            dhTs = [[Dep() for _ in range(8)] for _ in range(2)]
            stage = [k.sb("stage%d" % i, [128, 4, TT], BF16, st) for i in range(2)]
            dstage = [Dep() for _ in range(2)]
            nst = [0]
            npb = [0]
            nev = [0]

            def next_stage():
                i = nst[0] % 2
                nst[0] += 1
                return stage[i], dstage[i]

            def next_bank():
                i = 1 + npb[0] % 4
                npb[0] += 1
                return i

            def evac(out, pi, func=None, scale=1.0, wdep=None):
                if func is not None or nev[0] % 2 == 0:
                    f = func if func is not None else AF.Copy
                    act.op(lambda: nc.scalar.activation(out=out, in_=k.ps[pi], func=f, scale=scale),
                           reads=[k.dps[pi]], writes=[wdep])
                else:
                    dve.op(lambda: nc.vector.tensor_copy(out=out, in_=k.ps[pi]), reads=[k.dps[pi]], writes=[wdep])
                nev[0] += 1

            def units_for(tt, hT, dhT):
                t0 = tt * TT
                units = []

                def fmaj(col0, nch, func, scale, dst, ddst):
                    for g4 in range(nch // 4):
                        holder = {}
                        for f4 in range(4):
                            def u(g4=g4, f4=f4, holder=holder):
                                if f4 == 0:
                                    holder["s"] = next_stage()
                                sg_, dsg_ = holder["s"]
                                fc = g4 * 4 + f4
                                pi = next_bank()
                                for kk in range(8):
                                    pe.op(lambda: nc.tensor.matmul(k.ps[pi], win[:, kk, col0 + fc * 128:col0 + (fc + 1) * 128], hT[:, kk, :],
                                                                   start=(kk == 0), stop=(kk == 7)),
                                          reads=[dW, dhT[kk]], writes=[k.dps[pi]], inc=(kk == 7))
                                evac(sg_[:, f4, :], pi, func, scale, dsg_)
                                if f4 == 3:
                                    qs.dma(dst[g4 * 4:(g4 + 1) * 4, :, t0:t0 + TT].rearrange("c p t -> p c t"), sg_[:], reads=[dsg_], writes=[ddst[tt]])
                            units.append(u)

                def tmaj(col0, dst, dcol0, ddst):
                    holder = {}
                    for tb in range(4):
                        def u(tb=tb, holder=holder):
                            if tb == 0:
                                holder["s"] = next_stage()
                            sg_, dsg_ = holder["s"]
                            pi = next_bank()
                            for kk in range(8):
                                pe.op(lambda: nc.tensor.matmul(k.ps[pi], hT[:, kk, tb * 128:(tb + 1) * 128], win[:, kk, col0:col0 + 512],
                                                               start=(kk == 0), stop=(kk == 7)),
                                      reads=[dW, dhT[kk]], writes=[k.dps[pi]], inc=(kk == 7))
                            evac(sg_[:, tb, :], pi, None, 1.0, dsg_)
                            if tb == 3:
                                qs.dma(dst[t0:t0 + TT, dcol0:dcol0 + 512].rearrange("(b t) f -> t b f", t=128), sg_[:], reads=[dsg_], writes=[ddst[tt]])
                        units.append(u)

                fmaj(0, 4, AF.Copy, 0.125, k.qT_s, k.d_q)
                fmaj(512, 4, None, 1.0, k.kT_s, k.d_k)
                fmaj(2048, 8, AF.Sigmoid, 1.0, k.oT_s, k.d_o)
                tmaj(512, k.kt_s, 0, k.d_kt)
                tmaj(1024, k.vt_s, 0, k.d_vt)
                tmaj(1536, k.vt_s, 512, k.d_vt)
                for tb in range(4):
                    def u(tb=tb):
                        pgt = k.ps[5][:, tb * 32:(tb + 1) * 32]
                        for kk in range(8):
                            pe.op(lambda: nc.tensor.matmul(pgt, hT[:, kk, tb * 128:(tb + 1) * 128], win[:, kk, 3072:3104],
                                                           start=(kk == 0), stop=(kk == 7)),
                                  reads=[dW, dhT[kk]], writes=[k.dps[5]], inc=(kk == 7))
                        dve.op(lambda: nc.vector.tensor_tensor(out=gates[:, tt * 4 + tb, :], in0=pgt, in1=k.bgate[:], op=ALU.add),
                               reads=[k.dps[5], dc_], writes=[d_gates])
                    units.append(u)
                return units

            qs.dma(xt1[:], _xs_tile(k, 0), reads=[k.d_xs[0]], writes=dxt1)
            _modnorm(k, B, xt1, dxt1, l, j, 0, hTs[0], dhTs[0])
            for tt in range(NTILE):
                units = units_for(tt, hTs[tt % 2], dhTs[tt % 2])
                _emit_units_pipelined(k, B, units, tt, xt1, dxt1, l, j, hTs, dhTs)
        _barrier(k)
        if getattr(k, "dbg_stop", "") == "P":
            return

        SEQS = [(list(range(0, 32)), 46, None)] + [([32 + 2 * i, 33 + 2 * i], 47, i) for i in range(4)]
        with ExitStack() as st:
            nlf = k.sb("nlf", [128, NCH, 64], F32, st)
            apad = k.sb("apad", [128, NCH, 64], F32, st)
            nbS = k.sb("nbS", [128, NCH, 16], F32, st)
            pre = k.sb("pre", [128, NCH, 16], F32, st)
            pre2 = k.sb("pre2", [128, NCH, 16], F32, st)
            amaxT = k.sb("amaxT", [64, NCH], F32, st)
            totS = k.sb("totS", [64, NCH], F32, st)
            MT = k.sb("MT", [64, NCH], F32, st)
            dG = k.sb("dG", [64, NCH], F32, st)
            Gx = k.sb("Gx", [64, NCH], F32, st)
            mst = [k.sb("mstF", [64, 48], F32, st), k.sb("mstB", [64, 48], F32, st)]
            RM = k.sb("RM", [64, NCH, 16], F32, st)
            RG = k.sb("RG", [64, NCH, 16], F32, st)
            d_nlf, d_apad, d_nbS, d_pre, d_pre2, d_amax, d_tot, d_RM, d_RG, d_Gx = [Dep() for _ in range(10)]
            d_ch = [Dep(), Dep()]
            pool.op(lambda: nc.gpsimd.memset(nlf[:], 0.0), writes=[d_nlf])
            pool.op(lambda: nc.gpsimd.memset(apad[:], 0.0), writes=[d_apad])
            pool.op(lambda: nc.gpsimd.memset(MT[:], 0.0), writes=[d_ch[0], d_ch[1]])
            pool.op(lambda: nc.gpsimd.memset(dG[:], 0.0), writes=[d_ch[0], d_ch[1]])
            for d in range(2):
                pool.op(lambda: nc.gpsimd.memset(mst[d][:], 0.0), writes=[d_ch[d]])
            with nc.allow_non_contiguous_dma(reason="tiny state loads"):
                for d in range(2):
                    qs.dma(mst[d][32 * d:32 * d + 8, 46:47], k.st_m[d].rearrange("(h o) -> h o", o=1), writes=[d_ch[d]])
            for d in range(2):
                act.op(lambda: nc.scalar.activation(out=nlf[:, :, 32 * d:32 * d + 8], in_=gates[:, :, 16 * d + 8:16 * d + 16],
                                                    func=AF.Exp, scale=-1.0), reads=[d_gates, d_nlf], writes=[d_nlf])
            for d in range(2):
                act.op(lambda: nc.scalar.activation(out=nlf[:, :, 32 * d:32 * d + 8], in_=nlf[:, :, 32 * d:32 * d + 8],
                                                    func=AF.Ln, bias=1.0, scale=1.0), reads=[d_nlf], writes=[d_nlf])
            if getattr(k, "dbg_stop", "") == "G1":
                return
            pb = [k.ps[1][:, 0:320].rearrange("p (c h) -> p c h", h=8), k.ps[2][:, 0:320].rearrange("p (c h) -> p c h", h=8)]
            pe.op(lambda: nc.tensor.matmul(pb[0], k.masku_f[:], nlf[:, :, 0:8], start=True, stop=True),
                  reads=[d_nlf, dc_], writes=[k.dps[1]])
            if getattr(k, "dbg_stop", "") == "G2a":
                return
            pe.op(lambda: nc.tensor.matmul(pb[1], k.maskl_f[:], nlf[:, :, 32:40], start=True, stop=True),
                  reads=[d_nlf, dc_], writes=[k.dps[2]])
            if getattr(k, "dbg_stop", "") == "G2b":
                return
            for d in range(2):
                dve.op(lambda: nc.vector.tensor_tensor(out=apad[:, :, 32 * d:32 * d + 8], in0=pb[d], in1=gates[:, :, 16 * d:16 * d + 8], op=ALU.add),
                       reads=[k.dps[1 + d], d_gates, d_apad], writes=[d_apad])
                if getattr(k, "dbg_stop", "") == "G2c":
                    return
                dve.op(lambda: nc.vector.tensor_copy(out=nbS[:, :, 8 * d:8 * d + 8], in_=pb[d]), reads=[k.dps[1 + d]], writes=[d_nbS])
            if getattr(k, "dbg_stop", "") == "G2":
                return
            for g in range(NCH // 4):
                pi = 3 + g % 2
                pv = k.ps[pi][:].rearrange("p (a b) -> p a b", a=4)
                for i in range(4):
                    c = 4 * g + i
                    pe.op(lambda: nc.tensor.transpose(pv[0:64, i, :], apad[:, c, :], k.ident_f[:]), reads=[d_apad, dc_], writes=[k.dps[pi]])
                dve.op(lambda: nc.vector.reduce_max(out=amaxT[:, 4 * g:4 * g + 4], in_=pv[0:64, :, :], axis=AX.X),
                       reads=[k.dps[pi]], writes=[d_amax])
            if getattr(k, "dbg_stop", "") == "G3":
                return
            for c in range(NCH):
                pe.op(lambda: nc.tensor.matmul(k.ps[5][0:64, c:c + 1], nlf[:, c, :], k.ones_f[:, 0:1], start=True, stop=True),
                      reads=[d_nlf, dc_], writes=[k.dps[5]])
            act.op(lambda: nc.scalar.copy(out=totS[:], in_=k.ps[5][0:64, 0:NCH]), reads=[k.dps[5]], writes=[d_tot])
            if getattr(k, "dbg_stop", "") == "G4":
                return
            colp = [0, 0]
            for (chunks, init_col, pseq) in SEQS:
                for d in range(2):
                    eng = dve
                    E = nc.vector
                    r = slice(32 * d, 32 * d + 8)
                    order = chunks if d == 0 else chunks[::-1]
                    cur = init_col
                    for c in order:
                        eng.op(lambda: E.tensor_tensor(out=MT[r, c:c + 1], in0=mst[d][r, cur:cur + 1], in1=amaxT[r, c:c + 1], op=ALU.max),
                               reads=[d_ch[d], d_amax], writes=[d_ch[d]])
                        eng.op(lambda: E.tensor_tensor(out=dG[r, c:c + 1], in0=mst[d][r, cur:cur + 1], in1=MT[r, c:c + 1], op=ALU.subtract),
                               reads=[d_ch[d]], writes=[d_ch[d]])
                        nxt = colp[d]
                        colp[d] += 1
                        eng.op(lambda: E.tensor_tensor(out=mst[d][r, nxt:nxt + 1], in0=MT[r, c:c + 1], in1=totS[r, c:c + 1], op=ALU.subtract),
                               reads=[d_ch[d], d_tot], writes=[d_ch[d]])
                        cur = nxt
                    if pseq is not None:
                        with nc.allow_non_contiguous_dma(reason="tiny state store"):
                            qs.dma(k.nsm[pseq, d, :].rearrange("(h o) -> h o", o=1), mst[d][r, cur:cur + 1], reads=[d_ch[d]], writes=[k.d_out])
            if getattr(k, "dbg_stop", "") == "G5":
                return
            act.op(lambda: nc.scalar.activation(out=Gx[:], in_=dG[:], func=AF.Exp), reads=[d_ch[0], d_ch[1]], writes=[d_Gx])
            e16b = k.e16[:].unsqueeze(1).to_broadcast([64, NCH, 16])
            dve.op(lambda: nc.vector.tensor_tensor(out=RM[:], in0=e16b, in1=MT[:].unsqueeze(2).to_broadcast([64, NCH, 16]), op=ALU.mult),
                   reads=[d_ch[0], d_ch[1], dc_], writes=[d_RM])
            dve.op(lambda: nc.vector.tensor_tensor(out=RG[:], in0=e16b, in1=Gx[:].unsqueeze(2).to_broadcast([64, NCH, 16]), op=ALU.mult),
                   reads=[d_Gx, dc_], writes=[d_RG])
            if getattr(k, "dbg_stop", "") == "G6":
                return
            Mv, Gv = [], []
            for hh in range(2):
                pm = k.ps[1 + hh][:, 0:320].rearrange("p (c h) -> p c h", h=16)
                pg = k.ps[3 + hh][:, 0:320].rearrange("p (c h) -> p c h", h=16)
                pe.op(lambda: nc.tensor.matmul(pm, k.ones_f[0:64, :], RM[:, 20 * hh:20 * hh + 20, :], start=True, stop=True),
                      reads=[d_RM, dc_], writes=[k.dps[1 + hh]])
                pe.op(lambda: nc.tensor.matmul(pg, k.ones_f[0:64, :], RG[:, 20 * hh:20 * hh + 20, :], start=True, stop=True),
                      reads=[d_RG, dc_], writes=[k.dps[3 + hh]])
                Mv.append(pm)
                Gv.append(pg)
            if getattr(k, "dbg_stop", "") == "G7":
                return
            for hh in range(2):
                cr = slice(20 * hh, 20 * hh + 20)
                for d in range(2):
                    dve.op(lambda: nc.vector.tensor_tensor(out=pre[:, cr, 8 * d:8 * d + 8], in0=apad[:, cr, 32 * d:32 * d + 8],
                                                           in1=Mv[hh][:, :, 8 * d:8 * d + 8], op=ALU.subtract),
                           reads=[d_apad, k.dps[1 + hh]], writes=[d_pre])
                dve.op(lambda: nc.vector.tensor_tensor(out=pre2[:, cr, :], in0=nbS[:, cr, :], in1=Mv[hh], op=ALU.subtract),
                       reads=[d_nbS, k.dps[1 + hh]], writes=[d_pre2])
                gv5 = k.ps[3 + hh][:, 0:320].rearrange("p (c d h t) -> p c d h t", d=2, h=4, t=2)
                dve.op(lambda: nc.vector.tensor_copy(out=gsel[0:64, cr, :, :], in_=gv5[0:64, :, :, :, 0]), reads=[k.dps[3 + hh]], writes=[d_gsel])
                dve.op(lambda: nc.vector.tensor_copy(out=gsel[64:128, cr, :, :], in_=gv5[64:128, :, :, :, 1]), reads=[k.dps[3 + hh]], writes=[d_gsel])
            if getattr(k, "dbg_stop", "") == "G8":
                return
            if getattr(k, "dbg_stop", "") == "G9":
                _barrier(k)
            act.op(lambda: nc.scalar.activation(out=cs[:], in_=pre[:], func=AF.Exp), reads=[d_pre], writes=[d_cs])
            if getattr(k, "dbg_stop", "") == "G10":
                return
            act.op(lambda: nc.scalar.activation(out=ecl[:], in_=pre2[:], func=AF.Exp), reads=[d_pre2], writes=[d_ecl])
            dve.op(lambda: nc.vector.tensor_copy(out=cs_bf[:], in_=cs[:]), reads=[d_cs], writes=[d_cs])
        _barrier(k)
        if getattr(k, "dbg_stop", "") == "G":
            return

        with ExitStack() as st:
            sbl = lambda n, sh, dt: k.sb(n, sh, dt, st)
            qTc = [[sbl("qTc%d%d" % (d, i), [128, 4, 128], BF16) for i in range(2)] for d in range(2)]
            kTc = [[sbl("kTc%d%d" % (d, i), [128, 4, 128], BF16) for i in range(2)] for d in range(2)]
            ktc = [[sbl("ktc%d%d" % (d, i), [128, 8, 64], BF16) for i in range(2)] for d in range(2)]
            vtc = [[sbl("vtc%d%d" % (d, i), [128, 8, 128], BF16) for i in range(2)] for d in range(2)]
            d_ld = [[Dep(), Dep()] for d in range(2)]
            vp = [sbl("vp%d" % d, [128, 8, 128], BF16) for d in range(2)]
            d_vp = [Dep(), Dep()]
            C = [sbl("C%d" % d, [128, 4, 129], F32) for d in range(2)]
            Cg = [sbl("Cg%d" % d, [128, 4, 129], F32) for d in range(2)]
            Cgb = [sbl("Cgb%d" % d, [128, 4, 129], BF16) for d in range(2)]
            d_C, d_Cg, d_Cgb = [Dep(), Dep()], [Dep(), Dep()], [Dep(), Dep()]
            PT = [sbl("PT%d" % d, [128, 4, 2, 128], BF16) for d in range(2)]
            d_PT = [[Dep(), Dep()], [Dep(), Dep()]]
            ad = [sbl("ad%d" % d, [128, 8], F32) for d in range(2)]
            rr = [sbl("rr%d" % d, [128, 8], F32) for d in range(2)]
            d_ad, d_rr = [Dep(), Dep()], [Dep(), Dep()]
            hbuf = [[sbl("hbuf%d%d" % (d, i), [128, 8, 128], BF16) for i in range(2)] for d in range(2)]
            d_hbuf = [[Dep(), Dep()], [Dep(), Dep()]]
            NUM = [k.ps[4].rearrange("p (a b) -> p a b", a=4), k.ps[5].rearrange("p (a b) -> p a b", a=4)]
            DEN = k.ps[6][:, 0:8]
            UNn = k.ps[6][:, 8:12]
            UN = k.ps[7].rearrange("p (a b) -> p a b", a=4)
            d_NUM, d_DEN, d_UN = [Dep(), Dep()], Dep(), Dep()
            nld = [0, 0]

            def stageL(d, c):
                tt = c // 4
                t0 = c * 128
                s2 = nld[d] % 2
                nld[d] += 1
                qs.dma(qTc[d][s2][:], k.qT_s[0:4, :, t0:t0 + 128].rearrange("c p t -> p c t"), reads=[k.d_q[tt]], writes=[d_ld[d][s2]])
                qs.dma(kTc[d][s2][:], k.kT_s[0:4, :, t0:t0 + 128].rearrange("c p t -> p c t"), reads=[k.d_k[tt]], writes=[d_ld[d][s2]])
                qs.dma(ktc[d][s2][:], k.kt_s[t0:t0 + 128, :].rearrange("t (h e) -> t h e", h=8), reads=[k.d_kt[tt]], writes=[d_ld[d][s2]])
                qs.dma(vtc[d][s2][:], k.vt_s[t0:t0 + 128, :].rearrange("t (h e) -> t h e", h=8), reads=[k.d_vt[tt]], writes=[d_ld[d][s2]])
                return s2

            def stageA(d, c, s2):
                for h in range(8):
                    hp, par = h // 2, h % 2
                    rs_ = slice(64 * par, 64 * par + 64)
                    bank = 2 * d + par
                    STv = k.ps[bank].rearrange("p (a b) -> p a b", a=4)
                    pe.op(lambda: nc.tensor.matmul(STv[:, hp, :], kTc[d][s2][rs_, hp, :], qTc[d][s2][rs_, hp, :], start=True, stop=True),
                          reads=[d_ld[d][s2]], writes=[k.dps[bank]])

            def stageB(d, c, s2):
                mask = k.masku4 if d == 0 else k.maskl4
                dve.op(lambda: nc.vector.tensor_tensor(out=Cg[d][:], in0=C[d][:], in1=gsel[:, c, d, :].unsqueeze(2).to_broadcast([128, 4, 129]), op=ALU.mult),
                       reads=[d_C[d], d_gsel], writes=[d_Cg[d]])
                act.op(lambda: nc.scalar.copy(out=Cgb[d][:], in_=Cg[d][:]), reads=[d_Cg[d]], writes=[d_Cgb[d]])
                for par in range(2):
                    bank = 2 * d + par
                    STv = k.ps[bank].rearrange("p (a b) -> p a b", a=4)
                    dve.op(lambda: nc.vector.tensor_tensor(out=PT[d][:, :, par, :], in0=STv, in1=mask[:, 0:1, :].to_broadcast([128, 4, 128]), op=ALU.mult),
                           reads=[k.dps[bank], dc_], writes=[d_PT[d][par]])
                pool.op(lambda: nc.gpsimd.tensor_tensor(out=vp[d][:], in0=vtc[d][s2][:], in1=cs_bf[:, c, 8 * d:8 * d + 8].unsqueeze(2).to_broadcast([128, 8, 128]), op=ALU.mult),
                        reads=[d_ld[d][s2], d_cs], writes=[d_vp[d]])

            def stageC(d, c, s2):
                for h in range(8):
                    hp, par = h // 2, h % 2
                    rs_ = slice(64 * par, 64 * par + 64)
                    pth = PT[d][:, hp, par, :]
                    csc = cs_bf[:, c, 8 * d + h:8 * d + h + 1]
                    nb = h // 4
                    pe.op(lambda: nc.tensor.matmul(NUM[nb][:, h % 4, :], pth, vp[d][:, h, :], start=True, stop=False),
                          reads=[d_PT[d][par], d_vp[d]], writes=[d_NUM[nb], k.dbank[4 + nb]])
                    pe.op(lambda: nc.tensor.matmul(NUM[nb][:, h % 4, :], qTc[d][s2][rs_, hp, :], Cgb[d][rs_, hp, 0:128], start=False, stop=True),
                          reads=[d_Cgb[d], d_ld[d][s2]], writes=[d_NUM[nb], k.dbank[4 + nb]])
                    pe.op(lambda: nc.tensor.matmul(DEN[:, h:h + 1], pth, csc, start=True, stop=False),
                          reads=[d_PT[d][par], d_cs], writes=[d_DEN, k.dbank[6]])
                    pe.op(lambda: nc.tensor.matmul(DEN[:, h:h + 1], qTc[d][s2][rs_, hp, :], Cgb[d][rs_, hp, 128:129], start=False, stop=True),
                          reads=[d_Cgb[d], d_ld[d][s2]], writes=[d_DEN, k.dbank[6]])
                    pe.op(lambda: nc.tensor.matmul(UN[rs_, hp, :], ktc[d][s2][:, h, :], vp[d][:, h, :], start=True, stop=True),
                          reads=[d_vp[d], d_ld[d][s2]], writes=[d_UN, k.dbank[7]])
                    pe.op(lambda: nc.tensor.matmul(UNn[rs_, hp:hp + 1], ktc[d][s2][:, h, :], csc, start=True, stop=True),
                          reads=[d_cs, d_ld[d][s2]], writes=[d_DEN, k.dbank[6]])

            def stageD(d, c, s2):
                t0 = c * 128
                act.op(lambda: nc.scalar.activation(out=ad[d][:], in_=DEN, func=AF.Abs), reads=[d_DEN], writes=[d_ad[d], k.dbank[6]])
                dve.op(lambda: nc.vector.tensor_tensor(out=C[d][:, :, 128], in0=Cg[d][:, :, 128], in1=UNn, op=ALU.add),
                       reads=[d_Cg[d], d_DEN], writes=[d_C[d], k.dbank[6]])
                dve.op(lambda: nc.vector.tensor_tensor(out=C[d][:, :, 0:128], in0=Cg[d][:, :, 0:128], in1=UN, op=ALU.add),
                       reads=[d_Cg[d], d_UN], writes=[d_C[d], k.dbank[7]])
                dve.op(lambda: nc.vector.tensor_tensor(out=rr[d][:], in0=ad[d][:], in1=ecl[:, c, 8 * d:8 * d + 8], op=ALU.max),
                       reads=[d_ad[d], d_ecl], writes=[d_rr[d]])
                dve.op(lambda: nc.vector.reciprocal(out=rr[d][:], in_=rr[d][:]), reads=[d_rr[d]], writes=[d_rr[d]])
                hb_ = hbuf[d][s2]
                for g4 in range(2):
                    rb = rr[d][:, 4 * g4:4 * g4 + 4].unsqueeze(2).to_broadcast([128, 4, 128])
                    dve.op(lambda: nc.vector.tensor_tensor(out=hb_[:, 4 * g4:4 * g4 + 4, :], in0=NUM[g4], in1=rb, op=ALU.mult),
                           reads=[d_NUM[g4], d_rr[d]], writes=[d_hbuf[d][s2], k.dbank[4 + g4]])
                dst = k.hf_s if d == 0 else k.hb_s
                qs.dma(dst[t0:t0 + 128, :], hb_[:].rearrange("p h e -> p (h e)"), reads=[d_hbuf[d][s2]], writes=[k.d_hf[d][c]])

            for (chunks, init_col, pseq) in SEQS:
                orders = [chunks, chunks[::-1]]
                for d in range(2):
                    if pseq is None:
                        with nc.allow_non_contiguous_dma(reason="state load"):
                            cview = k.st_c[d].rearrange("(hp two) dd v -> two dd hp v", two=2)
                            nview = k.st_n[d].rearrange("(hp two) dd -> two dd hp", two=2)
                            for two in range(2):
                                qs.dma(C[d][64 * two:64 * two + 64, :, 0:128], cview[two], writes=[d_C[d]])
                                qs.dma(C[d][64 * two:64 * two + 64, :, 128], nview[two], writes=[d_C[d]])
                    else:
                        pool.op(lambda: nc.gpsimd.memset(C[d][:], 0.0), writes=[d_C[d]])
                n = len(chunks)
                slots = [[stageL(d, orders[d][0]) for d in range(2)]]
                for i in range(n):
                    if i + 1 < n:
                        slots.append([stageL(d, orders[d][i + 1]) for d in range(2)])
                    for d in range(2):
                        stageA(d, orders[d][i], slots[i][d])
                    for d in range(2):
                        stageB(d, orders[d][i], slots[i][d])
                    for d in range(2):
                        stageC(d, orders[d][i], slots[i][d])
                        stageD(d, orders[d][i], slots[i][d])
                if pseq is not None:
                    with nc.allow_non_contiguous_dma(reason="state store"):
                        for d in range(2):
                            cview = k.nsc[pseq, d].rearrange("(hp two) dd v -> two dd hp v", two=2)
                            nview = k.nsn[pseq, d].rearrange("(hp two) dd -> two dd hp", two=2)
                            for two in range(2):
                                qs.dma(cview[two], C[d][64 * two:64 * two + 64, :, 0:128], reads=[d_C[d]], writes=[k.d_out])
                                qs.dma(nview[two], C[d][64 * two:64 * two + 64, :, 128], reads=[d_C[d]], writes=[k.d_out])
        _barrier(k)

    with ExitStack() as st:
        sbl = lambda n, sh, dt: k.sb(n, sh, dt, st)
        hfc = [sbl("hfc%d" % i, [128, 8, 128], BF16) for i in range(2)]
        hbc = [sbl("hbc%d" % i, [128, 8, 128], BF16) for i in range(2)]
        hsum = [sbl("hsum%d" % i, [128, 8, 128], F32) for i in range(2)]
        oTc = [sbl("oTc%d" % i, [128, 8, 128], BF16) for i in range(2)]
        d_ld = [Dep(), Dep()]
        d_hs = [Dep(), Dep()]
        sqh = [sbl("sqh%d" % i, [128, 8, 128], F32) for i in range(2)]
        d_sqh = [Dep(), Dep()]
        ms = [sbl("ms%d" % i, [128, 8], F32) for i in range(2)]
        d_ms = [Dep(), Dep()]
        epsc = sbl("epsc2", [128, 1], F32)
        pool.op(lambda: nc.gpsimd.memset(epsc[:], EPS), writes=[d_ms[0], d_ms[1]])
        hn = [sbl("hn%d" % i, [128, 8, 128], BF16) for i in range(2)]
        d_hn = [Dep(), Dep()]
        hgT = [sbl("hgT%d" % i, [128, 8, TT], BF16) for i in range(2)]
        d_hgT = [Dep(), Dep()]
        xq = [sbl("xq%d" % i, [128, 2, TT], F32) for i in range(2)]
        dxq = [[Dep(), Dep()], [Dep(), Dep()]]
        nxq = [0]
        tpbs = [k.ps[1].bitcast(BF16).rearrange("p (a b) -> p a b", a=8), k.ps[2].bitcast(BF16).rearrange("p (a b) -> p a b", a=8)]
        nl = [0]

        d_lo = [Dep(), Dep()]

        def loadh(c):
            s2 = c % 2
            t0 = c * 128
            qs.dma(hfc[s2][:], k.hf_s[t0:t0 + 128, :].rearrange("t (h e) -> t h e", h=8), reads=[k.d_hf[0][c]], writes=[d_ld[s2]])
            qs.dma(hbc[s2][:], k.hb_s[t0:t0 + 128, :].rearrange("t (h e) -> t h e", h=8), reads=[k.d_hf[1][c]], writes=[d_ld[s2]])

        def loado(c):
            s2 = c % 2
            t0 = c * 128
            qs.dma(oTc[s2][:], k.oT_s[:, :, t0:t0 + 128].rearrange("c p t -> p c t"), reads=[k.d_o[c // 4]], writes=[d_lo[s2]])

        def prep_stages(c, s2, hg, dhg):
            b2 = c % 2
            hs_ = hsum[s2]
            tpb = tpbs[b2]
            cpos = c % 4

            def s1():
                pool.op(lambda: nc.gpsimd.tensor_tensor(out=hs_[:], in0=hfc[s2][:], in1=hbc[s2][:], op=ALU.add),
                        reads=[d_ld[s2]], writes=[d_hs[s2]])

            def s2_():
                act.op(lambda: nc.scalar.activation(out=sqh[b2][:], in_=hs_[:], func=AF.Square), reads=[d_hs[s2]], writes=[d_sqh[b2]])

            def s3():
                dve.op(lambda: nc.vector.reduce_sum(out=ms[b2][:], in_=sqh[b2][:], axis=AX.X), reads=[d_sqh[b2]], writes=[d_ms[b2]])

            def s4():
                act.op(lambda: nc.scalar.activation(out=ms[b2][:], in_=ms[b2][:], func=AF.Sqrt, bias=epsc[:, 0:1], scale=1.0 / 128),
                       reads=[d_ms[b2]], writes=[d_ms[b2]])

            def s5():
                dve.op(lambda: nc.vector.reciprocal(out=ms[b2][:], in_=ms[b2][:]), reads=[d_ms[b2]], writes=[d_ms[b2]])
                dve.op(lambda: nc.vector.tensor_tensor(out=hn[b2][:], in0=hs_[:], in1=ms[b2][:].unsqueeze(2).to_broadcast([128, 8, 128]), op=ALU.mult),
                       reads=[d_hs[s2], d_ms[b2]], writes=[d_hn[b2]])

            def s6():
                for fc in range(8):
                    pe.op(lambda: nc.tensor.transpose(tpb[:, fc, :], hn[b2][:, fc, :], k.ident_b[:]), reads=[d_hn[b2], dc_], writes=[k.dps[1 + b2]])

            def s7():
                dve.op(lambda: nc.vector.tensor_tensor(out=sqh[b2][:], in0=tpb, in1=k.ghT[:].unsqueeze(2).to_broadcast([128, 8, 128]), op=ALU.mult),
                       reads=[k.dps[1 + b2], dc_], writes=[d_sqh[b2]])

            def s8():
                pool.op(lambda: nc.gpsimd.tensor_tensor(out=hg[:, :, cpos * 128:(cpos + 1) * 128], in0=sqh[b2][:], in1=oTc[s2][:], op=ALU.mult),
                        reads=[d_sqh[b2], d_lo[s2]], writes=[dhg])

            return [s1, s2_, s3, s4, s5, s6, s7, s8]

        def outproj_part(tt, part):
            cond = 0 if tt < 8 else 1
            hg, dhg = hgT[tt % 2], d_hgT[tt % 2]
            qi = nxq[0] % 2
            nxq[0] += 1
            xt, dxt = xq[qi], dxq[qi]
            xv = k.xT_s[2 * part:2 * part + 2, :, tt * TT:(tt + 1) * TT].rearrange("c p t -> p c t")
            qs.dma(xt[:], xv, reads=[k.d_xs[tt]], writes=dxt)
            for d2 in range(2):
                dc = 2 * part + d2
                pb = 3 + dc % 2
                po = k.ps[pb]
                for fc in range(8):
                    pe.op(lambda: nc.tensor.matmul(po, wout[:, fc, dc * 128:(dc + 1) * 128], hg[:, fc, :],
                                                   start=(fc == 0), stop=(fc == 7)),
                          reads=[dW, dhg], writes=[k.dps[pb]], inc=(fc == 7))
                dve.op(lambda: nc.vector.scalar_tensor_tensor(out=xt[:, d2, :], in0=po, scalar=k.G[l][:, j, dc, cond:cond + 1],
                                                              in1=xt[:, d2, :], op0=ALU.mult, op1=ALU.add),
                       reads=[k.dps[pb], dxt[d2], dc_], writes=[dxt[d2]])
            qs.dma(xv, xt[:], reads=dxt, writes=[k.d_xs[tt]])

        for c in (0, 1):
            loadh(c)
            loado(c)
        for tt in range(NTILE):
            hg, dhg = hgT[tt % 2], d_hgT[tt % 2]
            for pr in range(2):
                c0 = 4 * tt + 2 * pr
                stA = prep_stages(c0, c0 % 2, hg, dhg)
                stB = prep_stages(c0 + 1, (c0 + 1) % 2, hg, dhg)
                for si_, (fa, fb) in enumerate(zip(stA, stB)):
                    fa()
                    fb()
                    if si_ == 0:
                        for cn in (c0 + 2, c0 + 3):
                            if cn < NCH:
                                loadh(cn)
                    if si_ == 7:
                        for cn in (c0 + 2, c0 + 3):
                            if cn < NCH:
                                loado(cn)
                if tt > 0:
                    outproj_part(tt - 1, 2 * pr)
                    outproj_part(tt - 1, 2 * pr + 1)
        for part in range(4):
            outproj_part(NTILE - 1, part)


def _attn_phase(k, s):
    nc = k.nc
    pe, act, dve, pool, qs, qg = k.pe, k.act, k.dve, k.pool, k.qs, k.qg
    W = k.W[s]
    win = W[:, 0:22528].rearrange("p (k n) -> p k n", k=8)
    wout = W[:, 22528:22528 + 8192].rearrange("p (k n) -> p k n", k=8)
    dW = k.dW[s]
    dc_ = k.d_const
    l, j = 1, 1
    with ExitStack() as st0:
        kctxT = k.sb("kctxT", [128, 2, 512], BF16, st0)
        vctx = k.sb("vctx", [128, 4, 256], BF16, st0)
        d_kctx, d_vctx = Dep(), Dep()
        with ExitStack() as st:
            B = _norm_bufs(k, st)
            xt1 = k.sb("xtP", [128, 8, TT], F32, st)
            dxt1 = [Dep() for _ in range(8)]
            hTs = [k.sb("hT0", [128, 8, TT], BF16, st), k.sb("hT1", [128, 8, TT], BF16, st)]
            dhTs = [[Dep() for _ in range(8)] for _ in range(2)]
            stage = [k.sb("stage%d" % i, [128, 4, TT], BF16, st) for i in range(2)]
            dstage = [Dep() for _ in range(2)]
            ckf = stage[0][:].bitcast(F32)
            d_ckf = dstage[0]
            cosT = k.sb("cosT", [128, TT], F32, st)
            sinT = k.sb("sinT", [128, TT], F32, st)
            d_rope = Dep()
            tA = k.sb("tA", [128, TT], F32, st)
            tB = k.sb("tB", [128, TT], F32, st)
            d_tA, d_tB = Dep(), Dep()
            stf = [tA[:, 0:256], tB[:, 0:256]]
            d_stf = [d_tA, d_tB]
            nst, npb, nev, nsf = [0], [0], [0], [0]

            def next_stage():
                i = nst[0] % 2
                nst[0] += 1
                return stage[i], dstage[i]

            def next_bank():
                i = 1 + npb[0] % 6
                npb[0] += 1
                return i

            def evac(out, src, pi, wdep):
                if nev[0] % 2 == 0:
                    act.op(lambda: nc.scalar.activation(out=out, in_=src, func=AF.Copy), reads=[k.dps[pi]], writes=[wdep])
                else:
                    dve.op(lambda: nc.vector.tensor_copy(out=out, in_=src), reads=[k.dps[pi]], writes=[wdep])
                nev[0] += 1

            qs.dma(ckf, k.ck.rearrange("(b s) f -> s b f", s=128), writes=[d_ckf])
            qg.dma(vctx[:], k.cv.rearrange("(b s) f -> s b f", s=128), writes=[d_vctx])
            for b in range(4):
                for gp in range(2):
                    pi = next_bank()
                    pe.op(lambda: nc.tensor.transpose(k.ps[pi][:, 0:128], ckf[:, b, gp * 128:(gp + 1) * 128], k.ident_f[:]),
                          reads=[d_ckf, dc_], writes=[k.dps[pi]])
                    evac(kctxT[:, gp, b * 128:(b + 1) * 128], k.ps[pi][:, 0:128], pi, d_kctx)

            def units_for(tt, hT, dhT):
                rope = tt < 8
                t0 = tt * TT
                units = []

                def proj(col, pi):
                    for kk in range(8):
                        pe.op(lambda: nc.tensor.matmul(k.ps[pi], win[:, kk, col:col + 128], hT[:, kk, :],
                                                       start=(kk == 0), stop=(kk == 7)),
                              reads=[dW, dhT[kk]], writes=[k.dps[pi]], inc=(kk == 7))

                def fmaj(colA, colB, nch, dst, ddst):
                    for g4 in range((nch + 3) // 4):
                        holder = {}
                        n4 = min(4, nch - 4 * g4)
                        for f4 in range(n4):
                            def uA(g4=g4, f4=f4, holder=holder, n4=n4):
                                if f4 == 0:
                                    holder["s"] = next_stage()
                                sg_, dsg_ = holder["s"]
                                fc = g4 * 4 + f4
                                pa = next_bank()
                                holder["pa"] = pa
                                proj(colA + fc * 128, pa)
                                if not rope:
                                    evac(sg_[:, f4, :], k.ps[pa], pa, dsg_)
                                    if f4 == n4 - 1:
                                        qs.dma(dst[g4 * 4:g4 * 4 + n4, :, t0:t0 + TT].rearrange("c p t -> p c t"), sg_[:, 0:n4, :], reads=[dsg_], writes=[ddst[tt]])
                            units.append(uA)
                            if rope:
                                def uB(g4=g4, f4=f4, holder=holder, n4=n4):
                                    sg_, dsg_ = holder["s"]
                                    fc = g4 * 4 + f4
                                    pa = holder["pa"]
                                    pb_ = next_bank()
                                    proj(colB + fc * 128, pb_)
                                    dve.op(lambda: nc.vector.tensor_tensor(out=tA[:], in0=k.ps[pa], in1=cosT[:], op=ALU.mult),
                                           reads=[k.dps[pa], d_rope], writes=[d_tA])
                                    dve.op(lambda: nc.vector.tensor_tensor(out=tB[:], in0=k.ps[pb_], in1=sinT[:], op=ALU.mult),
                                           reads=[k.dps[pb_], d_rope], writes=[d_tB])
                                    pool.op(lambda: nc.gpsimd.tensor_tensor(out=sg_[:, f4, :], in0=tA[:], in1=tB[:], op=ALU.add),
                                            reads=[d_tA, d_tB], writes=[dsg_])
                                    if f4 == n4 - 1:
                                        qs.dma(dst[g4 * 4:g4 * 4 + n4, :, t0:t0 + TT].rearrange("c p t -> p c t"), sg_[:, 0:n4, :], reads=[dsg_], writes=[ddst[tt]])
                                units.append(uB)

                fmaj(0, 1536, 8, k.qT_s, k.d_q)
                fmaj(1024, 2560, 2, k.kT_s, k.d_k)
                holder = {}
                for tb in range(4):
                    def uV(tb=tb, holder=holder):
                        if tb == 0:
                            holder["s"] = next_stage()
                        sg_, dsg_ = holder["s"]
                        pi = next_bank()
                        pv = k.ps[pi][:, 0:256]
                        for kk in range(8):
                            pe.op(lambda: nc.tensor.matmul(pv, hT[:, kk, tb * 128:(tb + 1) * 128], win[:, kk, 1280:1536],
                                                           start=(kk == 0), stop=(kk == 7)),
                                  reads=[dW, dhT[kk]], writes=[k.dps[pi]], inc=(kk == 7))
                        if rope:
                            dve.op(lambda: nc.vector.tensor_copy(out=sg_[:, tb, 0:256], in_=pv), reads=[k.dps[pi]], writes=[dsg_])
                        else:
                            seq = (tt - 8) * 2 + tb // 2
                            tl = (tb % 2) * 128
                            sf = nsf[0] % 2
                            nsf[0] += 1
                            act.op(lambda: nc.scalar.activation(out=stf[sf], in_=pv, func=AF.Copy), reads=[k.dps[pi]], writes=[d_stf[sf]])
                            dve.op(lambda: nc.vector.tensor_copy(out=sg_[:, tb, 0:256], in_=stf[sf]), reads=[d_stf[sf]], writes=[dsg_])
                            qs.dma(k.ncv[seq, tl:tl + 128, :], stf[sf], reads=[d_stf[sf]], writes=[k.d_out])
                        if tb == 3:
                            qs.dma(k.vt_s[t0:t0 + TT, 0:256].rearrange("(b t) f -> t b f", t=128), sg_[:, :, 0:256], reads=[dsg_], writes=[k.d_vt[tt]])
                    units.append(uV)
                    if not rope:
                        def uK(tb=tb):
                            seq = (tt - 8) * 2 + tb // 2
                            tl = (tb % 2) * 128
                            pi2 = next_bank()
                            pk = k.ps[pi2][:, 0:256]
                            for kk in range(8):
                                pe.op(lambda: nc.tensor.matmul(pk, hT[:, kk, tb * 128:(tb + 1) * 128], win[:, kk, 1024:1280],
                                                               start=(kk == 0), stop=(kk == 7)),
                                      reads=[dW, dhT[kk]], writes=[k.dps[pi2]], inc=(kk == 7))
                            sf = nsf[0] % 2
                            nsf[0] += 1
                            act.op(lambda: nc.scalar.activation(out=stf[sf], in_=pk, func=AF.Copy), reads=[k.dps[pi2]], writes=[d_stf[sf]])
                            qs.dma(k.nck[seq, tl:tl + 128, :], stf[sf], reads=[d_stf[sf]], writes=[k.d_out])
                        units.append(uK)
                return units

            qs.dma(xt1[:], _xs_tile(k, 0), reads=[k.d_xs[0]], writes=dxt1)
            _modnorm(k, B, xt1, dxt1, l, j, 0, hTs[0], dhTs[0])
            for tt in range(NTILE):
                if tt < 8:
                    t0 = tt * TT
                    qs.dma(cosT[:], k.c_cos[:, t0:t0 + TT], writes=[d_rope])
                    qs.dma(sinT[:], k.c_sin[:, t0:t0 + TT], writes=[d_rope])
                units = units_for(tt, hTs[tt % 2], dhTs[tt % 2])
                _emit_units_pipelined(k, B, units, tt, xt1, dxt1, l, j, hTs, dhTs)
        _barrier(k)
        if getattr(k, "dbg_stop", "") == "AP":
            return

        with ExitStack() as st:
            sbl = lambda n, sh, dt: k.sb(n, sh, dt, st)
            qT = [sbl("qT%d" % i, [128, 8, TT], BF16) for i in range(2)]
            kwin = [sbl("kwin%d" % i, [128, 2, 768], BF16) for i in range(2)]
            vwin = [sbl("vwin%d" % i, [128, 6, 256], BF16) for i in range(2)]
            d_ld = [Dep(), Dep()]
            xt = sbl("xtA", [128, 8, TT], F32)
            dxt = [Dep() for _ in range(8)]
            oT = sbl("oT", [128, 8, TT], BF16)
            d_oT = Dep()
            PT = [sbl("PT%d" % i, [128, 2, 4, 128], BF16) for i in range(3)]
            d_PT = [Dep() for _ in range(3)]
            dtmp = sbl("dtmp", [128, 4, 128], F32)
            d_dtmp = Dep()
            npt, nstb = [0], [0]

            def issue_loads(tt):
                s2 = tt % 2
                if tt < 8:
                    blo, bhi = max(4 * tt - 1, 0), min(4 * tt + 4, 31)
                else:
                    blo, bhi = 4 * tt, 4 * tt + 3
                nb = bhi - blo + 1
                tiles = sorted(set(b // 4 for b in range(blo, bhi + 1)))
                qs.dma(qT[s2][:], k.qT_s[:, :, tt * TT:(tt + 1) * TT].rearrange("c p t -> p c t"), reads=[k.d_q[tt]], writes=[d_ld[s2]])
                qs.dma(kwin[s2][:, :, 0:nb * 128], k.kT_s[0:2, :, blo * 128:(bhi + 1) * 128].rearrange("c p t -> p c t"),
                       reads=[k.d_k[t] for t in tiles], writes=[d_ld[s2]])
                qs.dma(vwin[s2][:, 0:nb, :], k.vt_s[blo * 128:(bhi + 1) * 128, 0:256].rearrange("(b t) f -> t b f", t=128),
                       reads=[k.d_vt[t] for t in tiles], writes=[d_ld[s2]])
                return blo

            blo_next = issue_loads(0)
            for tt in range(NTILE):
                s2 = tt % 2
                cond = 0 if tt < 8 else 1
                blo = blo_next
                if tt + 1 < NTILE:
                    blo_next = issue_loads(tt + 1)
                qs.dma(xt[:], _xs_tile(k, tt), reads=[k.d_xs[tt]], writes=dxt)
                steps = []
                for qb in range(4):
                    jb = 4 * tt + qb
                    keys = []
                    if tt < 8:
                        for kb, m in ((jb - 1, k.negl4), (jb, None), (jb + 1, k.negu4)):
                            if 0 <= kb <= 31:
                                keys.append(("lat", kb - blo, m))
                        for cb in range(4):
                            keys.append(("ctx", cb, None))
                    else:
                        base = jb - (jb % 2)
                        keys = [("lat", base - blo, None), ("lat", base + 1 - blo, None)]
                    for gp in range(2):
                        for ki, (kind, bi, m) in enumerate(keys):
                            steps.append((qb, gp, kind, bi, m, ki == 0, ki == len(keys) - 1))

                def emit_qk(stp):
                    qb, gp, kind, bi, m, first, last = stp
                    pr = nstb[0] % 2
                    nstb[0] += 1
                    pt = npt[0] % 3
                    npt[0] += 1
                    for ph in range(2):
                        rs_ = slice(64 * ph, 64 * ph + 64)
                        if kind == "lat":
                            kop, kd = kwin[s2][rs_, gp, bi * 128:(bi + 1) * 128], d_ld[s2]
                        else:
                            kop, kd = kctxT[rs_, gp, bi * 128:(bi + 1) * 128], d_kctx
                        pst = 2 * pr + ph
                        STv = k.ps[pst].rearrange("p (a b) -> p a b", a=4)
                        pe.op(lambda: nc.tensor.matmul(STv, kop, qT[s2][rs_, 4 * gp:4 * gp + 4, qb * 128:(qb + 1) * 128], start=True, stop=(m is None)),
                              reads=[kd, d_ld[s2]], writes=[k.dps[pst]])
                    if m is not None:
                        for ph in range(2):
                            pst = 2 * pr + ph
                            STv = k.ps[pst].rearrange("p (a b) -> p a b", a=4)
                            pe.op(lambda: nc.tensor.matmul(STv, k.ident_b[:], m[:, 0:1, :].to_broadcast([128, 4, 128]), start=False, stop=True),
                                  reads=[dc_], writes=[k.dps[pst]])
                    ST2 = k.psall[:, 2 * pr * 512:(2 * pr + 2) * 512]
                    act.op(lambda: nc.scalar.activation(out=PT[pt][:].rearrange("p a b c -> p (a b c)"), in_=ST2, func=AF.Exp, scale=0.125),
                           reads=[k.dps[2 * pr], k.dps[2 * pr + 1]], writes=[d_PT[pt]])
                    return pt

                def emit_pv(stp, pt):
                    qb, gp, kind, bi, m, first, last = stp
                    pn, pd = 4 + gp, 6 + gp
                    NUMv = k.ps[pn].rearrange("p (a b) -> p a b", a=4)
                    DENv = k.ps[pd].rearrange("p (a b) -> p a b", a=4)
                    for ph in range(2):
                        g = 2 * gp + ph
                        rs_ = slice(64 * ph, 64 * ph + 64)
                        if kind == "lat":
                            vop, vd = vwin[s2][:, bi, g * 64:(g + 1) * 64], d_ld[s2]
                        else:
                            vop, vd = vctx[:, bi, g * 64:(g + 1) * 64], d_vctx
                        pe.op(lambda: nc.tensor.matmul(NUMv[rs_, :, :], vop, PT[pt][:, ph], start=first, stop=last),
                              reads=[vd, d_PT[pt]], writes=[k.dps[pn]])
                    for ph in range(2):
                        rs_ = slice(64 * ph, 64 * ph + 64)
                        pe.op(lambda: nc.tensor.matmul(DENv[rs_, :, :], k.ones_b[:, 0:64], PT[pt][:, ph], start=first, stop=last),
                              reads=[dc_, d_PT[pt]], writes=[k.dps[pd]])
                    if last:
                        dve.op(lambda: nc.vector.tensor_tensor(out=dtmp[:], in0=DENv, in1=k.esT[:, 4 * gp:4 * gp + 4].unsqueeze(2).to_broadcast([128, 4, 128]), op=ALU.add),
                               reads=[k.dps[pd], dc_], writes=[d_dtmp])
                        dve.op(lambda: nc.vector.reciprocal(out=dtmp[:], in_=dtmp[:]), reads=[d_dtmp], writes=[d_dtmp])
                        dve.op(lambda: nc.vector.tensor_tensor(out=oT[:, 4 * gp:4 * gp + 4, qb * 128:(qb + 1) * 128], in0=NUMv, in1=dtmp[:], op=ALU.mult),
                               reads=[k.dps[pn], d_dtmp], writes=[d_oT])

                pend = emit_qk(steps[0])
                for si in range(len(steps)):
                    nxt_pts = emit_qk(steps[si + 1]) if si + 1 < len(steps) else None
                    emit_pv(steps[si], pend)
                    pend = nxt_pts
                for dc in range(8):
                    po = k.ps[0]
                    for fc in range(8):
                        pe.op(lambda: nc.tensor.matmul(po[:], wout[:, fc, dc * 128:(dc + 1) * 128], oT[:, fc, :], start=(fc == 0), stop=(fc == 7)),
                              reads=[dW, d_oT], writes=[k.dps[0]], inc=(fc == 7))
                    dve.op(lambda: nc.vector.scalar_tensor_tensor(out=xt[:, dc, :], in0=po[:], scalar=k.G[l][:, j, dc, cond:cond + 1],
                                                                  in1=xt[:, dc, :], op0=ALU.mult, op1=ALU.add),
                           reads=[k.dps[0], dxt[dc], dc_], writes=[dxt[dc]])
                qs.dma(_xs_tile(k, tt), xt[:], reads=dxt, writes=[k.d_xs[tt]])


_PROG = {}


def _consts():
    c = {}
    c["c_ident"] = np.eye(128, dtype=np.float32)
    s = np.arange(128)[:, None]
    t = np.arange(128)[None, :]
    c["c_masku"] = (t >= s).astype(np.float32)
    c["c_maskl"] = (t <= s).astype(np.float32)
    e = np.zeros((64, 16), np.float32)
    for h in range(8):
        e[h, h] = 1.0
        e[32 + h, 8 + h] = 1.0
    c["c_e16"] = e
    quarter = 16
    freqs = (np.float32(10000.0) ** (-np.arange(quarter, dtype=np.float32) / np.float32(quarter))).astype(np.float32)
    row = np.repeat(np.arange(64, dtype=np.float32), 64)
    col = np.tile(np.arange(64, dtype=np.float32), 64)
    ang_r = row[:, None] * freqs
    ang_c = col[:, None] * freqs
    ang = np.concatenate([ang_r, ang_r, ang_c, ang_c], axis=-1).astype(np.float32)
    cos = np.cos(ang).astype(np.float32).T
    sin = np.sin(ang).astype(np.float32).T
    sgn = np.where((np.arange(64) % 32) < 16, -1.0, 1.0).astype(np.float32)[:, None]
    c["c_cos"] = np.ascontiguousarray(np.concatenate([cos, cos], axis=0))
    c["c_sin"] = np.ascontiguousarray(np.concatenate([sin * sgn, sin * sgn], axis=0))
    return c


def _attn_perm():
    n = np.arange(1024)
    fc = n // 128
    ph = (n % 128) // 64
    d = n % 64
    h = 8 * (fc // 4) + 4 * ph + (fc % 4)
    sw = np.where((d % 32) < 16, d + 16, d - 16)
    cols_q = h * 64 + d
    cols_qs = h * 64 + sw
    nk = np.arange(256)
    dk = nk % 64
    swk = np.where((dk % 32) < 16, dk + 16, dk - 16)
    cols_k = 1024 + nk
    cols_ks = 1024 + (nk // 64) * 64 + swk
    return cols_q, cols_qs, cols_k, cols_ks, h


def _make_in_maps(inp):
    f = lambda a: np.ascontiguousarray(np.asarray(a, dtype=np.float32))
    cols_q, cols_qs, cols_k, cols_ks, hperm = _attn_perm()
    awi = np.asarray(inp["attn_w_in"][0])
    a_ext = f(np.concatenate([awi[:, cols_q], awi[:, cols_k], awi[:, 1280:1536], awi[:, cols_qs], awi[:, cols_ks]], axis=1))
    a_wout = f(np.asarray(inp["attn_w_out"][0])[cols_q, :])
    sink = np.asarray(inp["attn_sink"][0])
    hT = hperm.reshape(8, 2, 64)[:, :, 0]
    sinkT = np.empty((128, 8), np.float32)
    for fc in range(8):
        for ph in range(2):
            sinkT[ph * 64:(ph + 1) * 64, fc] = sink[hT[fc, ph]]
    shared = dict(
        w_mod=f(inp["w_mod"]), b_mod=f(inp["b_mod"]), norm_g=f(inp["norm_g"]),
        ffn1_w_gu=f(inp["ffn1_w_gu"]), ffn1_w_down=f(inp["ffn1_w_down"]),
        ffn2_w_gu=f(inp["ffn2_w_gu"]), ffn2_w_down=f(inp["ffn2_w_down"]),
        mlstm_w_in=f(inp["mlstm_w_in"][0]), mlstm_b_gate=f(inp["mlstm_b_gate"]), mlstm_g_head=f(inp["mlstm_g_head"][0]),
        mlstm_w_out=f(inp["mlstm_w_out"][0]), attn_w_in_ext=a_ext, attn_sinkT=sinkT, attn_w_out_p=a_wout,
        final_g=f(inp["final_g"]),
    )
    shared.update(_consts())
    maps = []
    for i in range(8):
        m = dict(shared)
        xs = np.asarray(inp["x_sample"][i])
        xp = np.asarray(inp["x_prompt"][4 * i:4 * i + 4]).reshape(1024, 1024)
        m["x_in"] = f(np.concatenate([xs, xp], axis=0))
        m["cvec"] = f(np.stack([np.asarray(inp["c"][i]), np.asarray(inp["c_ctx"])], axis=0))
        m["st_c"] = f(inp["state_c"][i, 0])
        m["st_n"] = f(inp["state_n"][i, 0])
        m["st_m"] = f(inp["state_m"][i, 0])
        m["ck"] = f(np.asarray(inp["cache_k"][i, 0]).reshape(512, 256))
        m["cv"] = f(np.asarray(inp["cache_v"][i, 0]).reshape(512, 256))
        maps.append(m)
    return maps


def kernel(**inputs):
    if "p" not in _PROG:
        _PROG["p"] = build_program()
    nc = _PROG["p"]
    maps = _make_in_maps(inputs)
    res = run_bass_kernel_spmd(nc, maps, core_ids=list(range(8)))
    R = res.results
    y = np.stack([r["y_out"] for r in R], axis=0)
    y_sample = np.ascontiguousarray(y[:, :4096, :])
    y_prompt = np.ascontiguousarray(y[:, 4096:, :].reshape(32, 256, 1024))
    nsc = np.concatenate([r["nsc"] for r in R], axis=0).reshape(32, 1, 2, 8, 64, 128)
    nsn = np.concatenate([r["nsn"] for r in R], axis=0).reshape(32, 1, 2, 8, 64)
    nsm = np.concatenate([r["nsm"] for r in R], axis=0).reshape(32, 1, 2, 8)
    nck = np.concatenate([r["nck"] for r in R], axis=0).reshape(32, 1, 256, 4, 64)
    ncv = np.concatenate([r["ncv"] for r in R], axis=0).reshape(32, 1, 256, 4, 64)
    return (y_prompt.astype(np.float32), y_sample.astype(np.float32), nsc.astype(np.float32), nsn.astype(np.float32),
            nsm.astype(np.float32), nck.astype(np.float32), ncv.astype(np.float32))
```

```python
import numpy as np
from contextlib import ExitStack
import concourse.bass as bass
import concourse.mybir as mybir
from concourse.bass_utils import run_bass_kernel_spmd

F32 = mybir.dt.float32
BF16 = mybir.dt.bfloat16
AF = mybir.ActivationFunctionType
ALU = mybir.AluOpType
AX = mybir.AxisListType

NTOK = 5120
TT = 512
NTILE = 10
NCH = 40
EPS = 1e-6
WSLOT = 33792
STRICT_SAME_ENGINE = False
_LOCAL_STRICT = [False]


class Dep:
    __slots__ = ("w", "rd")

    def __init__(self):
        self.w = None
        self.rd = {}


def _flat(xs):
    out = []
    for x in xs:
        if isinstance(x, (list, tuple)):
            out.extend(_flat(x))
        elif x is not None:
            out.append(x)
    return out


class Eng:
    def __init__(self, name, h, sem):
        self.name = name
        self.h = h
        self.sem = sem
        self.cnt = 0
        self.known = {}
        self.nwait = 0
        self.nins = 0

    def wait_tok(self, tok):
        sem, val, _ = tok
        k = id(sem)
        if self.known.get(k, 0) < val:
            self.h.wait_ge(sem, val)
            self.known[k] = val
            self.nwait += 1

    def _collect(self, reads, writes):
        toks = []
        for d in reads:
            if d.w is not None:
                toks.append(d.w)
        strict = (STRICT_SAME_ENGINE or _LOCAL_STRICT[0]) and self.name != "pe"
        for d in writes:
            if d.w is not None and (strict or d.w[2] != self.name):
                toks.append(d.w)
            for en, t in d.rd.items():
                if strict or en != self.name:
                    toks.append(t)
        return toks

    def op(self, fn, reads=(), writes=(), inc=True):
        reads = _flat(reads)
        writes = _flat(writes)
        for t in self._collect(reads, writes):
            self.wait_tok(t)
        ins = fn()
        self.nins += 1
        if inc:
            self.cnt += 1
            ins.then_inc(self.sem, 1)
            tok = (self.sem, self.cnt, self.name)
        else:
            tok = (self.sem, self.cnt + 1, self.name)
        for d in reads:
            d.rd[self.name] = tok
        for d in writes:
            d.w = tok
            d.rd = {}
        return ins

    def last_tok(self):
        return (self.sem, self.cnt, self.name) if self.cnt else None


class Queue(Eng):
    def __init__(self, name, h, sems):
        super().__init__(name, h, None)
        self.sems = sems
        self.k = 0

    def dma(self, out, in_, reads=(), writes=(), **kw):
        reads = _flat(reads)
        writes = _flat(writes)
        for t in self._collect(reads, writes):
            self.wait_tok(t)
        ns = len(self.sems)
        slot = self.k % ns
        gen = self.k // ns
        sem = self.sems[slot]
        if gen > 0:
            self.wait_tok((sem, 16 * gen, self.name))
        ins = self.h.dma_start(out=out, in_=in_, **kw)
        ins.then_inc(sem, 16)
        self.nins += 1
        tok = (sem, 16 * (gen + 1), "%s#%d" % (self.name, slot))
        self.k += 1
        for d in reads:
            d.rd[tok[2]] = tok
        for d in writes:
            d.w = tok
            d.rd = {}
        return ins

    def all_toks(self):
        ns = len(self.sems)
        out = []
        for slot in range(min(ns, self.k)):
            n = (self.k - 1 - slot) // ns + 1
            out.append((self.sems[slot], 16 * n, "%s#%d" % (self.name, slot)))
        return out


class K:
    pass


def _barrier(k):
    toks = []
    for e in (k.pe, k.act, k.dve, k.pool):
        t = e.last_tok()
        if t:
            toks.append(t)
    toks += k.qs.all_toks() + k.qg.all_toks()
    for e in (k.pe, k.act, k.dve, k.pool, k.qs):
        for t in toks:
            if t[2] != e.name:
                e.wait_tok(t)


def build_program(upto=99):
    nc = bass.Bass("TRN2", target_bir_lowering=False)
    k = K()
    k.nc = nc
    k.upto = upto

    def din(name, shape):
        return nc.dram_tensor(name, list(shape), F32, kind="ExternalInput").ap()

    def dout(name, shape):
        return nc.dram_tensor(name, list(shape), F32, kind="ExternalOutput").ap()

    def dscr(name, shape, dt):
        return nc.dram_tensor(name, list(shape), dt, kind="Internal").ap()

    k.x_in = din("x_in", [NTOK, 1024])
    k.cvec = din("cvec", [2, 1024])
    k.st_c = din("st_c", [2, 8, 64, 128])
    k.st_n = din("st_n", [2, 8, 64])
    k.st_m = din("st_m", [2, 8])
    k.ck = din("ck", [512, 256])
    k.cv = din("cv", [512, 256])
    k.w_mod = din("w_mod", [2, 1024, 9216])
    k.b_mod = din("b_mod", [2, 9216])
    k.norm_g = din("norm_g", [2, 3, 1024])
    k.f_gu = [din("ffn1_w_gu", [2, 1024, 5632]), din("ffn2_w_gu", [2, 1024, 5632])]
    k.f_dn = [din("ffn1_w_down", [2, 2816, 1024]), din("ffn2_w_down", [2, 2816, 1024])]
    k.m_win = din("mlstm_w_in", [1024, 3104])
    k.m_bg = din("mlstm_b_gate", [1, 32])
    k.m_gh = din("mlstm_g_head", [1024])
    k.m_wout = din("mlstm_w_out", [1024, 1024])
    k.a_win = din("attn_w_in_ext", [1024, 2816])
    k.a_sink = din("attn_sinkT", [128, 8])
    k.a_wout = din("attn_w_out_p", [1024, 1024])
    k.final_g = din("final_g", [1024])
    k.c_ident = din("c_ident", [128, 128])
    k.c_masku = din("c_masku", [128, 128])
    k.c_maskl = din("c_maskl", [128, 128])
    k.c_e16 = din("c_e16", [64, 16])
    k.c_cos = din("c_cos", [128, 4096])
    k.c_sin = din("c_sin", [128, 4096])
    k.y_out = dout("y_out", [NTOK, 1024])
    k.nsc = dout("nsc", [4, 2, 8, 64, 128])
    k.nsn = dout("nsn", [4, 2, 8, 64])
    k.nsm = dout("nsm", [4, 2, 8])
    k.nck = dout("nck", [4, 256, 256])
    k.ncv = dout("ncv", [4, 256, 256])
    if upto < 99:
        k.xT_s = nc.dram_tensor("xT_s", [8, 128, NTOK], F32, kind="ExternalOutput").ap()
    else:
        k.xT_s = dscr("xT_s", [8, 128, NTOK], F32)
    k.hT_s = dscr("hT_s", [8, 128, NTOK], BF16)
    k.qT_s = dscr("qT_s", [8, 128, NTOK], BF16)
    k.kT_s = dscr("kT_s", [4, 128, NTOK], BF16)
    k.oT_s = dscr("oT_s", [8, 128, NTOK], BF16)
    k.kt_s = dscr("kt_s", [NTOK, 512], BF16)
    k.vt_s = dscr("vt_s", [NTOK, 1024], BF16)
    k.hf_s = dscr("hf_s", [NTOK, 1024], BF16)
    k.hb_s = dscr("hb_s", [NTOK, 1024], BF16)
    k.d_xs = [Dep() for _ in range(NTILE)]
    k.d_hs = [Dep() for _ in range(NTILE)]
    k.d_q = [Dep() for _ in range(NTILE)]
    k.d_k = [Dep() for _ in range(NTILE)]
    k.d_o = [Dep() for _ in range(NTILE)]
    k.d_kt = [Dep() for _ in range(NTILE)]
    k.d_vt = [Dep() for _ in range(NTILE)]
    k.d_hf = [[Dep() for _ in range(NCH)] for _ in range(2)]
    k.d_out = Dep()

    with ExitStack() as es:
        k.es = es

        def S(n):
            return es.enter_context(nc.semaphore(n))

        k.pe = Eng("pe", nc.tensor, S("s_pe"))
        k.act = Eng("act", nc.scalar, S("s_act"))
        k.dve = Eng("dve", nc.vector, S("s_dve"))
        k.pool = Eng("pool", nc.gpsimd, S("s_pool"))
        k.qs = Queue("qs", nc.sync, [S("s_qs%d" % i) for i in range(8)])
        k.qg = Queue("qg", nc.gpsimd, [S("s_qg%d" % i) for i in range(6)])
        k.qg.known = k.pool.known

        k.uid = 0

        def sb(name, shape, dt, st=es):
            k.uid += 1
            return st.enter_context(nc.sbuf_tensor("%s_%d" % (name, k.uid), list(shape), dt))

        k.sb = sb
        k.W = [sb("W0", [128, WSLOT], BF16), sb("W1", [128, WSLOT], BF16)]
        k.dW = [Dep(), Dep()]
        k.ident_f = sb("ident_f", [128, 128], F32)
        k.ident_b = sb("ident_b", [128, 128], BF16)
        k.ones_b = sb("ones_b", [128, 128], BF16)
        k.ones_f = sb("ones_f", [128, 128], F32)
        k.masku_f = sb("masku_f", [128, 128], F32)
        k.maskl_f = sb("maskl_f", [128, 128], F32)
        k.masku4 = sb("masku4", [128, 1, 128], BF16)
        k.maskl4 = sb("maskl4", [128, 1, 128], BF16)
        k.negu4 = sb("negu4", [128, 1, 128], BF16)
        k.negl4 = sb("negl4", [128, 1, 128], BF16)
        k.e16 = sb("e16", [64, 16], F32)
        k.modT = [sb("modT0", [128, 72, 2], F32), sb("modT1", [128, 72, 2], F32)]
        k.A = [sb("A0", [128, 3, 8, 2], F32), sb("A1", [128, 3, 8, 2], F32)]
        k.G = [sb("G0", [128, 3, 8, 2], F32), sb("G1", [128, 3, 8, 2], F32)]
        k.fgT = sb("fgT", [128, 8], F32)
        k.ghT = sb("ghT", [128, 8], F32)
        k.esT = sb("esT", [128, 8], F32)
        k.bgate = sb("bgate", [128, 32], F32)
        k.d_const = Dep()
        k.psall = es.enter_context(nc.psum_tensor("psall", [128, 4096], F32))
        k.ps = [k.psall[:, i * 512:(i + 1) * 512] for i in range(8)]
        k.dps = [Dep() for _ in range(8)]
        k.dbank = [Dep() for _ in range(8)]

        phases = _phase_list()
        _phase0(k)
        for ph in phases:
            ph(k)
        _barrier(k)
    return nc


def _wslot_views_ffn(k, s):
    W = k.W[s]
    wgu = W[:, 0:22528].rearrange("p (k n) -> p k n", k=8)
    wdn = W[:, 22528:33792].rearrange("p (k n) -> p k n", k=11)
    return wgu, wdn


def _load_ffn_weights(k, s, l, which, half):
    wgu, wdn = _wslot_views_ffn(k, s)
    gu = k.f_gu[which][l].rearrange("(k p) n -> p k n", p=128)
    dn = k.f_dn[which][l]
    c0 = half * 1408
    for kk in range(0, 8, 2):
        k.qg.dma(wgu[:, kk:kk + 2, 0:1408], gu[:, kk:kk + 2, c0:c0 + 1408], writes=[k.dW[s]])
        k.qg.dma(wgu[:, kk:kk + 2, 1408:2816], gu[:, kk:kk + 2, 2816 + c0:2816 + c0 + 1408], writes=[k.dW[s]])
    dnv = dn[c0:c0 + 1408, :].rearrange("(k p) n -> p k n", p=128)
    k.qg.dma(wdn[:, 0:6, :], dnv[:, 0:6, :], writes=[k.dW[s]])
    k.qg.dma(wdn[:, 6:11, :], dnv[:, 6:11, :], writes=[k.dW[s]])


def _load_mlstm_weights(k, s):
    W = k.W[s]
    win = W[:, 0:24832].rearrange("p (k n) -> p k n", k=8)
    wout = W[:, 24832:24832 + 8192].rearrange("p (k n) -> p k n", k=8)
    src = k.m_win.rearrange("(k p) n -> p k n", p=128)
    for kk in range(0, 8, 2):
        k.qg.dma(win[:, kk:kk + 2, :], src[:, kk:kk + 2, :], writes=[k.dW[s]])
    k.qg.dma(wout, k.m_wout.rearrange("(k p) n -> p k n", p=128), writes=[k.dW[s]])
    return win, wout


def _load_attn_weights(k, s):
    W = k.W[s]
    win = W[:, 0:22528].rearrange("p (k n) -> p k n", k=8)
    wout = W[:, 22528:22528 + 8192].rearrange("p (k n) -> p k n", k=8)
    src = k.a_win.rearrange("(k p) n -> p k n", p=128)
    for kk in range(0, 8, 2):
        k.qg.dma(win[:, kk:kk + 2, :], src[:, kk:kk + 2, :], writes=[k.dW[s]])
    k.qg.dma(wout, k.a_wout.rearrange("(k p) n -> p k n", p=128), writes=[k.dW[s]])
    return win, wout


def _phase_list():
    specs = []
    for l in range(2):
        specs.append(("ffn", l, 0, 0))
        specs.append(("ffn", l, 0, 1))
        specs.append(("mix", l))
        specs.append(("ffn", l, 1, 0))
        specs.append(("ffn", l, 1, 1))
    n = len(specs)

    def loader(i):
        sp = specs[i]
        s = i % 2
        if sp[0] == "ffn":
            return lambda k: _load_ffn_weights(k, s, sp[1], sp[2], sp[3])
        if sp[1] == 0:
            return lambda k: _load_mlstm_weights(k, s)
        return lambda k: _load_attn_weights(k, s)

    groups = []
    i = 0
    while i < n:
        if specs[i][0] == "ffn":
            g = [i]
            while i + 1 < n and specs[i + 1][0] == "ffn":
                i += 1
                g.append(i)
            groups.append(g)
        else:
            groups.append([i])
        i += 1

    phases = []
    for g in groups:
        def run(k, g=g):
            g2 = [i for i in g if i < k.upto]
            if not g2:
                return
            if specs[g2[0]][0] == "ffn":
                segs = []
                for i in g2:
                    nxt = loader(i + 1) if (i + 1 < n and i + 1 < k.upto) else None
                    segs.append((i % 2, specs[i][1], specs[i][2], specs[i][3], i == n - 1, nxt))
                _ffn_run(k, segs)
            else:
                i = g2[0]
                if i + 1 < n and i + 1 < k.upto:
                    loader(i + 1)(k)
                if specs[i][1] == 0:
                    _mlstm_phase(k, i % 2)
                else:
                    _attn_phase(k, i % 2)
            _barrier(k)
        phases.append(run)
    return phases


def _xs_tile(k, tt):
    return k.xT_s[:, :, tt * TT:(tt + 1) * TT].rearrange("c p t -> p c t")


def _phase0(k):
    nc = k.nc
    pe, act, dve, pool, qs, qg = k.pe, k.act, k.dve, k.pool, k.qs, k.qg
    dc = k.d_const
    with ExitStack() as st:
        sb = lambda n, s, d: k.sb(n, s, d, st)
        with nc.allow_non_contiguous_dma(reason="small strided constant loads"):
            qs.dma(k.ident_f[:], k.c_ident, writes=[dc])
            qs.dma(k.masku_f[:], k.c_masku, writes=[dc])
            qs.dma(k.maskl_f[:], k.c_maskl, writes=[dc])
            qs.dma(k.e16[:], k.c_e16, writes=[dc])
            qs.dma(k.esT[:], k.a_sink, writes=[dc])
            qs.dma(k.fgT[:], k.final_g.rearrange("(c p) -> p c", p=128), writes=[dc])
            qs.dma(k.ghT[:], k.m_gh.rearrange("(c p) -> p c", p=128), writes=[dc])
            qs.dma(k.bgate[:], k.m_bg.partition_broadcast(128), writes=[dc])
            ngT = sb("ngT", [128, 2, 3, 8], F32)
            for l in range(2):
                for j in range(3):
                    qs.dma(ngT[:, l, j, :], k.norm_g[l, j].rearrange("(c p) -> p c", p=128), writes=[dc])
            sT = sb("sT", [128, 8, 2], F32)
            for c in range(2):
                qs.dma(sT[:, :, c], k.cvec[c].rearrange("(k p) -> p k", p=128), writes=[dc])
            bmT = sb("bmT", [128, 2, 72], F32)
            for l in range(2):
                qs.dma(bmT[:, l, :], k.b_mod[l].rearrange("(j p) -> p j", p=128), writes=[dc])
        pool.op(lambda: nc.gpsimd.memset(k.ones_b[:], 1.0), writes=[dc])
        pool.op(lambda: nc.gpsimd.memset(k.ones_f[:], 1.0), writes=[dc])
        act.op(lambda: nc.scalar.copy(out=k.ident_b[:], in_=k.ident_f[:]), reads=[dc], writes=[dc])
        for i in range(1):
            act.op(lambda: nc.scalar.copy(out=k.masku4[:, i, :], in_=k.masku_f[:]), reads=[dc], writes=[dc])
            act.op(lambda: nc.scalar.copy(out=k.maskl4[:, i, :], in_=k.maskl_f[:]), reads=[dc], writes=[dc])
            dve.op(lambda: nc.vector.tensor_scalar(out=k.negu4[:, i, :], in0=k.masku_f[:], scalar1=-1.0, scalar2=30000.0, op0=ALU.add, op1=ALU.mult),
                   reads=[dc], writes=[dc])
            dve.op(lambda: nc.vector.tensor_scalar(out=k.negl4[:, i, :], in0=k.maskl_f[:], scalar1=-1.0, scalar2=30000.0, op0=ALU.add, op1=ALU.mult),
                   reads=[dc], writes=[dc])
        act.op(lambda: nc.scalar.activation(out=k.esT[:], in_=k.esT[:], func=AF.Exp), reads=[dc], writes=[dc])
        sTb = sb("sTb", [128, 8, 2], BF16)
        act.op(lambda: nc.scalar.activation(out=sTb[:], in_=sT[:], func=AF.Silu), reads=[dc], writes=[dc])

        stg = [sb("stg0", [128, 1024], F32), sb("stg1", [128, 1024], F32)]
        dstg = [Dep(), Dep()]
        xt = [sb("p0xt0", [128, 8, 512], F32), sb("p0xt1", [128, 8, 512], F32)]
        dxt = [Dep(), Dep()]
        n = 0
        for tt in range(NTILE):
            xs = tt % 2
            for tb in range(4):
                s = n % 2
                n += 1
                t0 = tt * TT + tb * 128
                qs.dma(stg[s][:], k.x_in[t0:t0 + 128, :], writes=[dstg[s]])
                for hb in range(2):
                    pi = 4 + 2 * (tb % 2) + hb
                    pv = k.ps[pi][:].rearrange("p (a b) -> p a b", a=4)
                    for c4 in range(4):
                        c = hb * 4 + c4
                        pe.op(lambda: nc.tensor.transpose(pv[:, c4, :], stg[s][:, c * 128:(c + 1) * 128], k.ident_f[:]),
                              reads=[dstg[s], dc], writes=[k.dps[pi]])
                    eng = act if hb == 0 else dve
                    if hb == 0:
                        act.op(lambda: nc.scalar.copy(out=xt[xs][:, 0:4, tb * 128:(tb + 1) * 128], in_=pv),
                               reads=[k.dps[pi]], writes=[dxt[xs]])
                    else:
                        dve.op(lambda: nc.vector.tensor_copy(out=xt[xs][:, 4:8, tb * 128:(tb + 1) * 128], in_=pv),
                               reads=[k.dps[pi]], writes=[dxt[xs]])
            qs.dma(_xs_tile(k, tt), xt[xs][:], reads=[dxt[xs]], writes=[k.d_xs[tt]])
        modtok = sb("modtok", [2, 4608], F32)
        d_mt = Dep()
        NB = 1152
        wv = [k.W[1][:, i * 9216:(i + 1) * 9216].rearrange("p (k n) -> p k n", k=8) for i in range(3)]
        dwv = [Dep() for _ in range(3)]
        tmpA = sb("tmpA", [128, 8, 2], F32)
        d_tmpA = Dep()
        bi = 0
        for l in range(2):
            src = k.w_mod[l].rearrange("(k p) n -> p k n", p=128)
            for hh in range(2):
                for b4 in range(4):
                    b = hh * 4 + b4
                    s = bi % 3
                    bi += 1
                    qg.dma(wv[s], src[:, :, b * NB:(b + 1) * NB], writes=[dwv[s]])
                    if bi == 3:
                        _load_ffn_weights(k, 0, 0, 0, 0)
                    for cg in range(3):
                        pi = 1 + (cg % 2)
                        for kk in range(8):
                            pe.op(lambda: nc.tensor.matmul(k.ps[pi][0:2, 0:384], sTb[:, kk, :], wv[s][:, kk, cg * 384:(cg + 1) * 384],
                                                           start=(kk == 0), stop=(kk == 7)),
                                  reads=[dc, dwv[s]], writes=[k.dps[pi]], inc=(kk == 7))
                        c0 = b4 * NB + cg * 384
                        dve.op(lambda: nc.vector.tensor_copy(out=modtok[0:2, c0:c0 + 384], in_=k.ps[pi][0:2, 0:384]),
                               reads=[k.dps[pi]], writes=[d_mt])
                pT = k.ps[3][:, 0:72].rearrange("p (j c) -> p j c", c=2)
                for j in range(36):
                    pe.op(lambda: nc.tensor.matmul(pT[:, j, :], modtok[0:2, j * 128:(j + 1) * 128], k.ident_f[0:2, 0:2],
                                                   start=True, stop=True),
                          reads=[d_mt, dc], writes=[k.dps[3]])
                dve.op(lambda: nc.vector.tensor_tensor(out=k.modT[l][:, hh * 36:(hh + 1) * 36, :], in0=pT,
                                                       in1=bmT[:, l, hh * 36:(hh + 1) * 36].unsqueeze(2).to_broadcast([128, 36, 2]),
                                                       op=ALU.add),
                       reads=[k.dps[3], dc], writes=[dc])
            for j in range(3):
                dve.op(lambda: nc.vector.tensor_scalar(out=tmpA[:], in0=k.modT[l][:, (3 * j + 1) * 8:(3 * j + 2) * 8, :],
                                                       scalar1=1.0, scalar2=None, op0=ALU.add),
                       reads=[dc], writes=[d_tmpA])
                dve.op(lambda: nc.vector.tensor_tensor(out=k.A[l][:, j], in0=tmpA[:],
                                                       in1=ngT[:, l, j, :].unsqueeze(2).to_broadcast([128, 8, 2]), op=ALU.mult),
                       reads=[d_tmpA, dc], writes=[dc])
                dve.op(lambda: nc.vector.tensor_scalar(out=k.G[l][:, j], in0=k.modT[l][:, (3 * j + 2) * 8:(3 * j + 3) * 8, :],
                                                       scalar1=(1.0 if j == 1 else 0.5), scalar2=None, op0=ALU.mult),
                       reads=[dc], writes=[dc])

        _barrier(k)


def _norm_stat(k, B, xt, dxt, c):
    nc = k.nc
    s = c % 2
    k.act.op(lambda: nc.scalar.activation(out=B.sqc[s][:], in_=xt[:, c, :], func=AF.Square),
             reads=[dxt[c]], writes=[B.dsq[s]])
    k.pe.op(lambda: nc.tensor.matmul(k.ps[0][:], k.ones_b[:], B.sqc[s][:], start=(c == 0), stop=(c == 7)),
            reads=[B.dsq[s], k.d_const], writes=[k.dps[0]])


def _norm_rstd(k, B):
    nc = k.nc
    k.act.op(lambda: nc.scalar.activation(out=B.rstd[:], in_=k.ps[0][:], func=AF.Sqrt, bias=B.epsc[:, 0:1], scale=1.0 / 1024),
             reads=[k.dps[0]], writes=[B.drstd])
    k.dve.op(lambda: nc.vector.reciprocal(out=B.rstd[:], in_=B.rstd[:]), reads=[B.drstd], writes=[B.drstd])


def _norm_mod(k, B, xt, dxt, l, j, cond, hT, dhT, c):
    nc = k.nc
    s = c % 2
    k.dve.op(lambda: nc.vector.scalar_tensor_tensor(out=B.tmp[s][:], in0=xt[:, c, :],
                                                    scalar=k.A[l][:, j, c, cond:cond + 1], in1=B.rstd[:],
                                                    op0=ALU.mult, op1=ALU.mult),
             reads=[dxt[c], B.drstd, k.d_const], writes=[B.dtmp[s]])
    k.act.op(lambda: nc.scalar.activation(out=hT[:, c, :], in_=B.tmp[s][:], func=AF.Identity,
                                          bias=k.modT[l][:, 3 * j * 8 + c, cond:cond + 1], scale=1.0),
             reads=[B.dtmp[s], k.d_const], writes=[dhT[c]])


def _modnorm(k, B, xt, dxt, l, j, cond, hT, dhT):
    for c in range(8):
        _norm_stat(k, B, xt, dxt, c)
    _norm_rstd(k, B)
    if hT is None:
        return
    for c in range(8):
        _norm_mod(k, B, xt, dxt, l, j, cond, hT, dhT, c)


def _emit_units_pipelined(k, B, units, tt, xt1, dxt1, l, j, hTs, dhTs):
    n = len(units)
    pipe = tt + 1 < NTILE
    if pipe:
        k.qs.dma(xt1[:], _xs_tile(k, tt + 1), reads=[k.d_xs[tt + 1]], writes=dxt1)
    s0 = max(0, n - 20)
    ncond = 0 if tt + 1 < 8 else 1
    for gi, u in enumerate(units):
        u()
        if pipe:
            if s0 <= gi < s0 + 8:
                _norm_stat(k, B, xt1, dxt1, gi - s0)
            if gi == s0 + 7:
                _norm_rstd(k, B)
            if s0 + 8 <= gi < s0 + 16:
                _norm_mod(k, B, xt1, dxt1, l, j, ncond, hTs[(tt + 1) % 2], dhTs[(tt + 1) % 2], gi - s0 - 8)
    assert n >= s0 + 16


def _norm_bufs(k, st):
    B = K()
    B.sqc = [k.sb("sqc0", [128, TT], BF16, st), k.sb("sqc1", [128, TT], BF16, st)]
    B.dsq = [Dep(), Dep()]
    B.rstd = k.sb("rstd", [128, TT], F32, st)
    B.drstd = Dep()
    B.tmp = [k.sb("ntmp0", [128, TT], F32, st), k.sb("ntmp1", [128, TT], F32, st)]
    B.dtmp = [Dep(), Dep()]
    B.epsc = k.sb("epsc", [128, 1], F32, st)
    k.pool.op(lambda: k.nc.gpsimd.memset(B.epsc[:], EPS), writes=[B.drstd])
    return B


def _ffn_run(k, segs):
    nc = k.nc
    pe, act, dve, pool, qs = k.pe, k.act, k.dve, k.pool, k.qs
    tiles = []
    for si, sg_ in enumerate(segs):
        for tt in range(NTILE):
            tiles.append((si, tt))
    with ExitStack() as st:
        B = _norm_bufs(k, st)
        xt = [k.sb("xt0", [128, 8, TT], F32, st), k.sb("xt1", [128, 8, TT], F32, st)]
        dxt = [[Dep() for _ in range(8)] for _ in range(2)]
        hTs = [k.sb("hT0", [128, 8, TT], BF16, st), k.sb("hT1", [128, 8, TT], BF16, st)]
        dhTs = [[Dep() for _ in range(8)] for _ in range(2)]
        actT = k.sb("actT", [128, 11, TT], BF16, st)
        dact = [Dep() for _ in range(11)]
        sg = k.sb("sg", [128, TT], F32, st)
        dsg = Dep()
        dyv = [Dep(), Dep()]

        def hview(tt):
            return k.hT_s[:, :, tt * TT:(tt + 1) * TT].rearrange("c p t -> p c t")

        def cond_of(tt):
            return 0 if tt < 8 else 1

        def par(g):
            si, tt = tiles[g]
            slot, l, which, half, final, loader = segs[si]
            return slot, l, (0 if which == 0 else 2), half, final, tt

        slot, l, j, half, final, tt = par(0)
        qs.dma(xt[0][:], _xs_tile(k, tt), reads=[k.d_xs[tt]], writes=dxt[0])
        if half == 0:
            _modnorm(k, B, xt[0], dxt[0], l, j, cond_of(tt), hTs[0], dhTs[0])
            qs.dma(hview(tt), hTs[0][:], reads=dhTs[0], writes=[k.d_hs[tt]])
        else:
            qs.dma(hTs[0][:], hview(tt), reads=[k.d_hs[tt]], writes=dhTs[0])
        fin_ops = []
        for g in range(len(tiles)):
            slot, l, j, half, final, tt = par(g)
            pend_fin = fin_ops
            fin_ops = []
            si = tiles[g][0]
            if tt == 0 and segs[si][5] is not None:
                segs[si][5](k)
            wgu, wdn = _wslot_views_ffn(k, slot)
            dW = k.dW[slot]
            xs = g % 2
            cond = cond_of(tt)
            X, dX = xt[xs], dxt[xs]
            hT, dhT = hTs[xs], dhTs[xs]
            nxt = g + 1 < len(tiles)
            pipe = False
            defer_x = bool(pend_fin)
            if nxt:
                nslot, nl, nj, nhalf, nfinal, ntt = par(g + 1)
                if not defer_x:
                    qs.dma(xt[1 - xs][:], _xs_tile(k, ntt), reads=[k.d_xs[ntt]], writes=dxt[1 - xs])
                if nhalf == 1:
                    qs.dma(hTs[1 - xs][:], hview(ntt), reads=[k.d_hs[ntt]], writes=dhTs[1 - xs])
                pipe = nhalf == 0
            for fc in range(11):
                pg = 1 + 2 * (fc % 2)
                pu = pg + 1
                for kk in range(8):
                    pe.op(lambda: nc.tensor.matmul(k.ps[pg], wgu[:, kk, fc * 128:(fc + 1) * 128], hT[:, kk, :],
                                                   start=(kk == 0), stop=(kk == 7)),
                          reads=[dW, dhT[kk]], writes=[k.dps[pg]], inc=(kk == 7))
                for kk in range(8):
                    pe.op(lambda: nc.tensor.matmul(k.ps[pu], wgu[:, kk, 1408 + fc * 128:1408 + (fc + 1) * 128], hT[:, kk, :],
                                                   start=(kk == 0), stop=(kk == 7)),
                          reads=[dW, dhT[kk]], writes=[k.dps[pu]], inc=(kk == 7))
                act.op(lambda: nc.scalar.activation(out=sg[:], in_=k.ps[pg], func=AF.Silu),
                       reads=[k.dps[pg]], writes=[dsg])
                dve.op(lambda: nc.vector.tensor_tensor(out=actT[:, fc, :], in0=k.ps[pu], in1=sg[:], op=ALU.mult),
                       reads=[k.dps[pu], dsg], writes=[dact[fc]])
                if pipe and fc >= 7:
                    for c in (2 * (fc - 7), 2 * (fc - 7) + 1):
                        _norm_stat(k, B, xt[1 - xs], dxt[1 - xs], c)
                if pend_fin:
                    pend_fin.pop(0)()
            if pipe:
                _norm_rstd(k, B)
            for dc in range(8):
                po = 5 + (dc % 2)
                for fc in range(11):
                    pe.op(lambda: nc.tensor.matmul(k.ps[po], wdn[:, fc, dc * 128:(dc + 1) * 128], actT[:, fc, :],
                                                   start=(fc == 0), stop=(fc == 10)),
                          reads=[dW, dact[fc]], writes=[k.dps[po]], inc=(fc == 10))
                dve.op(lambda: nc.vector.scalar_tensor_tensor(out=X[:, dc, :], in0=k.ps[po],
                                                              scalar=k.G[l][:, j, dc, cond:cond + 1], in1=X[:, dc, :],
                                                              op0=ALU.mult, op1=ALU.add),
                       reads=[k.dps[po], dX[dc], k.d_const], writes=[dX[dc]])
                if pipe:
                    _norm_mod(k, B, xt[1 - xs], dxt[1 - xs], nl, nj, cond_of(ntt), hTs[1 - xs], dhTs[1 - xs], dc)
                if pend_fin:
                    pend_fin.pop(0)()
            if pipe:
                qs.dma(hview(ntt), hTs[1 - xs][:], reads=dhTs[1 - xs], writes=[k.d_hs[ntt]])
            while pend_fin:
                pend_fin.pop(0)()
            if nxt and defer_x:
                qs.dma(xt[1 - xs][:], _xs_tile(k, ntt), reads=[k.d_xs[ntt]], writes=dxt[1 - xs])
            if not final:
                qs.dma(_xs_tile(k, tt), X[:], reads=dX, writes=[k.d_xs[tt]])
            else:
                fin_ops = _final_ops(k, B, X, dX, slot, tt, dyv)
        for f in fin_ops:
            f()


def _final_ops(k, B, X, dX, slot, tt, dyv):
    nc = k.nc
    pe, act, dve, qs = k.pe, k.act, k.dve, k.qs
    yv = k.W[1 - slot][:, 0:4096].bitcast(F32).rearrange("p (a n) -> p a n", a=2)
    ops = []
    for c2 in range(4):
        def st_(c2=c2):
            _norm_stat(k, B, X, dX, 2 * c2)
            _norm_stat(k, B, X, dX, 2 * c2 + 1)
        ops.append(st_)
    ops.append(lambda: _norm_rstd(k, B))
    for c2 in range(4):
        def sc_(c2=c2):
            for c in (2 * c2, 2 * c2 + 1):
                dve.op(lambda: nc.vector.scalar_tensor_tensor(out=X[:, c, :], in0=X[:, c, :], scalar=k.fgT[:, c:c + 1],
                                                              in1=B.rstd[:], op0=ALU.mult, op1=ALU.mult),
                       reads=[dX[c], B.drstd, k.d_const], writes=[dX[c]])
        ops.append(sc_)
    for tb in range(4):
        for hb in range(2):
            def tr_(tb=tb, hb=hb):
                ys = tb % 2
                pi = 7 if hb == 0 else 0
                pv = k.ps[pi].rearrange("p (a b) -> p a b", a=4)
                for c4 in range(4):
                    c = hb * 4 + c4
                    pe.op(lambda: nc.tensor.transpose(pv[:, c4, :], X[:, c, tb * 128:(tb + 1) * 128], k.ident_f[:]),
                          reads=[dX[c], k.d_const], writes=[k.dps[pi]])
                if hb == 0:
                    act.op(lambda: nc.scalar.copy(out=yv[:, ys, 0:512], in_=k.ps[pi]), reads=[k.dps[pi]], writes=[dyv[ys], k.dW[1 - slot]])
                else:
                    dve.op(lambda: nc.vector.tensor_copy(out=yv[:, ys, 512:1024], in_=k.ps[pi]), reads=[k.dps[pi]], writes=[dyv[ys], k.dW[1 - slot]])
                    t0 = tt * TT + tb * 128
                    qs.dma(k.y_out[t0:t0 + 128, :], yv[:, ys, :], reads=[dyv[ys]], writes=[k.d_out])
            ops.append(tr_)
    return ops


def _mlstm_phase(k, s):
    nc = k.nc
    pe, act, dve, pool, qs = k.pe, k.act, k.dve, k.pool, k.qs
    W = k.W[s]
    win = W[:, 0:24832].rearrange("p (k n) -> p k n", k=8)
    wout = W[:, 24832:24832 + 8192].rearrange("p (k n) -> p k n", k=8)
    dW = k.dW[s]
    dc_ = k.d_const
    l, j = 0, 1
    with ExitStack() as st0:
        gates = k.sb("gates", [128, NCH, 32], F32, st0)
        d_gates = Dep()
        cs = k.sb("cs", [128, NCH, 16], F32, st0)
        ecl = k.sb("ecl", [128, NCH, 16], F32, st0)
        gsel = k.sb("gsel", [128, NCH, 2, 4], F32, st0)
        cs_bf = k.sb("cs_bf", [128, NCH, 16], BF16, st0)
        d_cs, d_ecl, d_gsel = Dep(), Dep(), Dep()

        with ExitStack() as st:
            B = _norm_bufs(k, st)
            xt1 = k.sb("xtP", [128, 8, TT], F32, st)
            dxt1 = [Dep() for _ in range(8)]
            hTs = [k.sb("hT0", [128, 8, TT], BF16, st), k.sb("hT1", [128, 8, TT], BF16, st)]
            dhTs = [[Dep() for _ in range(8)] for _ in range(2)]
            stage = [k.sb("stage%d" % i, [128, 4, TT], BF16, st) for i in range(2)]
            dstage = [Dep() for _ in range(2)]
            nst = [0]
            npb = [0]
            nev = [0]

            def next_stage():
                i = nst[0] % 2
                nst[0] += 1
                return stage[i], dstage[i]

            def next_bank():
                i = 1 + npb[0] % 4
                npb[0] += 1
                return i

            def evac(out, pi, func=None, scale=1.0, wdep=None):
                if func is not None or nev[0] % 2 == 0:
                    f = func if func is not None else AF.Copy
                    act.op(lambda: nc.scalar.activation(out=out, in_=k.ps[pi], func=f, scale=scale),
                           reads=[k.dps[pi]], writes=[wdep])
                else:
                    dve.op(lambda: nc.vector.tensor_copy(out=out, in_=k.ps[pi]), reads=[k.dps[pi]], writes=[wdep])
                nev[0] += 1

            def units_for(tt, hT, dhT):
                t0 = tt * TT
                units = []

                def fmaj(col0, nch, func, scale, dst, ddst):
                    for g4 in range(nch // 4):
                        holder = {}
                        for f4 in range(4):
                            def u(g4=g4, f4=f4, holder=holder):
                                if f4 == 0:
                                    holder["s"] = next_stage()
                                sg_, dsg_ = holder["s"]
                                fc = g4 * 4 + f4
                                pi = next_bank()
                                for kk in range(8):
                                    pe.op(lambda: nc.tensor.matmul(k.ps[pi], win[:, kk, col0 + fc * 128:col0 + (fc + 1) * 128], hT[:, kk, :],
                                                                   start=(kk == 0), stop=(kk == 7)),
                                          reads=[dW, dhT[kk]], writes=[k.dps[pi]], inc=(kk == 7))
                                evac(sg_[:, f4, :], pi, func, scale, dsg_)
                                if f4 == 3:
                                    qs.dma(dst[g4 * 4:(g4 + 1) * 4, :, t0:t0 + TT].rearrange("c p t -> p c t"), sg_[:], reads=[dsg_], writes=[ddst[tt]])
                            units.append(u)

                def tmaj(col0, dst, dcol0, ddst):
                    holder = {}
                    for tb in range(4):
                        def u(tb=tb, holder=holder):
                            if tb == 0:
                                holder["s"] = next_stage()
                            sg_, dsg_ = holder["s"]
                            pi = next_bank()
                            for kk in range(8):
                                pe.op(lambda: nc.tensor.matmul(k.ps[pi], hT[:, kk, tb * 128:(tb + 1) * 128], win[:, kk, col0:col0 + 512],
                                                               start=(kk == 0), stop=(kk == 7)),
                                      reads=[dW, dhT[kk]], writes=[k.dps[pi]], inc=(kk == 7))
                            evac(sg_[:, tb, :], pi, None, 1.0, dsg_)
                            if tb == 3:
                                qs.dma(dst[t0:t0 + TT, dcol0:dcol0 + 512].rearrange("(b t) f -> t b f", t=128), sg_[:], reads=[dsg_], writes=[ddst[tt]])
                        units.append(u)

                fmaj(0, 4, AF.Copy, 0.125, k.qT_s, k.d_q)
                fmaj(512, 4, None, 1.0, k.kT_s, k.d_k)
                fmaj(2048, 8, AF.Sigmoid, 1.0, k.oT_s, k.d_o)
                tmaj(512, k.kt_s, 0, k.d_kt)
                tmaj(1024, k.vt_s, 0, k.d_vt)
                tmaj(1536, k.vt_s, 512, k.d_vt)
                for tb in range(4):
                    def u(tb=tb):
                        pgt = k.ps[5][:, tb * 32:(tb + 1) * 32]
                        for kk in range(8):
                            pe.op(lambda: nc.tensor.matmul(pgt, hT[:, kk, tb * 128:(tb + 1) * 128], win[:, kk, 3072:3104],
                                                           start=(kk == 0), stop=(kk == 7)),
                                  reads=[dW, dhT[kk]], writes=[k.dps[5]], inc=(kk == 7))
                        dve.op(lambda: nc.vector.tensor_tensor(out=gates[:, tt * 4 + tb, :], in0=pgt, in1=k.bgate[:], op=ALU.add),
                               reads=[k.dps[5], dc_], writes=[d_gates])
                    units.append(u)
                return units

            qs.dma(xt1[:], _xs_tile(k, 0), reads=[k.d_xs[0]], writes=dxt1)
            _modnorm(k, B, xt1, dxt1, l, j, 0, hTs[0], dhTs[0])
            for tt in range(NTILE):
                units = units_for(tt, hTs[tt % 2], dhTs[tt % 2])
                _emit_units_pipelined(k, B, units, tt, xt1, dxt1, l, j, hTs, dhTs)
        _barrier(k)
        if getattr(k, "dbg_stop", "") == "P":
            return

        SEQS = [(list(range(0, 32)), 46, None)] + [([32 + 2 * i, 33 + 2 * i], 47, i) for i in range(4)]
        with ExitStack() as st:
            nlf = k.sb("nlf", [128, NCH, 64], F32, st)
            apad = k.sb("apad", [128, NCH, 64], F32, st)
            nbS = k.sb("nbS", [128, NCH, 16], F32, st)
            pre = k.sb("pre", [128, NCH, 16], F32, st)
            pre2 = k.sb("pre2", [128, NCH, 16], F32, st)
            amaxT = k.sb("amaxT", [64, NCH], F32, st)
            totS = k.sb("totS", [64, NCH], F32, st)
            MT = k.sb("MT", [64, NCH], F32, st)
            dG = k.sb("dG", [64, NCH], F32, st)
            Gx = k.sb("Gx", [64, NCH], F32, st)
            mst = [k.sb("mstF", [64, 48], F32, st), k.sb("mstB", [64, 48], F32, st)]
            RM = k.sb("RM", [64, NCH, 16], F32, st)
            RG = k.sb("RG", [64, NCH, 16], F32, st)
            d_nlf, d_apad, d_nbS, d_pre, d_pre2, d_amax, d_tot, d_RM, d_RG, d_Gx = [Dep() for _ in range(10)]
            d_ch = [Dep(), Dep()]
            pool.op(lambda: nc.gpsimd.memset(nlf[:], 0.0), writes=[d_nlf])
            pool.op(lambda: nc.gpsimd.memset(apad[:], 0.0), writes=[d_apad])
            pool.op(lambda: nc.gpsimd.memset(MT[:], 0.0), writes=[d_ch[0], d_ch[1]])
            pool.op(lambda: nc.gpsimd.memset(dG[:], 0.0), writes=[d_ch[0], d_ch[1]])
            for d in range(2):
                pool.op(lambda: nc.gpsimd.memset(mst[d][:], 0.0), writes=[d_ch[d]])
            with nc.allow_non_contiguous_dma(reason="tiny state loads"):
                for d in range(2):
                    qs.dma(mst[d][32 * d:32 * d + 8, 46:47], k.st_m[d].rearrange("(h o) -> h o", o=1), writes=[d_ch[d]])
            for d in range(2):
                act.op(lambda: nc.scalar.activation(out=nlf[:, :, 32 * d:32 * d + 8], in_=gates[:, :, 16 * d + 8:16 * d + 16],
                                                    func=AF.Exp, scale=-1.0), reads=[d_gates, d_nlf], writes=[d_nlf])
            for d in range(2):
                act.op(lambda: nc.scalar.activation(out=nlf[:, :, 32 * d:32 * d + 8], in_=nlf[:, :, 32 * d:32 * d + 8],
                                                    func=AF.Ln, bias=1.0, scale=1.0), reads=[d_nlf], writes=[d_nlf])
            if getattr(k, "dbg_stop", "") == "G1":
                return
            pb = [k.ps[1][:, 0:320].rearrange("p (c h) -> p c h", h=8), k.ps[2][:, 0:320].rearrange("p (c h) -> p c h", h=8)]
            pe.op(lambda: nc.tensor.matmul(pb[0], k.masku_f[:], nlf[:, :, 0:8], start=True, stop=True),
                  reads=[d_nlf, dc_], writes=[k.dps[1]])
            if getattr(k, "dbg_stop", "") == "G2a":
                return
            pe.op(lambda: nc.tensor.matmul(pb[1], k.maskl_f[:], nlf[:, :, 32:40], start=True, stop=True),
                  reads=[d_nlf, dc_], writes=[k.dps[2]])
            if getattr(k, "dbg_stop", "") == "G2b":
                return
            for d in range(2):
                dve.op(lambda: nc.vector.tensor_tensor(out=apad[:, :, 32 * d:32 * d + 8], in0=pb[d], in1=gates[:, :, 16 * d:16 * d + 8], op=ALU.add),
                       reads=[k.dps[1 + d], d_gates, d_apad], writes=[d_apad])
                if getattr(k, "dbg_stop", "") == "G2c":
                    return
                dve.op(lambda: nc.vector.tensor_copy(out=nbS[:, :, 8 * d:8 * d + 8], in_=pb[d]), reads=[k.dps[1 + d]], writes=[d_nbS])
            if getattr(k, "dbg_stop", "") == "G2":
                return
            for g in range(NCH // 4):
                pi = 3 + g % 2
                pv = k.ps[pi][:].rearrange("p (a b) -> p a b", a=4)
                for i in range(4):
                    c = 4 * g + i
                    pe.op(lambda: nc.tensor.transpose(pv[0:64, i, :], apad[:, c, :], k.ident_f[:]), reads=[d_apad, dc_], writes=[k.dps[pi]])
                dve.op(lambda: nc.vector.reduce_max(out=amaxT[:, 4 * g:4 * g + 4], in_=pv[0:64, :, :], axis=AX.X),
                       reads=[k.dps[pi]], writes=[d_amax])
            if getattr(k, "dbg_stop", "") == "G3":
                return
            for c in range(NCH):
                pe.op(lambda: nc.tensor.matmul(k.ps[5][0:64, c:c + 1], nlf[:, c, :], k.ones_f[:, 0:1], start=True, stop=True),
                      reads=[d_nlf, dc_], writes=[k.dps[5]])
            act.op(lambda: nc.scalar.copy(out=totS[:], in_=k.ps[5][0:64, 0:NCH]), reads=[k.dps[5]], writes=[d_tot])
            if getattr(k, "dbg_stop", "") == "G4":
                return
            colp = [0, 0]
            for (chunks, init_col, pseq) in SEQS:
                for d in range(2):
                    eng = dve
                    E = nc.vector
                    r = slice(32 * d, 32 * d + 8)
                    order = chunks if d == 0 else chunks[::-1]
                    cur = init_col
                    for c in order:
                        eng.op(lambda: E.tensor_tensor(out=MT[r, c:c + 1], in0=mst[d][r, cur:cur + 1], in1=amaxT[r, c:c + 1], op=ALU.max),
                               reads=[d_ch[d], d_amax], writes=[d_ch[d]])
                        eng.op(lambda: E.tensor_tensor(out=dG[r, c:c + 1], in0=mst[d][r, cur:cur + 1], in1=MT[r, c:c + 1], op=ALU.subtract),
                               reads=[d_ch[d]], writes=[d_ch[d]])
                        nxt = colp[d]
                        colp[d] += 1
                        eng.op(lambda: E.tensor_tensor(out=mst[d][r, nxt:nxt + 1], in0=MT[r, c:c + 1], in1=totS[r, c:c + 1], op=ALU.subtract),
                               reads=[d_ch[d], d_tot], writes=[d_ch[d]])
                        cur = nxt
                    if pseq is not None:
                        with nc.allow_non_contiguous_dma(reason="tiny state store"):
                            qs.dma(k.nsm[pseq, d, :].rearrange("(h o) -> h o", o=1), mst[d][r, cur:cur + 1], reads=[d_ch[d]], writes=[k.d_out])
            if getattr(k, "dbg_stop", "") == "G5":
                return
            act.op(lambda: nc.scalar.activation(out=Gx[:], in_=dG[:], func=AF.Exp), reads=[d_ch[0], d_ch[1]], writes=[d_Gx])
            e16b = k.e16[:].unsqueeze(1).to_broadcast([64, NCH, 16])
            dve.op(lambda: nc.vector.tensor_tensor(out=RM[:], in0=e16b, in1=MT[:].unsqueeze(2).to_broadcast([64, NCH, 16]), op=ALU.mult),
                   reads=[d_ch[0], d_ch[1], dc_], writes=[d_RM])
            dve.op(lambda: nc.vector.tensor_tensor(out=RG[:], in0=e16b, in1=Gx[:].unsqueeze(2).to_broadcast([64, NCH, 16]), op=ALU.mult),
                   reads=[d_Gx, dc_], writes=[d_RG])
            if getattr(k, "dbg_stop", "") == "G6":
                return
            Mv, Gv = [], []
            for hh in range(2):
                pm = k.ps[1 + hh][:, 0:320].rearrange("p (c h) -> p c h", h=16)
                pg = k.ps[3 + hh][:, 0:320].rearrange("p (c h) -> p c h", h=16)
                pe.op(lambda: nc.tensor.matmul(pm, k.ones_f[0:64, :], RM[:, 20 * hh:20 * hh + 20, :], start=True, stop=True),
                      reads=[d_RM, dc_], writes=[k.dps[1 + hh]])
                pe.op(lambda: nc.tensor.matmul(pg, k.ones_f[0:64, :], RG[:, 20 * hh:20 * hh + 20, :], start=True, stop=True),
                      reads=[d_RG, dc_], writes=[k.dps[3 + hh]])
                Mv.append(pm)
                Gv.append(pg)
            if getattr(k, "dbg_stop", "") == "G7":
                return
            for hh in range(2):
                cr = slice(20 * hh, 20 * hh + 20)
                for d in range(2):
                    dve.op(lambda: nc.vector.tensor_tensor(out=pre[:, cr, 8 * d:8 * d + 8], in0=apad[:, cr, 32 * d:32 * d + 8],
                                                           in1=Mv[hh][:, :, 8 * d:8 * d + 8], op=ALU.subtract),
                           reads=[d_apad, k.dps[1 + hh]], writes=[d_pre])
                dve.op(lambda: nc.vector.tensor_tensor(out=pre2[:, cr, :], in0=nbS[:, cr, :], in1=Mv[hh], op=ALU.subtract),
                       reads=[d_nbS, k.dps[1 + hh]], writes=[d_pre2])
                gv5 = k.ps[3 + hh][:, 0:320].rearrange("p (c d h t) -> p c d h t", d=2, h=4, t=2)
                dve.op(lambda: nc.vector.tensor_copy(out=gsel[0:64, cr, :, :], in_=gv5[0:64, :, :, :, 0]), reads=[k.dps[3 + hh]], writes=[d_gsel])
                dve.op(lambda: nc.vector.tensor_copy(out=gsel[64:128, cr, :, :], in_=gv5[64:128, :, :, :, 1]), reads=[k.dps[3 + hh]], writes=[d_gsel])
            if getattr(k, "dbg_stop", "") == "G8":
                return
            if getattr(k, "dbg_stop", "") == "G9":
                _barrier(k)
            act.op(lambda: nc.scalar.activation(out=cs[:], in_=pre[:], func=AF.Exp), reads=[d_pre], writes=[d_cs])
            if getattr(k, "dbg_stop", "") == "G10":
                return
            act.op(lambda: nc.scalar.activation(out=ecl[:], in_=pre2[:], func=AF.Exp), reads=[d_pre2], writes=[d_ecl])
            dve.op(lambda: nc.vector.tensor_copy(out=cs_bf[:], in_=cs[:]), reads=[d_cs], writes=[d_cs])
        _barrier(k)
        if getattr(k, "dbg_stop", "") == "G":
            return

        with ExitStack() as st:
            sbl = lambda n, sh, dt: k.sb(n, sh, dt, st)
            qTc = [[sbl("qTc%d%d" % (d, i), [128, 4, 128], BF16) for i in range(2)] for d in range(2)]
            kTc = [[sbl("kTc%d%d" % (d, i), [128, 4, 128], BF16) for i in range(2)] for d in range(2)]
            ktc = [[sbl("ktc%d%d" % (d, i), [128, 8, 64], BF16) for i in range(2)] for d in range(2)]
            vtc = [[sbl("vtc%d%d" % (d, i), [128, 8, 128], BF16) for i in range(2)] for d in range(2)]
            d_ld = [[Dep(), Dep()] for d in range(2)]
            vp = [sbl("vp%d" % d, [128, 8, 128], BF16) for d in range(2)]
            d_vp = [Dep(), Dep()]
            C = [sbl("C%d" % d, [128, 4, 129], F32) for d in range(2)]
            Cg = [sbl("Cg%d" % d, [128, 4, 129], F32) for d in range(2)]
            Cgb = [sbl("Cgb%d" % d, [128, 4, 129], BF16) for d in range(2)]
            d_C, d_Cg, d_Cgb = [Dep(), Dep()], [Dep(), Dep()], [Dep(), Dep()]
            PT = [sbl("PT%d" % d, [128, 4, 2, 128], BF16) for d in range(2)]
            d_PT = [[Dep(), Dep()], [Dep(), Dep()]]
            ad = [sbl("ad%d" % d, [128, 8], F32) for d in range(2)]
            rr = [sbl("rr%d" % d, [128, 8], F32) for d in range(2)]
            d_ad, d_rr = [Dep(), Dep()], [Dep(), Dep()]
            hbuf = [[sbl("hbuf%d%d" % (d, i), [128, 8, 128], BF16) for i in range(2)] for d in range(2)]
            d_hbuf = [[Dep(), Dep()], [Dep(), Dep()]]
            NUM = [k.ps[4].rearrange("p (a b) -> p a b", a=4), k.ps[5].rearrange("p (a b) -> p a b", a=4)]
            DEN = k.ps[6][:, 0:8]
            UNn = k.ps[6][:, 8:12]
            UN = k.ps[7].rearrange("p (a b) -> p a b", a=4)
            d_NUM, d_DEN, d_UN = [Dep(), Dep()], Dep(), Dep()
            nld = [0, 0]

            def stageL(d, c):
                tt = c // 4
                t0 = c * 128
                s2 = nld[d] % 2
                nld[d] += 1
                qs.dma(qTc[d][s2][:], k.qT_s[0:4, :, t0:t0 + 128].rearrange("c p t -> p c t"), reads=[k.d_q[tt]], writes=[d_ld[d][s2]])
                qs.dma(kTc[d][s2][:], k.kT_s[0:4, :, t0:t0 + 128].rearrange("c p t -> p c t"), reads=[k.d_k[tt]], writes=[d_ld[d][s2]])
                qs.dma(ktc[d][s2][:], k.kt_s[t0:t0 + 128, :].rearrange("t (h e) -> t h e", h=8), reads=[k.d_kt[tt]], writes=[d_ld[d][s2]])
                qs.dma(vtc[d][s2][:], k.vt_s[t0:t0 + 128, :].rearrange("t (h e) -> t h e", h=8), reads=[k.d_vt[tt]], writes=[d_ld[d][s2]])
                return s2

            def stageA(d, c, s2):
                for h in range(8):
                    hp, par = h // 2, h % 2
                    rs_ = slice(64 * par, 64 * par + 64)
                    bank = 2 * d + par
                    STv = k.ps[bank].rearrange("p (a b) -> p a b", a=4)
                    pe.op(lambda: nc.tensor.matmul(STv[:, hp, :], kTc[d][s2][rs_, hp, :], qTc[d][s2][rs_, hp, :], start=True, stop=True),
                          reads=[d_ld[d][s2]], writes=[k.dps[bank]])

            def stageB(d, c, s2):
                mask = k.masku4 if d == 0 else k.maskl4
                dve.op(lambda: nc.vector.tensor_tensor(out=Cg[d][:], in0=C[d][:], in1=gsel[:, c, d, :].unsqueeze(2).to_broadcast([128, 4, 129]), op=ALU.mult),
                       reads=[d_C[d], d_gsel], writes=[d_Cg[d]])
                act.op(lambda: nc.scalar.copy(out=Cgb[d][:], in_=Cg[d][:]), reads=[d_Cg[d]], writes=[d_Cgb[d]])
                for par in range(2):
                    bank = 2 * d + par
                    STv = k.ps[bank].rearrange("p (a b) -> p a b", a=4)
                    dve.op(lambda: nc.vector.tensor_tensor(out=PT[d][:, :, par, :], in0=STv, in1=mask[:, 0:1, :].to_broadcast([128, 4, 128]), op=ALU.mult),
                           reads=[k.dps[bank], dc_], writes=[d_PT[d][par]])
                pool.op(lambda: nc.gpsimd.tensor_tensor(out=vp[d][:], in0=vtc[d][s2][:], in1=cs_bf[:, c, 8 * d:8 * d + 8].unsqueeze(2).to_broadcast([128, 8, 128]), op=ALU.mult),
                        reads=[d_ld[d][s2], d_cs], writes=[d_vp[d]])

            def stageC(d, c, s2):
                for h in range(8):
                    hp, par = h // 2, h % 2
                    rs_ = slice(64 * par, 64 * par + 64)
                    pth = PT[d][:, hp, par, :]
                    csc = cs_bf[:, c, 8 * d + h:8 * d + h + 1]
                    nb = h // 4
                    pe.op(lambda: nc.tensor.matmul(NUM[nb][:, h % 4, :], pth, vp[d][:, h, :], start=True, stop=False),
                          reads=[d_PT[d][par], d_vp[d]], writes=[d_NUM[nb], k.dbank[4 + nb]])
                    pe.op(lambda: nc.tensor.matmul(NUM[nb][:, h % 4, :], qTc[d][s2][rs_, hp, :], Cgb[d][rs_, hp, 0:128], start=False, stop=True),
                          reads=[d_Cgb[d], d_ld[d][s2]], writes=[d_NUM[nb], k.dbank[4 + nb]])
                    pe.op(lambda: nc.tensor.matmul(DEN[:, h:h + 1], pth, csc, start=True, stop=False),
                          reads=[d_PT[d][par], d_cs], writes=[d_DEN, k.dbank[6]])
                    pe.op(lambda: nc.tensor.matmul(DEN[:, h:h + 1], qTc[d][s2][rs_, hp, :], Cgb[d][rs_, hp, 128:129], start=False, stop=True),
                          reads=[d_Cgb[d], d_ld[d][s2]], writes=[d_DEN, k.dbank[6]])
                    pe.op(lambda: nc.tensor.matmul(UN[rs_, hp, :], ktc[d][s2][:, h, :], vp[d][:, h, :], start=True, stop=True),
                          reads=[d_vp[d], d_ld[d][s2]], writes=[d_UN, k.dbank[7]])
                    pe.op(lambda: nc.tensor.matmul(UNn[rs_, hp:hp + 1], ktc[d][s2][:, h, :], csc, start=True, stop=True),
                          reads=[d_cs, d_ld[d][s2]], writes=[d_DEN, k.dbank[6]])

            def stageD(d, c, s2):
                t0 = c * 128
                act.op(lambda: nc.scalar.activation(out=ad[d][:], in_=DEN, func=AF.Abs), reads=[d_DEN], writes=[d_ad[d], k.dbank[6]])
                dve.op(lambda: nc.vector.tensor_tensor(out=C[d][:, :, 128], in0=Cg[d][:, :, 128], in1=UNn, op=ALU.add),
                       reads=[d_Cg[d], d_DEN], writes=[d_C[d], k.dbank[6]])
                dve.op(lambda: nc.vector.tensor_tensor(out=C[d][:, :, 0:128], in0=Cg[d][:, :, 0:128], in1=UN, op=ALU.add),
                       reads=[d_Cg[d], d_UN], writes=[d_C[d], k.dbank[7]])
                dve.op(lambda: nc.vector.tensor_tensor(out=rr[d][:], in0=ad[d][:], in1=ecl[:, c, 8 * d:8 * d + 8], op=ALU.max),
                       reads=[d_ad[d], d_ecl], writes=[d_rr[d]])
                dve.op(lambda: nc.vector.reciprocal(out=rr[d][:], in_=rr[d][:]), reads=[d_rr[d]], writes=[d_rr[d]])
                hb_ = hbuf[d][s2]
                for g4 in range(2):
                    rb = rr[d][:, 4 * g4:4 * g4 + 4].unsqueeze(2).to_broadcast([128, 4, 128])
                    dve.op(lambda: nc.vector.tensor_tensor(out=hb_[:, 4 * g4:4 * g4 + 4, :], in0=NUM[g4], in1=rb, op=ALU.mult),
                           reads=[d_NUM[g4], d_rr[d]], writes=[d_hbuf[d][s2], k.dbank[4 + g4]])
                dst = k.hf_s if d == 0 else k.hb_s
                qs.dma(dst[t0:t0 + 128, :], hb_[:].rearrange("p h e -> p (h e)"), reads=[d_hbuf[d][s2]], writes=[k.d_hf[d][c]])

            for (chunks, init_col, pseq) in SEQS:
                orders = [chunks, chunks[::-1]]
                for d in range(2):
                    if pseq is None:
                        with nc.allow_non_contiguous_dma(reason="state load"):
                            cview = k.st_c[d].rearrange("(hp two) dd v -> two dd hp v", two=2)
                            nview = k.st_n[d].rearrange("(hp two) dd -> two dd hp", two=2)
                            for two in range(2):
                                qs.dma(C[d][64 * two:64 * two + 64, :, 0:128], cview[two], writes=[d_C[d]])
                                qs.dma(C[d][64 * two:64 * two + 64, :, 128], nview[two], writes=[d_C[d]])
                    else:
                        pool.op(lambda: nc.gpsimd.memset(C[d][:], 0.0), writes=[d_C[d]])
                n = len(chunks)
                slots = [[stageL(d, orders[d][0]) for d in range(2)]]
                for i in range(n):
                    if i + 1 < n:
                        slots.append([stageL(d, orders[d][i + 1]) for d in range(2)])
                    for d in range(2):
                        stageA(d, orders[d][i], slots[i][d])
                    for d in range(2):
                        stageB(d, orders[d][i], slots[i][d])
                    for d in range(2):
                        stageC(d, orders[d][i], slots[i][d])
                        stageD(d, orders[d][i], slots[i][d])
                if pseq is not None:
                    with nc.allow_non_contiguous_dma(reason="state store"):
                        for d in range(2):
                            cview = k.nsc[pseq, d].rearrange("(hp two) dd v -> two dd hp v", two=2)
                            nview = k.nsn[pseq, d].rearrange("(hp two) dd -> two dd hp", two=2)
                            for two in range(2):
                                qs.dma(cview[two], C[d][64 * two:64 * two + 64, :, 0:128], reads=[d_C[d]], writes=[k.d_out])
                                qs.dma(nview[two], C[d][64 * two:64 * two + 64, :, 128], reads=[d_C[d]], writes=[k.d_out])
        _barrier(k)

    with ExitStack() as st:
        sbl = lambda n, sh, dt: k.sb(n, sh, dt, st)
        hfc = [sbl("hfc%d" % i, [128, 8, 128], BF16) for i in range(2)]
        hbc = [sbl("hbc%d" % i, [128, 8, 128], BF16) for i in range(2)]
        hsum = [sbl("hsum%d" % i, [128, 8, 128], F32) for i in range(2)]
        oTc = [sbl("oTc%d" % i, [128, 8, 128], BF16) for i in range(2)]
        d_ld = [Dep(), Dep()]
        d_hs = [Dep(), Dep()]
        sqh = [sbl("sqh%d" % i, [128, 8, 128], F32) for i in range(2)]
        d_sqh = [Dep(), Dep()]
        ms = [sbl("ms%d" % i, [128, 8], F32) for i in range(2)]
        d_ms = [Dep(), Dep()]
        epsc = sbl("epsc2", [128, 1], F32)
        pool.op(lambda: nc.gpsimd.memset(epsc[:], EPS), writes=[d_ms[0], d_ms[1]])
        hn = [sbl("hn%d" % i, [128, 8, 128], BF16) for i in range(2)]
        d_hn = [Dep(), Dep()]
        hgT = [sbl("hgT%d" % i, [128, 8, TT], BF16) for i in range(2)]
        d_hgT = [Dep(), Dep()]
        xq = [sbl("xq%d" % i, [128, 2, TT], F32) for i in range(2)]
        dxq = [[Dep(), Dep()], [Dep(), Dep()]]
        nxq = [0]
        tpbs = [k.ps[1].bitcast(BF16).rearrange("p (a b) -> p a b", a=8), k.ps[2].bitcast(BF16).rearrange("p (a b) -> p a b", a=8)]
        nl = [0]

        d_lo = [Dep(), Dep()]

        def loadh(c):
            s2 = c % 2
            t0 = c * 128
            qs.dma(hfc[s2][:], k.hf_s[t0:t0 + 128, :].rearrange("t (h e) -> t h e", h=8), reads=[k.d_hf[0][c]], writes=[d_ld[s2]])
            qs.dma(hbc[s2][:], k.hb_s[t0:t0 + 128, :].rearrange("t (h e) -> t h e", h=8), reads=[k.d_hf[1][c]], writes=[d_ld[s2]])

        def loado(c):
            s2 = c % 2
            t0 = c * 128
            qs.dma(oTc[s2][:], k.oT_s[:, :, t0:t0 + 128].rearrange("c p t -> p c t"), reads=[k.d_o[c // 4]], writes=[d_lo[s2]])

        def prep_stages(c, s2, hg, dhg):
            b2 = c % 2
            hs_ = hsum[s2]
            tpb = tpbs[b2]
            cpos = c % 4

            def s1():
                pool.op(lambda: nc.gpsimd.tensor_tensor(out=hs_[:], in0=hfc[s2][:], in1=hbc[s2][:], op=ALU.add),
                        reads=[d_ld[s2]], writes=[d_hs[s2]])

            def s2_():
                act.op(lambda: nc.scalar.activation(out=sqh[b2][:], in_=hs_[:], func=AF.Square), reads=[d_hs[s2]], writes=[d_sqh[b2]])

            def s3():
                dve.op(lambda: nc.vector.reduce_sum(out=ms[b2][:], in_=sqh[b2][:], axis=AX.X), reads=[d_sqh[b2]], writes=[d_ms[b2]])

            def s4():
                act.op(lambda: nc.scalar.activation(out=ms[b2][:], in_=ms[b2][:], func=AF.Sqrt, bias=epsc[:, 0:1], scale=1.0 / 128),
                       reads=[d_ms[b2]], writes=[d_ms[b2]])

            def s5():
                dve.op(lambda: nc.vector.reciprocal(out=ms[b2][:], in_=ms[b2][:]), reads=[d_ms[b2]], writes=[d_ms[b2]])
                dve.op(lambda: nc.vector.tensor_tensor(out=hn[b2][:], in0=hs_[:], in1=ms[b2][:].unsqueeze(2).to_broadcast([128, 8, 128]), op=ALU.mult),
                       reads=[d_hs[s2], d_ms[b2]], writes=[d_hn[b2]])

            def s6():
                for fc in range(8):
                    pe.op(lambda: nc.tensor.transpose(tpb[:, fc, :], hn[b2][:, fc, :], k.ident_b[:]), reads=[d_hn[b2], dc_], writes=[k.dps[1 + b2]])

            def s7():
                dve.op(lambda: nc.vector.tensor_tensor(out=sqh[b2][:], in0=tpb, in1=k.ghT[:].unsqueeze(2).to_broadcast([128, 8, 128]), op=ALU.mult),
                       reads=[k.dps[1 + b2], dc_], writes=[d_sqh[b2]])

            def s8():
                pool.op(lambda: nc.gpsimd.tensor_tensor(out=hg[:, :, cpos * 128:(cpos + 1) * 128], in0=sqh[b2][:], in1=oTc[s2][:], op=ALU.mult),
                        reads=[d_sqh[b2], d_lo[s2]], writes=[dhg])

            return [s1, s2_, s3, s4, s5, s6, s7, s8]

        def outproj_part(tt, part):
            cond = 0 if tt < 8 else 1
            hg, dhg = hgT[tt % 2], d_hgT[tt % 2]
            qi = nxq[0] % 2
            nxq[0] += 1
            xt, dxt = xq[qi], dxq[qi]
            xv = k.xT_s[2 * part:2 * part + 2, :, tt * TT:(tt + 1) * TT].rearrange("c p t -> p c t")
            qs.dma(xt[:], xv, reads=[k.d_xs[tt]], writes=dxt)
            for d2 in range(2):
                dc = 2 * part + d2
                pb = 3 + dc % 2
                po = k.ps[pb]
                for fc in range(8):
                    pe.op(lambda: nc.tensor.matmul(po, wout[:, fc, dc * 128:(dc + 1) * 128], hg[:, fc, :],
                                                   start=(fc == 0), stop=(fc == 7)),
                          reads=[dW, dhg], writes=[k.dps[pb]], inc=(fc == 7))
                dve.op(lambda: nc.vector.scalar_tensor_tensor(out=xt[:, d2, :], in0=po, scalar=k.G[l][:, j, dc, cond:cond + 1],
                                                              in1=xt[:, d2, :], op0=ALU.mult, op1=ALU.add),
                       reads=[k.dps[pb], dxt[d2], dc_], writes=[dxt[d2]])
            qs.dma(xv, xt[:], reads=dxt, writes=[k.d_xs[tt]])

        for c in (0, 1):
            loadh(c)
            loado(c)
        for tt in range(NTILE):
            hg, dhg = hgT[tt % 2], d_hgT[tt % 2]
            for pr in range(2):
                c0 = 4 * tt + 2 * pr
                stA = prep_stages(c0, c0 % 2, hg, dhg)
                stB = prep_stages(c0 + 1, (c0 + 1) % 2, hg, dhg)
                for si_, (fa, fb) in enumerate(zip(stA, stB)):
                    fa()
                    fb()
                    if si_ == 0:
                        for cn in (c0 + 2, c0 + 3):
                            if cn < NCH:
                                loadh(cn)
                    if si_ == 7:
                        for cn in (c0 + 2, c0 + 3):
                            if cn < NCH:
                                loado(cn)
                if tt > 0:
                    outproj_part(tt - 1, 2 * pr)
                    outproj_part(tt - 1, 2 * pr + 1)
        for part in range(4):
            outproj_part(NTILE - 1, part)


def _attn_phase(k, s):
    nc = k.nc
    pe, act, dve, pool, qs, qg = k.pe, k.act, k.dve, k.pool, k.qs, k.qg
    W = k.W[s]
    win = W[:, 0:22528].rearrange("p (k n) -> p k n", k=8)
    wout = W[:, 22528:22528 + 8192].rearrange("p (k n) -> p k n", k=8)
    dW = k.dW[s]
    dc_ = k.d_const
    l, j = 1, 1
    with ExitStack() as st0:
        kctxT = k.sb("kctxT", [128, 2, 512], BF16, st0)
        vctx = k.sb("vctx", [128, 4, 256], BF16, st0)
        d_kctx, d_vctx = Dep(), Dep()
        with ExitStack() as st:
            B = _norm_bufs(k, st)
            xt1 = k.sb("xtP", [128, 8, TT], F32, st)
            dxt1 = [Dep() for _ in range(8)]
            hTs = [k.sb("hT0", [128, 8, TT], BF16, st), k.sb("hT1", [128, 8, TT], BF16, st)]
            dhTs = [[Dep() for _ in range(8)] for _ in range(2)]
            stage = [k.sb("stage%d" % i, [128, 4, TT], BF16, st) for i in range(2)]
            dstage = [Dep() for _ in range(2)]
            ckf = stage[0][:].bitcast(F32)
            d_ckf = dstage[0]
            cosT = k.sb("cosT", [128, TT], F32, st)
            sinT = k.sb("sinT", [128, TT], F32, st)
            d_rope = Dep()
            tA = k.sb("tA", [128, TT], F32, st)
            tB = k.sb("tB", [128, TT], F32, st)
            d_tA, d_tB = Dep(), Dep()
            stf = [tA[:, 0:256], tB[:, 0:256]]
            d_stf = [d_tA, d_tB]
            nst, npb, nev, nsf = [0], [0], [0], [0]

            def next_stage():
                i = nst[0] % 2
                nst[0] += 1
                return stage[i], dstage[i]

            def next_bank():
                i = 1 + npb[0] % 6
                npb[0] += 1
                return i

            def evac(out, src, pi, wdep):
                if nev[0] % 2 == 0:
                    act.op(lambda: nc.scalar.activation(out=out, in_=src, func=AF.Copy), reads=[k.dps[pi]], writes=[wdep])
                else:
                    dve.op(lambda: nc.vector.tensor_copy(out=out, in_=src), reads=[k.dps[pi]], writes=[wdep])
                nev[0] += 1

            qs.dma(ckf, k.ck.rearrange("(b s) f -> s b f", s=128), writes=[d_ckf])
            qg.dma(vctx[:], k.cv.rearrange("(b s) f -> s b f", s=128), writes=[d_vctx])
            for b in range(4):
                for gp in range(2):
                    pi = next_bank()
                    pe.op(lambda: nc.tensor.transpose(k.ps[pi][:, 0:128], ckf[:, b, gp * 128:(gp + 1) * 128], k.ident_f[:]),
                          reads=[d_ckf, dc_], writes=[k.dps[pi]])
                    evac(kctxT[:, gp, b * 128:(b + 1) * 128], k.ps[pi][:, 0:128], pi, d_kctx)

            def units_for(tt, hT, dhT):
                rope = tt < 8
                t0 = tt * TT
                units = []

                def proj(col, pi):
                    for kk in range(8):
                        pe.op(lambda: nc.tensor.matmul(k.ps[pi], win[:, kk, col:col + 128], hT[:, kk, :],
                                                       start=(kk == 0), stop=(kk == 7)),
                              reads=[dW, dhT[kk]], writes=[k.dps[pi]], inc=(kk == 7))

                def fmaj(colA, colB, nch, dst, ddst):
                    for g4 in range((nch + 3) // 4):
                        holder = {}
                        n4 = min(4, nch - 4 * g4)
                        for f4 in range(n4):
                            def uA(g4=g4, f4=f4, holder=holder, n4=n4):
                                if f4 == 0:
                                    holder["s"] = next_stage()
                                sg_, dsg_ = holder["s"]
                                fc = g4 * 4 + f4
                                pa = next_bank()
                                holder["pa"] = pa
                                proj(colA + fc * 128, pa)
                                if not rope:
                                    evac(sg_[:, f4, :], k.ps[pa], pa, dsg_)
                                    if f4 == n4 - 1:
                                        qs.dma(dst[g4 * 4:g4 * 4 + n4, :, t0:t0 + TT].rearrange("c p t -> p c t"), sg_[:, 0:n4, :], reads=[dsg_], writes=[ddst[tt]])
                            units.append(uA)
                            if rope:
                                def uB(g4=g4, f4=f4, holder=holder, n4=n4):
                                    sg_, dsg_ = holder["s"]
                                    fc = g4 * 4 + f4
                                    pa = holder["pa"]
                                    pb_ = next_bank()
                                    proj(colB + fc * 128, pb_)
                                    dve.op(lambda: nc.vector.tensor_tensor(out=tA[:], in0=k.ps[pa], in1=cosT[:], op=ALU.mult),
                                           reads=[k.dps[pa], d_rope], writes=[d_tA])
                                    dve.op(lambda: nc.vector.tensor_tensor(out=tB[:], in0=k.ps[pb_], in1=sinT[:], op=ALU.mult),
                                           reads=[k.dps[pb_], d_rope], writes=[d_tB])
                                    pool.op(lambda: nc.gpsimd.tensor_tensor(out=sg_[:, f4, :], in0=tA[:], in1=tB[:], op=ALU.add),
                                            reads=[d_tA, d_tB], writes=[dsg_])
                                    if f4 == n4 - 1:
                                        qs.dma(dst[g4 * 4:g4 * 4 + n4, :, t0:t0 + TT].rearrange("c p t -> p c t"), sg_[:, 0:n4, :], reads=[dsg_], writes=[ddst[tt]])
                                units.append(uB)

                fmaj(0, 1536, 8, k.qT_s, k.d_q)
                fmaj(1024, 2560, 2, k.kT_s, k.d_k)
                holder = {}
                for tb in range(4):
                    def uV(tb=tb, holder=holder):
                        if tb == 0:
                            holder["s"] = next_stage()
                        sg_, dsg_ = holder["s"]
                        pi = next_bank()
                        pv = k.ps[pi][:, 0:256]
                        for kk in range(8):
                            pe.op(lambda: nc.tensor.matmul(pv, hT[:, kk, tb * 128:(tb + 1) * 128], win[:, kk, 1280:1536],
                                                           start=(kk == 0), stop=(kk == 7)),
                                  reads=[dW, dhT[kk]], writes=[k.dps[pi]], inc=(kk == 7))
                        if rope:
                            dve.op(lambda: nc.vector.tensor_copy(out=sg_[:, tb, 0:256], in_=pv), reads=[k.dps[pi]], writes=[dsg_])
                        else:
                            seq = (tt - 8) * 2 + tb // 2
                            tl = (tb % 2) * 128
                            sf = nsf[0] % 2
                            nsf[0] += 1
                            act.op(lambda: nc.scalar.activation(out=stf[sf], in_=pv, func=AF.Copy), reads=[k.dps[pi]], writes=[d_stf[sf]])
                            dve.op(lambda: nc.vector.tensor_copy(out=sg_[:, tb, 0:256], in_=stf[sf]), reads=[d_stf[sf]], writes=[dsg_])
                            qs.dma(k.ncv[seq, tl:tl + 128, :], stf[sf], reads=[d_stf[sf]], writes=[k.d_out])
                        if tb == 3:
                            qs.dma(k.vt_s[t0:t0 + TT, 0:256].rearrange("(b t) f -> t b f", t=128), sg_[:, :, 0:256], reads=[dsg_], writes=[k.d_vt[tt]])
                    units.append(uV)
                    if not rope:
                        def uK(tb=tb):
                            seq = (tt - 8) * 2 + tb // 2
                            tl = (tb % 2) * 128
                            pi2 = next_bank()
                            pk = k.ps[pi2][:, 0:256]
                            for kk in range(8):
                                pe.op(lambda: nc.tensor.matmul(pk, hT[:, kk, tb * 128:(tb + 1) * 128], win[:, kk, 1024:1280],
                                                               start=(kk == 0), stop=(kk == 7)),
                                      reads=[dW, dhT[kk]], writes=[k.dps[pi2]], inc=(kk == 7))
                            sf = nsf[0] % 2
                            nsf[0] += 1
                            act.op(lambda: nc.scalar.activation(out=stf[sf], in_=pk, func=AF.Copy), reads=[k.dps[pi2]], writes=[d_stf[sf]])
                            qs.dma(k.nck[seq, tl:tl + 128, :], stf[sf], reads=[d_stf[sf]], writes=[k.d_out])
                        units.append(uK)
                return units

            qs.dma(xt1[:], _xs_tile(k, 0), reads=[k.d_xs[0]], writes=dxt1)
            _modnorm(k, B, xt1, dxt1, l, j, 0, hTs[0], dhTs[0])
            for tt in range(NTILE):
                if tt < 8:
                    t0 = tt * TT
                    qs.dma(cosT[:], k.c_cos[:, t0:t0 + TT], writes=[d_rope])
                    qs.dma(sinT[:], k.c_sin[:, t0:t0 + TT], writes=[d_rope])
                units = units_for(tt, hTs[tt % 2], dhTs[tt % 2])
                _emit_units_pipelined(k, B, units, tt, xt1, dxt1, l, j, hTs, dhTs)
        _barrier(k)
        if getattr(k, "dbg_stop", "") == "AP":
            return

        _LOCAL_STRICT[0] = True
        with ExitStack() as st:
            sbl = lambda n, sh, dt: k.sb(n, sh, dt, st)
            qT = [sbl("qT%d" % i, [128, 8, TT], BF16) for i in range(2)]
            kwin = [sbl("kwin%d" % i, [128, 2, 768], BF16) for i in range(2)]
            vwin = [sbl("vwin%d" % i, [128, 6, 256], BF16) for i in range(2)]
            d_ld = [Dep(), Dep()]
            xt = sbl("xtA", [128, 8, TT], F32)
            dxt = [Dep() for _ in range(8)]
            oT = sbl("oT", [128, 8, TT], BF16)
            d_oT = Dep()
            PT = [sbl("PT%d" % i, [128, 2, 4, 128], BF16) for i in range(3)]
            d_PT = [Dep() for _ in range(3)]
            dtmp = sbl("dtmp", [128, 4, 128], F32)
            d_dtmp = Dep()
            npt, nstb = [0], [0]

            def issue_loads(tt):
                s2 = tt % 2
                if tt < 8:
                    blo, bhi = max(4 * tt - 1, 0), min(4 * tt + 4, 31)
                else:
                    blo, bhi = 4 * tt, 4 * tt + 3
                nb = bhi - blo + 1
                tiles = sorted(set(b // 4 for b in range(blo, bhi + 1)))
                qs.dma(qT[s2][:], k.qT_s[:, :, tt * TT:(tt + 1) * TT].rearrange("c p t -> p c t"), reads=[k.d_q[tt]], writes=[d_ld[s2]])
                qs.dma(kwin[s2][:, :, 0:nb * 128], k.kT_s[0:2, :, blo * 128:(bhi + 1) * 128].rearrange("c p t -> p c t"),
                       reads=[k.d_k[t] for t in tiles], writes=[d_ld[s2]])
                qs.dma(vwin[s2][:, 0:nb, :], k.vt_s[blo * 128:(bhi + 1) * 128, 0:256].rearrange("(b t) f -> t b f", t=128),
                       reads=[k.d_vt[t] for t in tiles], writes=[d_ld[s2]])
                return blo

            blo_next = issue_loads(0)
            for tt in range(NTILE):
                s2 = tt % 2
                cond = 0 if tt < 8 else 1
                blo = blo_next
                if tt + 1 < NTILE:
                    blo_next = issue_loads(tt + 1)
                qs.dma(xt[:], _xs_tile(k, tt), reads=[k.d_xs[tt]], writes=dxt)
                steps = []
                for qb in range(4):
                    jb = 4 * tt + qb
                    keys = []
                    if tt < 8:
                        for kb, m in ((jb - 1, k.negl4), (jb, None), (jb + 1, k.negu4)):
                            if 0 <= kb <= 31:
                                keys.append(("lat", kb - blo, m))
                        for cb in range(4):
                            keys.append(("ctx", cb, None))
                    else:
                        base = jb - (jb % 2)
                        keys = [("lat", base - blo, None), ("lat", base + 1 - blo, None)]
                    for gp in range(2):
                        for ki, (kind, bi, m) in enumerate(keys):
                            steps.append((qb, gp, kind, bi, m, ki == 0, ki == len(keys) - 1))

                def emit_qk(stp):
                    qb, gp, kind, bi, m, first, last = stp
                    pr = nstb[0] % 2
                    nstb[0] += 1
                    pt = npt[0] % 3
                    npt[0] += 1
                    for ph in range(2):
                        rs_ = slice(64 * ph, 64 * ph + 64)
                        if kind == "lat":
                            kop, kd = kwin[s2][rs_, gp, bi * 128:(bi + 1) * 128], d_ld[s2]
                        else:
                            kop, kd = kctxT[rs_, gp, bi * 128:(bi + 1) * 128], d_kctx
                        pst = 2 * pr + ph
                        STv = k.ps[pst].rearrange("p (a b) -> p a b", a=4)
                        pe.op(lambda: nc.tensor.matmul(STv, kop, qT[s2][rs_, 4 * gp:4 * gp + 4, qb * 128:(qb + 1) * 128], start=True, stop=(m is None)),
                              reads=[kd, d_ld[s2]], writes=[k.dps[pst]])
                    if m is not None:
                        for ph in range(2):
                            pst = 2 * pr + ph
                            STv = k.ps[pst].rearrange("p (a b) -> p a b", a=4)
                            pe.op(lambda: nc.tensor.matmul(STv, k.ident_b[:], m[:, 0:1, :].to_broadcast([128, 4, 128]), start=False, stop=True),
                                  reads=[dc_], writes=[k.dps[pst]])
                    ST2 = k.psall[:, 2 * pr * 512:(2 * pr + 2) * 512]
                    act.op(lambda: nc.scalar.activation(out=PT[pt][:].rearrange("p a b c -> p (a b c)"), in_=ST2, func=AF.Exp, scale=0.125),
                           reads=[k.dps[2 * pr], k.dps[2 * pr + 1]], writes=[d_PT[pt]])
                    return pt

                def emit_pv(stp, pt):
                    qb, gp, kind, bi, m, first, last = stp
                    pn, pd = 4 + gp, 6 + gp
                    NUMv = k.ps[pn].rearrange("p (a b) -> p a b", a=4)
                    DENv = k.ps[pd].rearrange("p (a b) -> p a b", a=4)
                    for ph in range(2):
                        g = 2 * gp + ph
                        rs_ = slice(64 * ph, 64 * ph + 64)
                        if kind == "lat":
                            vop, vd = vwin[s2][:, bi, g * 64:(g + 1) * 64], d_ld[s2]
                        else:
                            vop, vd = vctx[:, bi, g * 64:(g + 1) * 64], d_vctx
                        pe.op(lambda: nc.tensor.matmul(NUMv[rs_, :, :], vop, PT[pt][:, ph], start=first, stop=last),
                              reads=[vd, d_PT[pt]], writes=[k.dps[pn]])
                    for ph in range(2):
                        rs_ = slice(64 * ph, 64 * ph + 64)
                        pe.op(lambda: nc.tensor.matmul(DENv[rs_, :, :], k.ones_b[:, 0:64], PT[pt][:, ph], start=first, stop=last),
                              reads=[dc_, d_PT[pt]], writes=[k.dps[pd]])
                    if last:
                        dve.op(lambda: nc.vector.tensor_tensor(out=dtmp[:], in0=DENv, in1=k.esT[:, 4 * gp:4 * gp + 4].unsqueeze(2).to_broadcast([128, 4, 128]), op=ALU.add),
                               reads=[k.dps[pd], dc_], writes=[d_dtmp])
                        dve.op(lambda: nc.vector.reciprocal(out=dtmp[:], in_=dtmp[:]), reads=[d_dtmp], writes=[d_dtmp])
                        dve.op(lambda: nc.vector.tensor_tensor(out=oT[:, 4 * gp:4 * gp + 4, qb * 128:(qb + 1) * 128], in0=NUMv, in1=dtmp[:], op=ALU.mult),
                               reads=[k.dps[pn], d_dtmp], writes=[d_oT])

                pend = emit_qk(steps[0])
                for si in range(len(steps)):
                    nxt_pts = emit_qk(steps[si + 1]) if si + 1 < len(steps) else None
                    emit_pv(steps[si], pend)
                    pend = nxt_pts
                for dc in range(8):
                    po = k.ps[0]
                    for fc in range(8):
                        pe.op(lambda: nc.tensor.matmul(po[:], wout[:, fc, dc * 128:(dc + 1) * 128], oT[:, fc, :], start=(fc == 0), stop=(fc == 7)),
                              reads=[dW, d_oT], writes=[k.dps[0]], inc=(fc == 7))
                    dve.op(lambda: nc.vector.scalar_tensor_tensor(out=xt[:, dc, :], in0=po[:], scalar=k.G[l][:, j, dc, cond:cond + 1],
                                                                  in1=xt[:, dc, :], op0=ALU.mult, op1=ALU.add),
                           reads=[k.dps[0], dxt[dc], dc_], writes=[dxt[dc]])
                qs.dma(_xs_tile(k, tt), xt[:], reads=dxt, writes=[k.d_xs[tt]])
        _LOCAL_STRICT[0] = False


_PROG = {}


def _consts():
    c = {}
    c["c_ident"] = np.eye(128, dtype=np.float32)
    s = np.arange(128)[:, None]
    t = np.arange(128)[None, :]
    c["c_masku"] = (t >= s).astype(np.float32)
    c["c_maskl"] = (t <= s).astype(np.float32)
    e = np.zeros((64, 16), np.float32)
    for h in range(8):
        e[h, h] = 1.0
        e[32 + h, 8 + h] = 1.0
    c["c_e16"] = e
    quarter = 16
    freqs = (np.float32(10000.0) ** (-np.arange(quarter, dtype=np.float32) / np.float32(quarter))).astype(np.float32)
    row = np.repeat(np.arange(64, dtype=np.float32), 64)
    col = np.tile(np.arange(64, dtype=np.float32), 64)
    ang_r = row[:, None] * freqs
    ang_c = col[:, None] * freqs
    ang = np.concatenate([ang_r, ang_r, ang_c, ang_c], axis=-1).astype(np.float32)
    cos = np.cos(ang).astype(np.float32).T
    sin = np.sin(ang).astype(np.float32).T
    sgn = np.where((np.arange(64) % 32) < 16, -1.0, 1.0).astype(np.float32)[:, None]
    c["c_cos"] = np.ascontiguousarray(np.concatenate([cos, cos], axis=0))
    c["c_sin"] = np.ascontiguousarray(np.concatenate([sin * sgn, sin * sgn], axis=0))
    return c


def _attn_perm():
    n = np.arange(1024)
    fc = n // 128
    ph = (n % 128) // 64
    d = n % 64
    h = 8 * (fc // 4) + 4 * ph + (fc % 4)
    sw = np.where((d % 32) < 16, d + 16, d - 16)
    cols_q = h * 64 + d
    cols_qs = h * 64 + sw
    nk = np.arange(256)
    dk = nk % 64
    swk = np.where((dk % 32) < 16, dk + 16, dk - 16)
    cols_k = 1024 + nk
    cols_ks = 1024 + (nk // 64) * 64 + swk
    return cols_q, cols_qs, cols_k, cols_ks, h


def _make_in_maps(inp):
    f = lambda a: np.ascontiguousarray(np.asarray(a, dtype=np.float32))
    cols_q, cols_qs, cols_k, cols_ks, hperm = _attn_perm()
    awi = np.asarray(inp["attn_w_in"][0])
    a_ext = f(np.concatenate([awi[:, cols_q], awi[:, cols_k], awi[:, 1280:1536], awi[:, cols_qs], awi[:, cols_ks]], axis=1))
    a_wout = f(np.asarray(inp["attn_w_out"][0])[cols_q, :])
    sink = np.asarray(inp["attn_sink"][0])
    hT = hperm.reshape(8, 2, 64)[:, :, 0]
    sinkT = np.empty((128, 8), np.float32)
    for fc in range(8):
        for ph in range(2):
            sinkT[ph * 64:(ph + 1) * 64, fc] = sink[hT[fc, ph]]
    shared = dict(
        w_mod=f(inp["w_mod"]), b_mod=f(inp["b_mod"]), norm_g=f(inp["norm_g"]),
        ffn1_w_gu=f(inp["ffn1_w_gu"]), ffn1_w_down=f(inp["ffn1_w_down"]),
        ffn2_w_gu=f(inp["ffn2_w_gu"]), ffn2_w_down=f(inp["ffn2_w_down"]),
        mlstm_w_in=f(inp["mlstm_w_in"][0]), mlstm_b_gate=f(inp["mlstm_b_gate"]), mlstm_g_head=f(inp["mlstm_g_head"][0]),
        mlstm_w_out=f(inp["mlstm_w_out"][0]), attn_w_in_ext=a_ext, attn_sinkT=sinkT, attn_w_out_p=a_wout,
        final_g=f(inp["final_g"]),
    )
    shared.update(_consts())
    maps = []
    for i in range(8):
        m = dict(shared)
        xs = np.asarray(inp["x_sample"][i])
        xp = np.asarray(inp["x_prompt"][4 * i:4 * i + 4]).reshape(1024, 1024)
        m["x_in"] = f(np.concatenate([xs, xp], axis=0))
        m["cvec"] = f(np.stack([np.asarray(inp["c"][i]), np.asarray(inp["c_ctx"])], axis=0))
        m["st_c"] = f(inp["state_c"][i, 0])
        m["st_n"] = f(inp["state_n"][i, 0])
        m["st_m"] = f(inp["state_m"][i, 0])
        m["ck"] = f(np.asarray(inp["cache_k"][i, 0]).reshape(512, 256))
        m["cv"] = f(np.asarray(inp["cache_v"][i, 0]).reshape(512, 256))
        maps.append(m)
    return maps


def kernel(**inputs):
    if "p" not in _PROG:
        _PROG["p"] = build_program()
    nc = _PROG["p"]
    maps = _make_in_maps(inputs)
    res = run_bass_kernel_spmd(nc, maps, core_ids=list(range(8)))
    R = res.results
    y = np.stack([r["y_out"] for r in R], axis=0)
    y_sample = np.ascontiguousarray(y[:, :4096, :])
    y_prompt = np.ascontiguousarray(y[:, 4096:, :].reshape(32, 256, 1024))
    nsc = np.concatenate([r["nsc"] for r in R], axis=0).reshape(32, 1, 2, 8, 64, 128)
    nsn = np.concatenate([r["nsn"] for r in R], axis=0).reshape(32, 1, 2, 8, 64)
    nsm = np.concatenate([r["nsm"] for r in R], axis=0).reshape(32, 1, 2, 8)
    nck = np.concatenate([r["nck"] for r in R], axis=0).reshape(32, 1, 256, 4, 64)
    ncv = np.concatenate([r["ncv"] for r in R], axis=0).reshape(32, 1, 256, 4, 64)
    return (y_prompt.astype(np.float32), y_sample.astype(np.float32), nsc.astype(np.float32), nsn.astype(np.float32),
            nsm.astype(np.float32), nck.astype(np.float32), ncv.astype(np.float32))
```
